# Optimizing a Trainium2 kernel written in Bass

```python
import jax, jax.numpy as jnp
from jax import lax
import numpy as np

D_MODEL = 2048
BATCH = 16
SEQ = 256
DEPTH = 2
DEC_BATCH = 8
DEC_SEQ = 2048
PAST_LEN = 512

GRID_W = 64
HEAD_DIM = 128
N_Q_HEADS = 8
N_KV_HEADS = 2
Q_PER_KV = N_Q_HEADS // N_KV_HEADS
ATT_WIDTH = N_Q_HEADS * HEAD_DIM
KV_WIDTH = N_KV_HEADS * HEAD_DIM
CONV_WIDTH = D_MODEL - ATT_WIDTH
CONF_WIDTH = 31
SHORT_WIDTH = 3
WINDOW = 128
Q_BLOCK = 128
ROPE_THETA = 10000.0
EPS = 1e-6
NEG_INF = -1e30
IN0 = 3 * CONV_WIDTH + 2 * ATT_WIDTH + 2 * KV_WIDTH
IN1 = 4 * CONV_WIDTH + 2 * ATT_WIDTH + 2 * KV_WIDTH

kernel_name = "hybrid_dit_prefix_context_step"


def _split(x, sizes):
    idx, acc = [], 0
    for s in sizes[:-1]:
        acc += s
        idx.append(acc)
    return jnp.split(x, idx, axis=-1)


def rms_norm(x, g):
    xf = x.astype(jnp.float32)
    y = xf * lax.rsqrt(jnp.mean(xf * xf, axis=-1, keepdims=True) + EPS)
    return (y * g.astype(jnp.float32)).astype(x.dtype)


def layer_norm(x, g, b):
    xf = x.astype(jnp.float32)
    mu = jnp.mean(xf, axis=-1, keepdims=True)
    var = jnp.mean(jnp.square(xf - mu), axis=-1, keepdims=True)
    y = (xf - mu) * lax.rsqrt(var + EPS)
    return (y * g.astype(jnp.float32) + b.astype(jnp.float32)).astype(x.dtype)


def adaln(cond, w_mod, b_mod):
    m = jax.nn.silu(cond) @ w_mod + b_mod
    shift, scale, gate = jnp.split(m, 3, axis=-1)
    return shift[:, None, :], scale[:, None, :], gate[:, None, :]


def depthwise_conv(x, w):
    k = w.shape[0]
    return lax.conv_general_dilated(
        x, w[:, None, :].astype(x.dtype), window_strides=(1,),
        padding=[(k // 2, k // 2)], dimension_numbers=("NWC", "WIO", "NWC"),
        feature_group_count=x.shape[-1])


def rope_tables(n_tokens):
    rows = n_tokens // GRID_W
    r = jnp.repeat(jnp.arange(rows), GRID_W).astype(jnp.float32)
    col = jnp.tile(jnp.arange(GRID_W), rows).astype(jnp.float32)
    half = HEAD_DIM // 2
    inv = ROPE_THETA ** (-jnp.arange(0, half, 2, dtype=jnp.float32) / half)
    ang_r = r[:, None] * inv
    ang_c = col[:, None] * inv
    ang = jnp.concatenate([ang_r, ang_r, ang_c, ang_c], axis=-1)
    return jnp.cos(ang)[None, :, None, :], jnp.sin(ang)[None, :, None, :]


def _rot_half(u):
    u1, u2 = jnp.split(u, 2, axis=-1)
    return jnp.concatenate([-u2, u1], axis=-1)


def apply_rope(x, cos, sin):
    xr, xc = jnp.split(x, 2, axis=-1)
    xrot = jnp.concatenate([_rot_half(xr), _rot_half(xc)], axis=-1)
    return (x.astype(jnp.float32) * cos + xrot.astype(jnp.float32) * sin).astype(x.dtype)


def attend(q, k, v, bias, sink):
    s = jnp.einsum("bqhgd,bkhd->bhgqk", q, k).astype(jnp.float32) * (HEAD_DIM ** -0.5)
    if bias is not None:
        s = s + bias
    if sink is None:
        p = jax.nn.softmax(s, axis=-1)
    else:
        sk = sink.astype(jnp.float32).reshape(N_KV_HEADS, Q_PER_KV)[None, :, :, None, None]
        sk = jnp.broadcast_to(sk, s.shape[:-1] + (1,))
        p = jax.nn.softmax(jnp.concatenate([s, sk], axis=-1), axis=-1)[..., :-1]
    return jnp.einsum("bhgqk,bkhd->bqhgd", p.astype(v.dtype), v)


def dense_attention(q, k, v, sink):
    bn, s, h, hd = q.shape
    nb = s // Q_BLOCK
    qb = q.reshape(bn, nb, Q_BLOCK, N_KV_HEADS, Q_PER_KV, hd).swapaxes(0, 1)
    out = lax.map(lambda qi: attend(qi, k, v, None, sink), qb)
    return out.swapaxes(0, 1).reshape(bn, s, h * hd)


def window_attention(q, k, v, k_ctx, v_ctx, sink):
    bn, s, h, hd = q.shape
    nb = s // Q_BLOCK
    span = Q_BLOCK + 2 * WINDOW
    n_ctx = k_ctx.shape[1]
    pad = ((0, 0), (WINDOW, WINDOW), (0, 0), (0, 0))
    kp = jnp.pad(k, pad)
    vp = jnp.pad(v, pad)
    qb = q.reshape(bn, nb, Q_BLOCK, N_KV_HEADS, Q_PER_KV, hd).swapaxes(0, 1)
    ctx_bias = jnp.zeros((Q_BLOCK, n_ctx), jnp.float32)

    def block(args):
        qi, bi = args
        start = bi * Q_BLOCK
        kb = lax.dynamic_slice_in_dim(kp, start, span, axis=1)
        vb = lax.dynamic_slice_in_dim(vp, start, span, axis=1)
        qpos = start + jnp.arange(Q_BLOCK)
        kpos = start - WINDOW + jnp.arange(span)
        valid = ((kpos[None, :] >= 0) & (kpos[None, :] < s)
                 & (jnp.abs(qpos[:, None] - kpos[None, :]) <= WINDOW))
        bias = jnp.concatenate(
            [ctx_bias, jnp.where(valid, 0.0, NEG_INF).astype(jnp.float32)], axis=-1)
        return attend(qi, jnp.concatenate([k_ctx, kb], axis=1),
                      jnp.concatenate([v_ctx, vb], axis=1), bias, sink)

    out = lax.map(block, (qb, jnp.arange(nb)))
    return out.swapaxes(0, 1).reshape(bn, s, h * hd)


def mixer_even(h, w_in, conv_w, conv_b, ln_g, ln_b, qn_g, kn_g, w_out, rope, ctx_kv):
    bn, s, _ = h.shape
    proj = h @ w_in
    ga, gb, gate_a, q, k, v, gate_b = _split(
        proj, [CONV_WIDTH, CONV_WIDTH, CONV_WIDTH, ATT_WIDTH, KV_WIDTH, KV_WIDTH, ATT_WIDTH])
    a = ga * jax.nn.sigmoid(gb)
    a = depthwise_conv(a, conv_w) + conv_b
    a = jax.nn.silu(layer_norm(a, ln_g, ln_b)) * jax.nn.silu(gate_a)
    q = rms_norm(q.reshape(bn, s, N_Q_HEADS, HEAD_DIM), qn_g)
    k = rms_norm(k.reshape(bn, s, N_KV_HEADS, HEAD_DIM), kn_g)
    v = v.reshape(bn, s, N_KV_HEADS, HEAD_DIM)
    if ctx_kv is None:
        k_all, v_all = k, v
    else:
        q = apply_rope(q, *rope)
        k = apply_rope(k, *rope)
        k_all = jnp.concatenate([ctx_kv[0], k], axis=1)
        v_all = jnp.concatenate([ctx_kv[1], v], axis=1)
    b = dense_attention(q, k_all, v_all, None) * jax.nn.silu(gate_b)
    return jnp.concatenate([a, b], axis=-1) @ w_out, k, v


def mixer_odd(h, w_in, sink, short_w, w_out, rope, ctx_kv):
    bn, s, _ = h.shape
    proj = h @ w_in
    q, k, v, gate_c, db, dc, dx, gate_d = _split(
        proj, [ATT_WIDTH, KV_WIDTH, KV_WIDTH, ATT_WIDTH,
               CONV_WIDTH, CONV_WIDTH, CONV_WIDTH, CONV_WIDTH])
    q = q.reshape(bn, s, N_Q_HEADS, HEAD_DIM)
    k = k.reshape(bn, s, N_KV_HEADS, HEAD_DIM)
    v = v.reshape(bn, s, N_KV_HEADS, HEAD_DIM)
    if ctx_kv is None:
        c_out = dense_attention(q, k, v, sink)
    else:
        q = apply_rope(q, *rope)
        kr = apply_rope(k, *rope)
        c_out = window_attention(q, kr, v, ctx_kv[0], ctx_kv[1], sink)
    c_out = c_out * jax.nn.silu(gate_c)
    d_out = db * depthwise_conv(dc * dx, short_w) * jax.nn.silu(gate_d)
    return jnp.concatenate([c_out, d_out], axis=-1) @ w_out, k, v


def setup_inputs(seed: int = 0) -> dict:
    key = jax.random.key(seed)
    ks = iter(jax.random.split(key, 40))
    f32 = jnp.float32

    def nrm(shape, scale):
        return jax.random.normal(next(ks), shape, f32) * scale

    def gain(n):
        return 1.0 + nrm((n,), 0.02)

    d = D_MODEL
    kv_shape = (DEC_BATCH, PAST_LEN, N_KV_HEADS, HEAD_DIM)
    return {
        "x_prompt": nrm((BATCH, SEQ, d), 1.0),
        "x_sample": nrm((DEC_BATCH, DEC_SEQ, d), 1.0),
        "cache_k0": nrm(kv_shape, 1.0),
        "cache_v0": nrm(kv_shape, 1.0),
        "cache_k1": nrm(kv_shape, 1.0),
        "cache_v1": nrm(kv_shape, 1.0),
        "c": nrm((DEC_BATCH, d), 1.0),
        "c_ctx": nrm((d,), 1.0),
        "mod_w0": nrm((d, 3 * d), 0.5 * d ** -0.5),
        "mod_b0": nrm((3 * d,), 0.01),
        "norm_g0": gain(d),
        "w_in0": nrm((d, IN0), d ** -0.5),
        "conv_w0": nrm((CONF_WIDTH, CONV_WIDTH), CONF_WIDTH ** -0.5),
        "conv_b0": nrm((CONV_WIDTH,), 0.01),
        "ln_g0": gain(CONV_WIDTH),
        "ln_b0": nrm((CONV_WIDTH,), 0.01),
        "q_norm_g0": gain(HEAD_DIM),
        "k_norm_g0": gain(HEAD_DIM),
        "w_out0": nrm((d, d), d ** -0.5),
        "mod_w1": nrm((d, 3 * d), 0.5 * d ** -0.5),
        "mod_b1": nrm((3 * d,), 0.01),
        "norm_g1": gain(d),
        "w_in1": nrm((d, IN1), d ** -0.5),
        "sink1": nrm((N_Q_HEADS,), 0.5),
        "short_w1": nrm((SHORT_WIDTH, CONV_WIDTH), SHORT_WIDTH ** -0.5),
        "w_out1": nrm((d, d), d ** -0.5),
        "final_norm_g": gain(d),
    }


def reference(x_prompt, x_sample, cache_k0, cache_v0, cache_k1, cache_v1, c, c_ctx,
              mod_w0, mod_b0, norm_g0, w_in0, conv_w0, conv_b0, ln_g0, ln_b0,
              q_norm_g0, k_norm_g0, w_out0,
              mod_w1, mod_b1, norm_g1, w_in1, sink1, short_w1, w_out1, final_norm_g):
    rope = rope_tables(x_sample.shape[1])
    caches = [(cache_k0, cache_v0), (cache_k1, cache_v1)]
    mods = [(mod_w0, mod_b0, norm_g0), (mod_w1, mod_b1, norm_g1)]
    cond_ctx = c_ctx[None, :]
    xp, xs = x_prompt, x_sample
    new_kv = []
    for layer in range(DEPTH):
        w_mod, b_mod, g = mods[layer]
        sh, sc, gt = adaln(cond_ctx, w_mod, b_mod)
        hp = rms_norm(xp, g) * (1.0 + sc) + sh
        sh2, sc2, gt2 = adaln(c, w_mod, b_mod)
        hs = rms_norm(xs, g) * (1.0 + sc2) + sh2
        if layer % 2 == 0:
            op, kc, vc = mixer_even(hp, w_in0, conv_w0, conv_b0, ln_g0, ln_b0,
                                    q_norm_g0, k_norm_g0, w_out0, rope, None)
            os_, _, _ = mixer_even(hs, w_in0, conv_w0, conv_b0, ln_g0, ln_b0,
                                   q_norm_g0, k_norm_g0, w_out0, rope, caches[layer])
        else:
            op, kc, vc = mixer_odd(hp, w_in1, sink1, short_w1, w_out1, rope, None)
            os_, _, _ = mixer_odd(hs, w_in1, sink1, short_w1, w_out1, rope, caches[layer])
        new_kv.append((kc, vc))
        xp = xp + gt * op
        xs = xs + gt2 * os_
    y_prompt = rms_norm(xp, final_norm_g)
    y_sample = rms_norm(xs, final_norm_g)
    return (y_prompt, y_sample, new_kv[0][0], new_kv[0][1], new_kv[1][0], new_kv[1][1])
```

```python
import numpy as np
from contextlib import ExitStack
import concourse.bass as bass
import concourse.mybir as mybir
from concourse.bass_utils import run_bass_kernel_spmd

F32 = mybir.dt.float32
BF16 = mybir.dt.bfloat16
AF = mybir.ActivationFunctionType
ALU = mybir.AluOpType

D = 2048
NTOK = 2560
EPS = 1e-6
SAME_ENGINE_SYNC = {'pe': False, 'dve': True, 'act': True, 'pool': True, 'sp': True}
NS = 8
NR = 2
TILES = [(0, 0), (0, 512), (0, 1024), (0, 1536), (1, 2048)]
IN_COLS = [5632, 6656]
SCALE = 128.0 ** -0.5


class Buf:
    __slots__ = ("w", "r", "n", "x")

    def __init__(self, n="", x=False):
        self.w = {}
        self.r = {}
        self.n = n
        self.x = x


class KB:
    def __init__(self):
        self.nc = bass.Bass("TRN2", target_bir_lowering=False)
        self.es = ExitStack()
        nc = self.nc
        self.eng = {"pe": nc.tensor, "act": nc.scalar, "dve": nc.vector, "pool": nc.gpsimd, "sp": nc.sync}
        self.sem = {n: self.es.enter_context(nc.semaphore("s_" + n)) for n in ("pe", "act", "dve", "pool")}
        self.cnt = {n: 0 for n in self.sem}
        self.seen = {n: {} for n in self.eng}
        self.dsem = {q: [self.es.enter_context(nc.semaphore("d_%s%d" % (q, i))) for i in range(NS)]
                     for q in ("sp", "pool", "pst")}
        self.dcnt = {q: [0] * NS for q in ("sp", "pool", "pst")}
        self.dnext = {q: 0 for q in ("sp", "pool", "pst")}
        self.qeng = {"sp": "sp", "pool": "pool", "pst": "pool"}
        self.nops = 0
        self.limit = 10 ** 9
        self.lines = []

    def _skip(self):
        import sys
        self.nops += 1
        if self.nops > self.limit:
            return True
        self.lines.append(sys._getframe(2).f_lineno)
        return False

    def sb(self, name, shape, dt):
        return self.es.enter_context(self.nc.sbuf_tensor(name, shape, dt))

    def ps(self, name, shape, dt):
        return self.es.enter_context(self.nc.psum_tensor(name, shape, dt))

    def _wait(self, e, key, val):
        if val <= 0 or self.seen[e].get(key, 0) >= val:
            return
        sem = self.sem[key] if isinstance(key, str) else self.dsem[key[1]][key[2]]
        self.eng[e].wait_ge(sem, val)
        self.seen[e][key] = val

    def _deps(self, e, r, w):
        need = {}
        for b in r:
            for k2, v in b.w.items():
                if need.get(k2, 0) < v:
                    need[k2] = v
            if b.x:
                for k2, v in b.r.items():
                    if k2 != e and need.get(k2, 0) < v:
                        need[k2] = v
        for b in w:
            for k2, v in b.w.items():
                if need.get(k2, 0) < v:
                    need[k2] = v
            for k2, v in b.r.items():
                if need.get(k2, 0) < v:
                    need[k2] = v
        for k2, v in need.items():
            if k2 == e and not SAME_ENGINE_SYNC[e]:
                continue
            self._wait(e, k2, v)

    def _mark(self, key, val, r, w):
        for b in w:
            b.w = {key: val}
            b.r = {}
        for b in r:
            if b.r.get(key, 0) < val:
                b.r[key] = val

    def op(self, e, fn, r=(), w=()):
        if self._skip():
            return
        self._deps(e, r, w)
        ins = fn()
        self.cnt[e] += 1
        ins.then_inc(self.sem[e], 1)
        self._mark(e, self.cnt[e], r, w)

    def group(self, e, fns, r=(), w=()):
        if self._skip():
            return
        self._deps(e, r, w)
        ins = None
        for f in fns:
            ins = f()
        self.cnt[e] += 1
        ins.then_inc(self.sem[e], 1)
        self._mark(e, self.cnt[e], r, w)

    def dma(self, q, out, in_, r=(), w=()):
        if self._skip():
            return
        i = self.dnext[q]
        self.dnext[q] = (i + 1) % NS
        key = ("d", q, i)
        e = self.qeng[q]
        self._wait(e, key, self.dcnt[q][i])
        self._deps(e, r, w)
        ins = self.eng[e].dma_start(out=out, in_=in_)
        self.dcnt[q][i] += 16
        ins.then_inc(self.dsem[q][i], 16)
        self._mark(key, self.dcnt[q][i], r, w)

    def finish(self):
        for q in ("sp", "pool", "pst"):
            for i in range(NS):
                self._wait("sp", ("d", q, i), self.dcnt[q][i])
        for e in ("pe", "act", "dve", "pool"):
            self._wait("sp", e, self.cnt[e])


def build(debug=False, stop=99, limit=10 ** 9):
    k = KB()
    k.limit = limit
    nc = k.nc
    TE, ACT, DVE, POOL = nc.tensor, nc.scalar, nc.vector, nc.gpsimd

    def din(name, shape):
        return nc.dram_tensor(name, shape, F32, kind="ExternalInput").ap()

    def dout(name, shape):
        return nc.dram_tensor(name, shape, F32, kind="ExternalOutput").ap()

    skind = "ExternalOutput" if debug else "Internal"

    def dscr(name, shape, dt):
        return nc.dram_tensor(name, shape, dt, kind=skind).ap()

    xs = din("xs", [2048, D])
    xp = din("xp", [512, D])
    cache = [[din("ck0", [512, 256]), din("cv0", [512, 256])], [din("ck1", [512, 256]), din("cv1", [512, 256])]]
    cond = din("cond", [2, D])
    mod_w = [din("mod_w0", [D, 3 * D]), din("mod_w1", [D, 3 * D])]
    mod_b = [din("mod_b0", [1, 3 * D]), din("mod_b1", [1, 3 * D])]
    norm_g = [din("norm_g0", [16, 128]), din("norm_g1", [16, 128])]
    w_in = [din("w_in0", [D, IN_COLS[0]]), din("w_in1", [D, IN_COLS[1]])]
    w_out = [din("w_out0", [D, D]), din("w_out1", [D, D])]
    conv_w0 = din("conv_w0", [31, 1024])
    cvec0 = din("cvec0", [24, 128])
    qk_g = din("qk_g", [2, 128])
    sink1 = din("sink1", [1, 8])
    short_w1 = din("short_w1", [3, 1024])
    final_g = din("final_g", [1, D])
    rope_c = din("rope_c", [2048, 128])
    rope_s = din("rope_s", [2048, 128])

    ys = dout("ys", [2048, D])
    yp = dout("yp", [512, D])
    nkv = [[dout("nk0", [512, 256]), dout("nv0", [512, 256])], [dout("nk1", [512, 256]), dout("nv1", [512, 256])]]

    def contig(b):
        return [(b * 512, 512)]

    PRE_BLK = [
        [("in", [(4096, 512)])] + [("in", [(2 * i * 128, 256), (1024 + 2 * i * 128, 256)]) for i in range(4)],
        [("in", [(1024, 512)])] + [("in", [(3584 + 2 * i * 128, 256), (4608 + 2 * i * 128, 256)]) for i in range(4)],
    ]
    MAIN_BLK = [
        [("in", contig(6)), ("in", contig(9)), ("in", contig(7)), ("in", contig(10)), ("in", contig(4)),
         ("in", contig(5))] + [("out", contig(i)) for i in range(4)],
        [("in", contig(0)), ("in", contig(3)), ("in", contig(1)), ("in", contig(4))]
        + [("in", [(2560 + 2 * i * 128, 256), (5632 + 2 * i * 128, 256)]) for i in range(4)]
        + [("out", contig(i)) for i in range(4)],
    ]
    nblk = [len(PRE_BLK[L]) + len(MAIN_BLK[L]) for L in range(2)]
    wsc = [dscr("wsc%d" % L, [nblk[L], 128, 16 * 512], BF16) for L in range(2)]
    wscB = [[Buf() for _ in range(nblk[L])] for L in range(2)]
    hs = [dscr("hs%d" % L, [5, 128, 16 * 512], BF16) for L in range(2)]
    hsB = [[Buf() for _ in range(5)] for L in range(2)]
    us = [[dscr("us%d_%d" % (L, g), [128, 8, 2048 if g == 0 else 512], BF16) for g in range(2)] for L in range(2)]
    usB = [[Buf() for _ in range(5)] for L in range(2)]
    x1s = dscr("x1s", [NTOK, D], F32)
    x1B = [Buf() for _ in range(20)]
    mvec = [dscr("mvec%d" % L, [2, 3 * D], F32) for L in range(2)]
    mvecB = [[Buf() for _ in range(12)] for L in range(2)]

    ring = [k.sb("ring%d" % i, [128, 16, 512], BF16) for i in range(NR)]
    ringB = [Buf() for _ in range(NR)]
    hTt = k.sb("hTt", [128, 16, 512], BF16)
    hb = [Buf() for _ in range(16)]
    KT = k.sb("KT", [128, 2, 2560], BF16)
    KTB = Buf()
    V = k.sb("V", [128, 20, 256], BF16)
    VB = Buf()
    KTp = k.sb("KTp", [128, 2, 2, 256], BF16)
    KTpB = Buf()
    Vp = k.sb("Vp", [128, 2, 2, 256], BF16)
    VpB = Buf()
    big = k.sb("big", [128, 4, 2048], F32)
    bigB = [Buf() for _ in range(4)]
    gtb = k.sb("gtb", [128, 2048], F32)
    gtbB = Buf()
    gt_cur = [None]
    ropet = k.sb("ropet", [128, 2, 4, 128], F32)
    ropeB = Buf()
    uxb = [k.sb("uxb%d" % i, [128, 4608], BF16) for i in range(2)]
    uxB = [Buf() for _ in range(2)]
    gts = k.sb("gts", [128, 2, 512], BF16)
    gtsB = [Buf() for _ in range(2)]
    abT = k.sb("abT", [128, 16, 512], BF16)
    abB = [Buf() for _ in range(16)]
    QT = k.sb("QT", [128, 8, 512], BF16)
    QTB = [Buf() for _ in range(2)]
    gbt = k.sb("gbt", [128, 8, 512], BF16)
    gbB = [Buf() for _ in range(8)]
    ptt = [k.sb("ptt%d" % i, [128, 512], BF16) for i in range(2)]
    ptB = [Buf() for _ in range(2)]
    f512 = [k.sb("f512_%d" % i, [128, 512], F32) for i in range(5)]
    f512B = [Buf() for _ in range(5)]
    qn = [k.sb("qn%d" % i, [128, 512], BF16) for i in range(2)]
    qnB = [Buf() for _ in range(2)]
    xn_all = k.sb("xn_all", [128, 2, 2048], BF16)
    xn = [xn_all[:, i, :] for i in range(2)]
    xnB = [Buf() for _ in range(2)]
    dgs = dscr("dgs", [8, 128, 31 * 128], BF16)
    dgsB = [Buf() for _ in range(8)]
    junk = k.sb("junk", [128, 512], BF16)
    junkB = Buf()
    ut = k.sb("ut", [128, 8, 512], BF16)
    utB = Buf()
    fgb = ut[:].rearrange("p c n -> p (c n)").bitcast(F32)
    fgbB = utB
    stt = k.sb("stt", [128, 16, 8], F32)
    sttB = [Buf() for _ in range(16)]
    st_i = [0]
    identf = k.sb("identf", [128, 128], F32)
    ident = k.sb("ident", [128, 128], BF16)
    onesf = k.sb("onesf", [128, 128], F32)
    onesb = k.sb("onesb", [128, 128], BF16)
    mkge = k.sb("mkge", [128, 128], BF16)
    mkle = k.sb("mkle", [128, 128], BF16)
    mtmp = k.sb("mtmp", [128, 128], F32)
    constB = Buf()
    cw = k.sb("cw", [128, 8, 32], F32)
    sw = k.sb("sw", [128, 8, 4], F32)
    cvc = k.sb("cvc", [128, 24], F32)
    ngc = k.sb("ngc", [128, 2, 16], F32)
    parB = Buf()
    mc = k.sb("mc", [128, 64], F32)
    gsc = k.sb("gsc", [128, 2, 16], F32)
    mcB = Buf()
    stage = k.sb("stage", [128, 128], F32)
    stageB = Buf()
    gqk = k.sb("gqk", [128, 2, 2, 128], F32)
    gqkB = Buf()
    es = k.sb("es", [128, 8], F32)
    esB = Buf()
    sT = k.sb("sT", [128, 16, 2], BF16)
    sTB = Buf()
    pb = [k.ps("pb%d" % i, [128, 512], F32) for i in range(8)]
    pbB = [Buf(x=True) for _ in range(8)]

    def pbf(i, kk):
        return pb[i][:].bitcast(BF16).rearrange("p (k t) -> p k t", k=kk)

    def st_next():
        i = st_i[0]
        st_i[0] = (i + 1) % 16
        return stt[:, i, :], sttB[i]

    msc = dscr("msc", [12, 128, 16 * 512], BF16)
    mscB = [Buf() for _ in range(12)]
    MOD1_PARTS = [[], [0, 1, 2], [3, 4, 5], [6, 7, 8], [9, 10, 11]]
    wseq = []
    for L in range(2):
        for j in range(5):
            wseq += [(L, i) for i in range(len(PRE_BLK[L]))]
        for j in range(5):
            nmb = len(MAIN_BLK[L])
            wseq += [(L, len(PRE_BLK[L]) + i) for i in range(nmb - 4)]
            if L == 0 and j == 0:
                wseq += [("m0", ch) for ch in range(8, 12)]
            if L == 0:
                wseq += [("m", ch) for ch in MOD1_PARTS[j]]
            wseq += [(L, len(PRE_BLK[L]) + i) for i in range(nmb - 4, nmb)]
    ws = {"i": 0, "issued": 0}

    pcq = []

    def pc_pop(n):
        for _ in range(n):
            if pcq:
                pcq.pop(0)()

    def precast(L, lo=0, hi=99, queue=False):
        blks = PRE_BLK[L] + MAIN_BLK[L]
        for bi, (which, segs) in enumerate(blks):
            if not (lo <= bi < hi):
                continue
            if queue:
                pcq.append(lambda bi=bi: precast(L, bi, bi + 1))
                continue
            src = (w_in[L] if which == "in" else w_out[L]).rearrange("(k p) n -> p k n", p=128)
            dst = wsc[L][bi].rearrange("p (k n) -> p k n", k=16)
            off = 0
            for (c0, n) in segs:
                k.dma("pool", dst[:, :, off:off + n], src[:, :, c0:c0 + n], w=[])
                key = ("d", "pool", (k.dnext["pool"] - 1) % NS)
                wscB[L][bi].w[key] = k.dcnt["pool"][key[2]]
                off += n

    def ws_issue():
        i = ws["issued"]
        L, bi = wseq[i]
        s = i % NR
        srcB = mscB[bi] if L == "m" else (msc0B[bi - 8] if L == "m0" else wscB[L][bi])
        assert srcB.w, ("precast not issued before ring load", L, bi)
        if L == "m":
            k.dma("sp", ring[s][:].rearrange("p k n -> p (k n)"), msc[bi], r=[mscB[bi]], w=[ringB[s]])
        elif L == "m0":
            k.dma("sp", ring[s][:].rearrange("p k n -> p (k n)"), msc0[bi - 8], r=[msc0B[bi - 8]], w=[ringB[s]])
        else:
            k.dma("sp", ring[s][:].rearrange("p k n -> p (k n)"), wsc[L][bi], r=[wscB[L][bi]], w=[ringB[s]])
        ws["issued"] = i + 1

    def ws_get(L, bi):
        i = ws["i"]
        assert wseq[i] == (L, bi), (wseq[i], L, bi)
        while ws["issued"] < min(len(wseq), i + NR):
            ws_issue()
        ws["i"] = i + 1
        s = i % NR
        return ring[s], ringB[s]

    k.op("pool", lambda: POOL.memset(identf[:], 1.0), w=[constB])
    k.op("pool", lambda: POOL.affine_select(out=identf[:], in_=identf[:], pattern=[[-1, 128]], compare_op=ALU.is_equal,
                                            fill=0.0, base=0, channel_multiplier=1), r=[constB], w=[constB])
    k.op("pool", lambda: POOL.memset(onesf[:], 1.0), w=[constB])
    k.op("pool", lambda: POOL.memset(onesb[:], 1.0), w=[constB])
    k.op("dve", lambda: DVE.tensor_copy(out=ident[:], in_=identf[:]), r=[constB], w=[constB])
    k.op("pool", lambda: POOL.affine_select(out=mtmp[:], in_=onesf[:], pattern=[[-1, 128]], compare_op=ALU.is_ge,
                                            fill=0.0, base=0, channel_multiplier=1), r=[constB], w=[constB])
    k.op("dve", lambda: DVE.tensor_scalar(out=mkge[:], in0=mtmp[:], scalar1=-1.0, scalar2=30000.0, op0=ALU.add, op1=ALU.mult),
         r=[constB], w=[constB])
    k.op("pool", lambda: POOL.affine_select(out=mtmp[:], in_=onesf[:], pattern=[[1, 128]], compare_op=ALU.is_ge,
                                            fill=0.0, base=0, channel_multiplier=-1), r=[constB], w=[constB])
    k.op("dve", lambda: DVE.tensor_scalar(out=mkle[:], in0=mtmp[:], scalar1=-1.0, scalar2=30000.0, op0=ALU.add, op1=ALU.mult),
         r=[constB], w=[constB])

    k.dma("sp", big[0:2, 0, :], cond[:, :], w=[bigB[0]])
    k.op("act", lambda: ACT.activation(out=xn_all[0:2, 0, :], in_=big[0:2, 0, :], func=AF.Silu), r=[bigB[0]], w=[xnB[0]])
    pv7 = pbf(7, 16)
    k.group("pe", [(lambda kk=kk: TE.transpose(pv7[:, kk, 0:2], xn_all[0:2, 0, kk * 128:(kk + 1) * 128], ident[0:2, 0:2]))
                   for kk in range(16)], r=[xnB[0], constB], w=[pbB[7]])
    k.op("dve", lambda: DVE.tensor_copy(out=sT[:], in_=pv7[:, :, 0:2]), r=[pbB[7]], w=[sTB])

    def colload(dst, rows, n, rB=(), wB=()):
        for (ap, r0, m) in rows:
            k.dma("sp", stage[r0:r0 + m, :], ap, r=list(rB), w=[stageB])
        k.group("pe", [lambda: TE.transpose(pb[7][:, 0:n], stage[0:n, :], identf[0:n, 0:n])], r=[stageB, constB], w=[pbB[7]])
        k.op("dve", lambda: DVE.tensor_copy(out=dst, in_=pb[7][:, 0:n]), r=[pbB[7]], w=list(wB))

    colload(cvc[:, :], [(cvec0[:, :], 0, 24)], 24, wB=[parB])
    colload(ngc[:].rearrange("p l k -> p (l k)"), [(norm_g[0][:, :], 0, 16), (norm_g[1][:, :], 16, 16)], 32, wB=[parB])
    k.dma("sp", big[0:31, 1, 0:1024], conv_w0[:, :], w=[bigB[1]])
    pv7c = pb[7][:, 0:256].rearrange("p (c j) -> p c j", c=8)
    k.group("pe", [(lambda c=c: TE.transpose(pv7c[:, c, 0:31], big[0:31, 1, c * 128:(c + 1) * 128], identf[0:31, 0:31]))
                   for c in range(8)], r=[bigB[1], constB], w=[pbB[7]])
    k.op("dve", lambda: DVE.tensor_copy(out=cw[:, :, 0:31], in_=pv7c[:, :, 0:31]), r=[pbB[7]], w=[parB])
    k.dma("sp", big[0:3, 1, 0:1024], short_w1[:, :], w=[bigB[1]])
    pv7s = pb[7][:, 0:32].rearrange("p (c j) -> p c j", c=8)
    k.group("pe", [(lambda c=c: TE.transpose(pv7s[:, c, 0:3], big[0:3, 1, c * 128:(c + 1) * 128], identf[0:3, 0:3]))
                   for c in range(8)], r=[bigB[1], constB], w=[pbB[7]])
    k.op("dve", lambda: DVE.tensor_copy(out=sw[:, :, 0:3], in_=pv7s[:, :, 0:3]), r=[pbB[7]], w=[parB])
    for i in range(2):
        k.dma("sp", gqk[:, i, 0, :], qk_g[i:i + 1, :].partition_broadcast(128), w=[gqkB])
    for i in range(2):
        g5 = gqk[:, i, 0, :].rearrange("p (a w e) -> p a w e", a=2, w=2)
        s5 = gqk[:, i, 1, :].rearrange("p (a w e) -> p a w e", a=2, w=2)
        k.op("dve", lambda g5=g5, s5=s5: DVE.tensor_copy(out=s5[:, :, 0, :], in_=g5[:, :, 1, :]), r=[gqkB], w=[gqkB])
        k.op("dve", lambda g5=g5, s5=s5: DVE.tensor_copy(out=s5[:, :, 1, :], in_=g5[:, :, 0, :]), r=[gqkB], w=[gqkB])
    k.dma("sp", es[:, :], sink1[0:1, :].partition_broadcast(128), w=[esB])
    k.op("act", lambda: ACT.activation(out=es[:, :], in_=es[:, :], func=AF.Exp), r=[esB], w=[esB])

    def build_diag(c):
        dgt = uxb[0][:, 0:4096]
        dg3 = dgt[:, 0:31 * 128].rearrange("p (j m) -> p j m", j=31)
        for jj in range(31):
            if jj % 2 == 0:
                k.op("act", lambda jj=jj: ACT.activation(out=dg3[:, jj, :], in_=identf[:, :], func=AF.Identity, scale=cw[:, c, jj:jj + 1]),
                     r=[constB, parB], w=[uxB[0]])
            else:
                k.op("dve", lambda jj=jj: DVE.tensor_scalar(out=dg3[:, jj, :], in0=identf[:, :], scalar1=cw[:, c, jj:jj + 1],
                                                           scalar2=None, op0=ALU.mult), r=[constB, parB], w=[uxB[0]])
        k.dma("pst", dgs[c], dgt[:, 0:31 * 128], r=[uxB[0]], w=[dgsB[c]])

    mtile = f512[4][0:2, :]
    mtileB = f512B[4]

    def adaln_block(L, ch, slot, slotB, bk=7):
        k.dma("sp", f512[0][0:2, :], mod_b[L][0:1, ch * 512:(ch + 1) * 512].partition_broadcast(2), w=[f512B[0]])
        k.group("pe", [(lambda kk=kk: TE.matmul(pb[bk][0:2, :], lhsT=sT[:, kk, :], rhs=slot[:, kk, :],
                                                start=(kk == 0), stop=(kk == 15))) for kk in range(16)],
                r=[slotB, sTB], w=[pbB[bk]])
        k.op("dve", lambda: DVE.tensor_tensor(out=mtile[:, 0:512], in0=pb[bk][0:2, :], in1=f512[0][0:2, :], op=ALU.add),
             r=[pbB[bk], f512B[0]], w=[mtileB])
        k.dma("sp", mvec[L][:, ch * 512:(ch + 1) * 512], mtile[:, 0:512], r=[mtileB], w=[mvecB[L][ch]])

    msc0 = dscr("msc0", [4, 128, 16 * 512], BF16)
    msc0B = [Buf() for _ in range(4)]

    def adaln0_late():
        for ch in range(8, 12):
            slot, slotB = ws_get("m0", ch)
            adaln_block(0, ch, slot, slotB)

    def adaln0():
        mwv = mod_w[0].rearrange("(k p) n -> p k n", p=128)
        for ch in range(8):
            s_ = ch % NR
            k.dma("pool", ring[s_][:, :, :], mwv[:, :, ch * 512:(ch + 1) * 512], w=[ringB[s_]])
            adaln_block(0, ch, ring[s_], ringB[s_])

    def precast_mod1():
        mwv = mod_w[1].rearrange("(k p) n -> p k n", p=128)
        for ch in range(12):
            pcq.append(lambda ch=ch: k.dma("pool", msc[ch].rearrange("p (k n) -> p k n", k=16), mwv[:, :, ch * 512:(ch + 1) * 512],
                                           w=[mscB[ch]]))

    def adaln1_part(j):
        for ch in MOD1_PARTS[j]:
            slot, slotB = ws_get("m", ch)
            adaln_block(1, ch, slot, slotB)

    def load_mod_cols(L):
        rows = []
        for g in range(2):
            rows.append((mvec[L][g:g + 1, 0:2048].rearrange("o (k p) -> (o k) p", p=128), 32 * g, 16))
            rows.append((mvec[L][g:g + 1, 2048:4096].rearrange("o (k p) -> (o k) p", p=128), 32 * g + 16, 16))
        colload(mc[:, :], rows, 64, rB=mvecB[L][0:8], wB=[mcB])
        for g in range(2):
            k.op("dve", lambda g=g: DVE.tensor_scalar(out=gsc[:, g, :], in0=mc[:, 32 * g + 16:32 * g + 32], scalar1=1.0,
                                                      scalar2=None, op0=ALU.add), r=[mcB], w=[mcB])
            k.op("dve", lambda g=g: DVE.tensor_tensor(out=gsc[:, g, :], in0=gsc[:, g, :], in1=ngc[:, L, :], op=ALU.mult),
                 r=[mcB, parB], w=[mcB])

    def load_gt(L, g):
        if gt_cur[0] == (L, g):
            return
        gt_cur[0] = (L, g)
        k.dma("sp", gtb[:, :], mvec[L][g:g + 1, 4096:6144].partition_broadcast(128), r=mvecB[L], w=[gtbB])

    def xsrc(L, t):
        if L == 1:
            return x1s[t:t + 128, :]
        return xs[t:t + 128, :] if t < 2048 else xp[t - 2048:t - 2048 + 128, :]

    def load_rope(L, t0, which):
        k.dma("sp", ropet[:, 0, :, :], rope_c[t0:t0 + 512, :].rearrange("(tb p) d -> p tb d", p=128), w=[ropeB])
        k.dma("sp", ropet[:, 1, :, :], rope_s[t0:t0 + 512, :].rearrange("(tb p) d -> p tb d", p=128), w=[ropeB])
        if L == 0:
            for cs_ in range(2):
                k.op("dve", lambda cs_=cs_: DVE.tensor_tensor(
                    out=ropet[:, cs_, :, :], in0=ropet[:, cs_, :, :],
                    in1=gqk[:, which, cs_, :].unsqueeze(1).to_broadcast([128, 4, 128]), op=ALU.mult),
                    r=[ropeB, gqkB], w=[ropeB])

    def rope_apply(src_ap, H, tb, dst_f32, dstB, srcB):
        t1, t1B = f512[0], f512B[0]
        x3 = src_ap.rearrange("p (h d) -> p h d", h=H)
        x5 = src_ap.rearrange("p (h a w e) -> p h a w e", h=H, a=2, w=2)
        c3 = ropet[:, 0, tb, :].unsqueeze(1).to_broadcast([128, H, 128])
        s5 = ropet[:, 1, tb, :].rearrange("p (a w e) -> p a w e", a=2, w=2)
        d3 = dst_f32.rearrange("p (h d) -> p h d", h=H)
        t5 = t1[:, 0:H * 128].rearrange("p (h a w e) -> p h a w e", h=H, a=2, w=2)
        k.op("dve", lambda: DVE.tensor_tensor(out=d3, in0=x3, in1=c3, op=ALU.mult), r=[srcB, ropeB], w=[dstB])
        k.op("dve", lambda: DVE.tensor_tensor(out=t5[:, :, :, 0, :], in0=x5[:, :, :, 1, :],
                                              in1=s5[:, :, 0, :].unsqueeze(1).to_broadcast([128, H, 2, 32]), op=ALU.mult),
             r=[srcB, ropeB], w=[t1B])
        k.op("dve", lambda: DVE.tensor_tensor(out=t5[:, :, :, 1, :], in0=x5[:, :, :, 0, :],
                                              in1=s5[:, :, 1, :].unsqueeze(1).to_broadcast([128, H, 2, 32]), op=ALU.mult),
             r=[srcB, ropeB, t1B], w=[t1B])
        k.op("dve", lambda: DVE.tensor_tensor(out=dst_f32, in0=dst_f32, in1=t1[:, 0:H * 128], op=ALU.add),
             r=[t1B, dstB], w=[dstB])

    def head_rstd(src_ap, H, srcB):
        st, stB = st_next()
        for hh in range(H):
            k.op("act", lambda hh=hh: ACT.activation(out=junk[:, hh * 128:(hh + 1) * 128], in_=src_ap[:, hh * 128:(hh + 1) * 128],
                                                     func=AF.Square, accum_out=st[:, hh:hh + 1]), r=[srcB], w=[junkB, stB])
        k.op("act", lambda: ACT.activation(out=st[:, 0:H], in_=st[:, 0:H], func=AF.Sqrt, bias=EPS, scale=1.0 / 128),
             r=[stB], w=[stB])
        k.op("dve", lambda: DVE.reciprocal(out=st[:, 0:H], in_=st[:, 0:H]), r=[stB], w=[stB])
        return st[:, 0:H], stB

    hTbuf = [hTt, abT]
    hTB = [hb, abB]

    def build_tb(L, j, tb):
        build_a(L, j, tb)
        build_b(L, j, tb)

    def build_a(L, j, tb):
        if j >= 5:
            return
        g, t0 = TILES[j]
        t = t0 + tb * 128
        s = tb % 2
        k.dma("sp", big[:, s, :], xsrc(L, t), r=([x1B[t // 128]] if L == 1 else []), w=[bigB[s]])
        st, stB = st_next()
        k.op("act", lambda: ACT.activation(out=xn[s], in_=big[:, s, :], func=AF.Square, accum_out=st[:, 0:1]),
             r=[bigB[s]], w=[xnB[s], stB])
        k.op("act", lambda: ACT.activation(out=st[:, 1:2], in_=st[:, 0:1], func=AF.Sqrt, bias=EPS, scale=1.0 / D), r=[stB], w=[stB])
        k.op("dve", lambda: DVE.reciprocal(out=st[:, 2:3], in_=st[:, 1:2]), r=[stB], w=[stB])
        k.op("dve", lambda: DVE.tensor_single_scalar(out=xn[s], in_=big[:, s, :], scalar=st[:, 2:3], op=ALU.mult),
             r=[bigB[s], stB], w=[xnB[s]])

    def build_b(L, j, tb):
        if j >= 5:
            return
        g, t0 = TILES[j]
        hT_, hb_ = hTbuf[j % 2], hTB[j % 2]
        s = tb % 2
        pa, pbk = 2 * s, 2 * s + 1
        va, vb = pbf(pa, 8), pbf(pbk, 8)
        k.group("pe", [(lambda kk=kk: TE.transpose((va if kk < 8 else vb)[:, kk % 8, :], xn_all[:, s, kk * 128:(kk + 1) * 128], ident[:]))
                       for kk in range(16)], r=[xnB[s], constB], w=[pbB[pa], pbB[pbk]])
        for kk in range(16):
            src = (va if kk < 8 else vb)[:, kk % 8, :]
            dst = hT_[:, kk, tb * 128:(tb + 1) * 128]
            if kk < 8:
                k.op("act", lambda src=src, dst=dst, kk=kk: ACT.activation(
                    out=dst, in_=src, func=AF.Identity, bias=mc[:, 32 * g + kk:32 * g + kk + 1],
                    scale=gsc[:, g, kk:kk + 1]), r=[pbB[pa], mcB], w=[hb_[kk]])
            else:
                k.op("dve", lambda src=src, dst=dst, kk=kk: DVE.tensor_scalar(
                    out=dst, in0=src, scalar1=gsc[:, g, kk:kk + 1], scalar2=mc[:, 32 * g + kk:32 * g + kk + 1],
                    op0=ALU.mult, op1=ALU.add), r=[pbB[pbk], mcB], w=[hb_[kk]])
        if tb == 3:
            k.dma("pst", hs[L][j], hT_[:].rearrange("p k n -> p (k n)"), r=hb_, w=[hsB[L][j]])

    def pre(L):
        for tb in range(4):
            build_tb(L, 0, tb)
        for j, (g, t0) in enumerate(TILES):
            hTt_, hb_ = hTbuf[j % 2], hTB[j % 2]
            if g == 0:
                load_rope(L, t0, 1)
            slot, slotB = ws_get(L, 0)
            kv_tail = [None]
            for tb in range(4):
                t = t0 + tb * 128
                a = 4 + tb % 2
                k.group("pe", [(lambda kk=kk, tb=tb, a=a, slot=slot: TE.matmul(
                    pb[a][:, :], lhsT=hTt_[:, kk, tb * 128:(tb + 1) * 128], rhs=slot[:, kk, :], start=(kk == 0), stop=(kk == 15)))
                    for kk in range(16)], r=[slotB] + hb_, w=[pbB[a]])
                kf, kfB = f512[1 + tb % 2], f512B[1 + tb % 2]
                raw, rawB = f512[3 + tb % 2], f512B[3 + tb % 2]
                k.op("act", lambda a=a, raw=raw: ACT.copy(out=raw[:, :], in_=pb[a][:, :]), r=[pbB[a]], w=[rawB])
                kraw = raw[:, 0:256]
                if kv_tail[0] is not None:
                    kv_tail[0]()
                    kv_tail[0] = None
                if g == 1:
                    tl = t - 2048
                    seq, blk = tl // 256, (tl % 256) // 128
                    if L == 0:
                        rk, rkB = head_rstd(kraw, 2, rawB)
                        k.op("dve", lambda kf=kf, kraw=kraw, rk=rk: DVE.tensor_tensor(
                            out=kf[:, 0:256].rearrange("p (h d) -> p h d", h=2), in0=kraw.rearrange("p (h d) -> p h d", h=2),
                            in1=rk.unsqueeze(2).to_broadcast([128, 2, 128]), op=ALU.mult), r=[rawB, rkB], w=[kfB])
                        k.op("dve", lambda kf=kf: DVE.tensor_tensor(
                            out=kf[:, 0:256].rearrange("p (h d) -> p h d", h=2), in0=kf[:, 0:256].rearrange("p (h d) -> p h d", h=2),
                            in1=gqk[:, 1, 0, :].unsqueeze(1).to_broadcast([128, 2, 128]), op=ALU.mult), r=[kfB, gqkB], w=[kfB])
                    else:
                        kf, kfB = raw, rawB
                    k.dma("pst", nkv[L][0][tl:tl + 128, :], kf[:, 0:256], r=[kfB])
                    k.dma("pst", nkv[L][1][tl:tl + 128, :], raw[:, 256:512], r=[rawB])
                    kb16, kb16B = qn[tb % 2], qnB[tb % 2]
                    k.op("dve", lambda kf=kf, kb16=kb16: DVE.tensor_copy(out=kb16[:, 0:256], in_=kf[:, 0:256]), r=[kfB], w=[kb16B])
                    k.op("pool", lambda raw=raw, seq=seq, blk=blk: POOL.tensor_copy(out=Vp[:, seq, blk, :], in_=raw[:, 256:512]),
                         r=[rawB], w=[VpB])
                    def tail(tb=tb, kb16=kb16, kb16B=kb16B, seq=seq, blk=blk):
                        pv = pbf(6 + tb % 2, 8)
                        k.group("pe", [(lambda hh=hh: TE.transpose(pv[:, hh, :], kb16[:, hh * 128:(hh + 1) * 128], ident[:]))
                                       for hh in range(2)], r=[kb16B, constB], w=[pbB[6 + tb % 2]])
                        k.op("act", lambda: ACT.copy(out=KTp[:, seq, :, blk * 128:(blk + 1) * 128], in_=pv[:, 0:2, :]),
                             r=[pbB[6 + tb % 2]], w=[KTpB])
                    kv_tail[0] = tail
                else:
                    rope_apply(kraw, 2, tb, kf[:, 0:256], kfB, rawB)
                    kb16, kb16B = qn[tb % 2], qnB[tb % 2]
                    if L == 0:
                        rk, rkB = head_rstd(kraw, 2, rawB)
                        k.op("dve", lambda kf=kf, rk=rk, kb16=kb16: DVE.tensor_tensor(
                            out=kb16[:, 0:256].rearrange("p (h d) -> p h d", h=2), in0=kf[:, 0:256].rearrange("p (h d) -> p h d", h=2),
                            in1=rk.unsqueeze(2).to_broadcast([128, 2, 128]), op=ALU.mult), r=[kfB, rkB], w=[kb16B])
                    else:
                        k.op("dve", lambda kf=kf, kb16=kb16: DVE.tensor_copy(out=kb16[:, 0:256], in_=kf[:, 0:256]), r=[kfB], w=[kb16B])
                    blkg = 4 + t // 128
                    k.op("pool", lambda raw=raw, blkg=blkg: POOL.tensor_copy(out=V[:, blkg, :], in_=raw[:, 256:512]), r=[rawB], w=[VB])
                    def tail(tb=tb, kb16=kb16, kb16B=kb16B, t=t):
                        pv = pbf(6 + tb % 2, 8)
                        k.group("pe", [(lambda hh=hh: TE.transpose(pv[:, hh, :], kb16[:, hh * 128:(hh + 1) * 128], ident[:]))
                                       for hh in range(2)], r=[kb16B, constB], w=[pbB[6 + tb % 2]])
                        k.op("act", lambda: ACT.copy(out=KT[:, :, 512 + t:512 + t + 128], in_=pv[:, 0:2, :]),
                             r=[pbB[6 + tb % 2]], w=[KTB])
                    kv_tail[0] = tail
            for i in range(4):
                if i > 0:
                    build_b(L, j + 1, i - 1)
                if i == 1 and kv_tail[0] is not None:
                    kv_tail[0]()
                    kv_tail[0] = None
                build_a(L, j + 1, i)
                if L == 0 and j < 4 and i in (1, 3):
                    build_diag(2 * j + (i - 1) // 2)
                slot, slotB = ws_get(L, 1 + i)
                for cc in range(2):
                    c = 2 * i + cc
                    pa, pbk = 4 + 2 * cc, 5 + 2 * cc
                    k.group("pe", [(lambda kk=kk, cc=cc, pa=pa, slot=slot: TE.matmul(
                        pb[pa][:, :], lhsT=slot[:, kk, cc * 128:(cc + 1) * 128], rhs=hTt_[:, kk, :], start=(kk == 0), stop=(kk == 15)))
                        for kk in range(16)], r=[slotB] + hb_, w=[pbB[pa]])
                    k.group("pe", [(lambda kk=kk, cc=cc, pbk=pbk, slot=slot: TE.matmul(
                        pb[pbk][:, :], lhsT=slot[:, kk, 256 + cc * 128:256 + (cc + 1) * 128], rhs=hTt_[:, kk, :], start=(kk == 0), stop=(kk == 15)))
                        for kk in range(16)], r=[slotB] + hb_, w=[pbB[pbk]])
                    sg, sgB = f512[3 + cc], f512B[3 + cc]
                    if L == 0:
                        k.op("act", lambda sg=sg, pbk=pbk: ACT.activation(out=sg[:, :], in_=pb[pbk][:, :], func=AF.Sigmoid),
                             r=[pbB[pbk]], w=[sgB])
                    else:
                        k.op("act", lambda sg=sg, pbk=pbk: ACT.copy(out=sg[:, :], in_=pb[pbk][:, :]), r=[pbB[pbk]], w=[sgB])
                    k.op("dve", lambda sg=sg, pa=pa, c=c: DVE.tensor_tensor(out=ut[:, c, :], in0=pb[pa][:, :], in1=sg[:, :], op=ALU.mult),
                         r=[pbB[pa], sgB], w=[utB])
            tl = t0 if g == 0 else 0
            k.dma("pst", us[L][g][:, :, tl:tl + 512], ut[:, :, :], r=[utB], w=[usB[L][j]])
            build_b(L, j + 1, 3)
            pc_pop(1 if L == 0 else 3)
        for half in range(2):
            s = half
            k.dma("sp", big[:, s, 0:1024].rearrange("p (b n) -> p b n", b=4),
                  cache[L][half].rearrange("(b p) n -> p b n", p=128), w=[bigB[s]])
        k.op("dve", lambda: DVE.tensor_copy(out=xn_all[:, 0, 0:1024], in_=big[:, 0, 0:1024]), r=[bigB[0]], w=[xnB[0]])
        k.op("act", lambda: ACT.copy(out=V[:, 0:4, :], in_=big[:, 1, 0:1024].rearrange("p (b n) -> p b n", b=4)), r=[bigB[1]], w=[VB])
        for b_ in range(4):
            pv = pbf(6 + b_ % 2, 8)
            k.group("pe", [(lambda hh=hh, b_=b_, pv=pv: TE.transpose(pv[:, hh, :], xn_all[:, 0, b_ * 256 + hh * 128:b_ * 256 + (hh + 1) * 128], ident[:]))
                           for hh in range(2)], r=[xnB[0], constB], w=[pbB[6 + b_ % 2]])
            k.op("act", lambda pv=pv, b_=b_: ACT.copy(out=KT[:, :, b_ * 128:(b_ + 1) * 128], in_=pv[:, 0:2, :]),
                 r=[pbB[6 + b_ % 2]], w=[KTB])

    def attention(L, j, g, hh, h, out_chunk):
        kvh = h // 4
        ob = 5 if h % 2 == 0 else 2
        chains = []
        if g == 1:
            for s in range(2):
                items = [(KTp[:, s, kvh, b_ * 128:(b_ + 1) * 128], Vp[:, s, b_, kvh * 128:(kvh + 1) * 128], s * 256, 256, [])
                         for b_ in range(2)]
                chains.append(items)
        else:
            items = []
            if L == 0:
                for kb in range(20):
                    items.append((KT[:, kvh, kb * 128:(kb + 1) * 128], V[:, kb, kvh * 128:(kvh + 1) * 128], 0, 512, []))
            else:
                for kb in range(4):
                    items.append((KT[:, kvh, kb * 128:(kb + 1) * 128], V[:, kb, kvh * 128:(kvh + 1) * 128], 0, 512, []))
                for kb in range(max(0, 4 * j - 1), min(15, 4 * j + 4) + 1):
                    lo, hi = max(4 * j, kb - 1), min(4 * j + 3, kb + 1)
                    masks = []
                    for qb in range(lo, hi + 1):
                        if qb == kb + 1:
                            masks.append(((qb - 4 * j) * 128, mkge))
                        elif qb == kb - 1:
                            masks.append(((qb - 4 * j) * 128, mkle))
                    items.append((KT[:, kvh, 512 + kb * 128:512 + (kb + 1) * 128], V[:, 4 + kb, kvh * 128:(kvh + 1) * 128],
                                  (lo - 4 * j) * 128, (hi - lo + 1) * 128, masks))
            chains.append(items)
        kB, vB = (KTpB, VpB) if g == 1 else (KTB, VB)
        flat = [(ci, ii, it) for ci, items in enumerate(chains) for ii, it in enumerate(items)]
        n = len(flat)

        SB_ = [3, 4, 7]
        ptt3 = [ptt[0], ptt[1], qn[0]]
        ptB3 = [ptB[0], ptB[1], qnB[0]]

        def qk(idx):
            ci, ii, (kap, vap, c0, nn, masks) = flat[idx]
            s = idx % 3
            fns = [lambda: TE.matmul(pb[SB_[s]][:, c0:c0 + nn], lhsT=kap, rhs=QT[:, h, c0:c0 + nn], start=True, stop=(len(masks) == 0),
                                     skip_group_check=True)]
            for mi, (mc0, mk) in enumerate(masks):
                fns.append(lambda mc0=mc0, mk=mk, mi=mi: TE.matmul(pb[SB_[s]][:, mc0:mc0 + 128], lhsT=ident[:, :], rhs=mk[:, :], start=False,
                                                                   stop=(mi == len(masks) - 1), skip_group_check=True))
            k.group("pe", fns, r=[kB, QTB[h // 4], constB], w=[pbB[SB_[s]]])

        qk(0)
        if n > 1:
            qk(1)
        for idx in range(n):
            ci, ii, (kap, vap, c0, nn, masks) = flat[idx]
            s = idx % 3
            if idx + 2 < n:
                qk(idx + 2)
            k.op("act", lambda: ACT.activation(out=ptt3[s][:, c0:c0 + nn], in_=pb[SB_[s]][:, c0:c0 + nn], func=AF.Exp, scale=SCALE),
                 r=[pbB[SB_[s]]], w=[ptB3[s]])
            first, last = (ii == 0), (ii == len(chains[ci]) - 1)
            k.group("pe", [
                lambda: TE.matmul(pb[ob][:, c0:c0 + nn], lhsT=vap, rhs=ptt3[s][:, c0:c0 + nn], start=first, stop=last,
                                  skip_group_check=True),
                lambda: TE.matmul(pb[6][:, c0:c0 + nn], lhsT=onesb[:, :], rhs=ptt3[s][:, c0:c0 + nn], start=first, stop=last,
                                  skip_group_check=True)],
                r=[ptB3[s], vB, constB], w=[pbB[ob], pbB[6]])
        rv, rvB = f512[1 + h % 2], f512B[1 + h % 2]
        if L == 1:
            k.op("dve", lambda: DVE.tensor_scalar(out=rv[:, :], in0=pb[6][:, :], scalar1=es[:, h:h + 1], scalar2=None, op0=ALU.add),
                 r=[pbB[6], esB], w=[rvB])
            k.op("dve", lambda: DVE.reciprocal(out=rv[:, :], in_=rv[:, :]), r=[rvB], w=[rvB])
        else:
            k.op("dve", lambda: DVE.reciprocal(out=rv[:, :], in_=pb[6][:, :]), r=[pbB[6]], w=[rvB])
        k.op("dve", lambda: DVE.tensor_tensor(out=rv[:, :], in0=pb[ob][:, :], in1=rv[:, :], op=ALU.mult), r=[pbB[ob], rvB], w=[rvB])
        k.op("dve", lambda: DVE.tensor_tensor(out=abT[:, out_chunk, :], in0=rv[:, :], in1=gbt[:, h, :], op=ALU.mult),
             r=[rvB, gbB[h]], w=[abB[out_chunk]])

    def main(L):
        H_ = 15 if L == 0 else 1
        nP = len(PRE_BLK[L])
        if L == 1:
            k.dma("sp", fgb, final_g[0:1, :].partition_broadcast(128), w=[fgbB])
        def dgbuf(c):
            return (ut[:].rearrange("p c n -> p (c n)"), [utB]) if c % 2 == 0 else (xn_all[:].rearrange("p a n -> p (a n)"), xnB)

        def dg_load(c):
            dgt, dgtB = dgbuf(c)
            k.dma("sp", dgt[:, 0:31 * 128], dgs[c], r=[dgsB[c]], w=dgtB)

        def uviews(j):
            g, t0 = TILES[j]
            ux = uxb[j % 2]
            if g == 0:
                W_ = 512 + 2 * H_
                return ux, ux[:, 0:8 * W_].rearrange("p (c w) -> p c w", c=8)
            W_ = 256 + 2 * H_
            return ux, ux[:, 0:16 * W_].rearrange("p (c s w) -> p c s w", c=8, s=2)

        def tile_loads(L_, j):
            if j >= 5:
                return
            g, t0 = TILES[j]
            k.dma("sp", hTt[:].rearrange("p k n -> p (k n)"), hs[L][j], r=[hsB[L][j]], w=hb)
            ux, uv = uviews(j)
            uB = uxB[j % 2]
            if g == 0:
                lo = H_ if j > 0 else 0
                hi = H_ if j < 3 else 0
                if lo == 0:
                    k.op("pool", lambda: POOL.memset(uv[:, :, 0:H_], 0.0), w=[uB])
                if hi == 0:
                    k.op("pool", lambda: POOL.memset(uv[:, :, H_ + 512:H_ + 512 + H_], 0.0), w=[uB])
                rb = [usB[L][jj] for jj in range(max(0, j - 1), min(3, j + 1) + 1)]
                k.dma("sp", uv[:, :, H_ - lo:H_ + 512 + hi], us[L][0][:, :, t0 - lo:t0 + 512 + hi], r=rb, w=[uB])
            else:
                W_ = 256 + 2 * H_
                k.op("pool", lambda: POOL.memset(ux[:, 0:16 * W_], 0.0), w=[uB])
                k.dma("sp", uv[:, :, :, H_:H_ + 256], us[L][1][:, :, :].rearrange("p c (s w) -> p c s w", s=2), r=[usB[L][4]], w=[uB])
            if L == 0:
                dg_load(0)
                dg_load(1)

        tile_loads(L, 0)
        for j, (g, t0) in enumerate(TILES):
            if L == 0 and j == 0:
                pc_pop(5)
            ux, uv = uviews(j)
            uB = uxB[j % 2]
            if g == 0:
                nseg, sl = 1, 512

                def uwin(c, jj, uv=uv):
                    return uv[:, c, jj:jj + 512]
            else:
                nseg, sl = 2, 256

                def uwin(c, jj, uv=uv):
                    return uv[:, c, :, jj:jj + 256]

            def seg(ap):
                return ap if g == 0 else ap.rearrange("p (s w) -> p s w", s=2)

            ap_ = big[:, 2:4, :].rearrange("p s (c n) -> p (s c) n", c=4)

            def apB(c):
                return bigB[2 + c // 4]
            mean, rstd = big[:, 0, 0:512], big[:, 0, 512:1024]
            meanB = rstdB = bigB[0]
            cur = {}
            pend = {}

            def conv_piece(c):
                dgt, dgtB = dgbuf(c)
                dg3 = dgt[:, 0:31 * 128].rearrange("p (j m) -> p j m", j=31)
                pc = 3 + c % 2
                fns = []
                for sgi in range(nseg):
                    for jj in range(31):
                        if g == 0:
                            fns.append(lambda jj=jj: TE.matmul(pb[pc][:, :], lhsT=dg3[:, jj, :], rhs=uwin(c, jj),
                                                               start=(jj == 0), stop=(jj == 30), skip_group_check=True))
                        else:
                            fns.append(lambda jj=jj, sgi=sgi: TE.matmul(
                                pb[pc][:, sgi * 256:(sgi + 1) * 256], lhsT=dg3[:, jj, :], rhs=uwin(c, jj)[:, sgi, :],
                                start=(jj == 0), stop=(jj == 30), skip_group_check=True))
                k.group("pe", fns, r=[uB] + dgtB, w=[pbB[pc]])
                if c + 2 < 8:
                    dg_load(c + 2)
                k.op("act", lambda: ACT.activation(out=ap_[:, c, :], in_=pb[pc][:, :], func=AF.Identity, bias=cvc[:, c:c + 1],
                                                   scale=1.0), r=[pbB[pc], parB], w=[apB(c)])
                sq = big[:, 1, (c % 2) * 512:(c % 2 + 1) * 512]
                k.op("act", lambda: ACT.activation(out=sq, in_=ap_[:, c, :], func=AF.Square), r=[apB(c)], w=[bigB[1]])

                def stats():
                    k.group("pe", [lambda: TE.matmul(pb[5][:, :], lhsT=onesf[:, :], rhs=ap_[:, c, :], start=(c == 0), stop=(c == 7),
                                                     skip_group_check=True),
                                   lambda: TE.matmul(pb[6][:, :], lhsT=onesf[:, :], rhs=sq, start=(c == 0), stop=(c == 7),
                                                     skip_group_check=True)],
                            r=[apB(c), bigB[1], constB], w=[pbB[5], pbB[6]])
                if "s" in pend:
                    pend.pop("s")()
                pend["s"] = stats

            def ln_finalize():
                pend.pop("s")()
                k.op("dve", lambda: DVE.tensor_scalar(out=mean, in0=pb[5][:, :], scalar1=1.0 / 1024, scalar2=None, op0=ALU.mult),
                     r=[pbB[5]], w=[meanB])
                k.op("dve", lambda: DVE.tensor_tensor(out=rstd, in0=mean, in1=mean, op=ALU.mult), r=[meanB], w=[rstdB])
                k.op("dve", lambda: DVE.scalar_tensor_tensor(out=rstd, in0=pb[6][:, :], scalar=1.0 / 1024, in1=rstd,
                                                             op0=ALU.mult, op1=ALU.subtract), r=[pbB[6], rstdB], w=[rstdB])
                k.op("act", lambda: ACT.activation(out=rstd, in_=rstd, func=AF.Sqrt, bias=EPS, scale=1.0), r=[rstdB], w=[rstdB])
                k.op("dve", lambda: DVE.reciprocal(out=rstd, in_=rstd), r=[rstdB], w=[rstdB])

            def gate_ln_piece(c):
                if c % 4 == 0:
                    cur["ga"] = ws_get(L, nP + 4 + c // 4)
                slot, slotB = cur["ga"]
                pa = c % 2
                k.group("pe", [(lambda kk=kk: TE.matmul(
                    pb[pa][:, :], lhsT=slot[:, kk, (c % 4) * 128:(c % 4 + 1) * 128], rhs=hTt[:, kk, :], start=(kk == 0), stop=(kk == 15)))
                    for kk in range(16)], r=[slotB] + hb, w=[pbB[pa]])
                k.op("act", lambda: ACT.activation(out=gts[:, pa, :], in_=pb[pa][:, :], func=AF.Silu), r=[pbB[pa]], w=[gtsB[pa]])
                k.op("dve", lambda: DVE.tensor_tensor(out=ap_[:, c, :], in0=ap_[:, c, :], in1=mean, op=ALU.subtract),
                     r=[apB(c), meanB], w=[apB(c)])
                k.op("dve", lambda: DVE.tensor_tensor(out=ap_[:, c, :], in0=ap_[:, c, :], in1=rstd, op=ALU.mult),
                     r=[apB(c), rstdB], w=[apB(c)])
                k.op("act", lambda: ACT.activation(out=ap_[:, c, :], in_=ap_[:, c, :], func=AF.Silu, bias=cvc[:, 16 + c:17 + c],
                                                   scale=cvc[:, 8 + c:9 + c]), r=[apB(c), parB], w=[apB(c)])
                k.op("dve", lambda: DVE.tensor_tensor(out=abT[:, c, :], in0=ap_[:, c, :], in1=gts[:, pa, :], op=ALU.mult),
                     r=[apB(c), gtsB[pa]], w=[abB[c]])

            def d_piece(c):
                i, cc = c // 2, c % 2
                if cc == 0:
                    cur["d"] = ws_get(L, nP + 4 + i)
                slot, slotB = cur["d"]
                k.group("pe", [(lambda kk=kk: TE.matmul(
                    pb[0][:, :], lhsT=slot[:, kk, cc * 128:(cc + 1) * 128], rhs=hTt[:, kk, :], start=(kk == 0), stop=(kk == 15)))
                    for kk in range(16)], r=[slotB] + hb, w=[pbB[0]])
                k.group("pe", [(lambda kk=kk: TE.matmul(
                    pb[1][:, :], lhsT=slot[:, kk, 256 + cc * 128:256 + (cc + 1) * 128], rhs=hTt[:, kk, :], start=(kk == 0), stop=(kk == 15)))
                    for kk in range(16)], r=[slotB] + hb, w=[pbB[1]])
                a3, a3B = f512[3 + cc], f512B[3 + cc]
                sgd = gts[:, cc, :]
                k.op("act", lambda: ACT.activation(out=sgd, in_=pb[1][:, :], func=AF.Silu), r=[pbB[1]], w=[gtsB[cc]])
                k.op("dve", lambda: DVE.tensor_scalar(out=seg(a3[:, :]), in0=uwin(c, 0), scalar1=sw[:, c, 0:1], scalar2=None,
                                                      op0=ALU.mult), r=[uB, parB], w=[a3B])
                for jj in range(1, 3):
                    k.op("dve", lambda jj=jj: DVE.scalar_tensor_tensor(out=seg(a3[:, :]), in0=uwin(c, jj), scalar=sw[:, c, jj:jj + 1],
                                                                         in1=seg(a3[:, :]), op0=ALU.mult, op1=ALU.add),
                         r=[uB, parB, a3B], w=[a3B])
                k.op("dve", lambda: DVE.tensor_tensor(out=a3[:, :], in0=pb[0][:, :], in1=a3[:, :], op=ALU.mult), r=[pbB[0], a3B], w=[a3B])
                k.op("dve", lambda: DVE.tensor_tensor(out=abT[:, 8 + c, :], in0=a3[:, :], in1=sgd, op=ALU.mult),
                     r=[a3B, gtsB[cc]], w=[abB[8 + c]])

            def q_piece(grp, tb):
                if tb == 0:
                    cur["q"] = ws_get(L, nP + 2 * grp)
                slot, slotB = cur["q"]
                pa = tb % 2
                k.group("pe", [(lambda kk=kk: TE.matmul(
                    pb[pa][:, :], lhsT=hTt[:, kk, tb * 128:(tb + 1) * 128], rhs=slot[:, kk, :], start=(kk == 0), stop=(kk == 15)))
                    for kk in range(16)], r=[slotB] + hb, w=[pbB[pa]])
                qf, qfB = f512[3 + tb % 2], f512B[3 + tb % 2]
                q16, q16B = qn[tb % 2], qnB[tb % 2]
                qr, qrB = f512[1 + tb % 2], f512B[1 + tb % 2]
                k.op("act", lambda: ACT.copy(out=qr[:, :], in_=pb[pa][:, :]), r=[pbB[pa]], w=[qrB])
                if L == 0:
                    rq, rqB = head_rstd(qr[:, :], 4, qrB)
                if g == 0:
                    rope_apply(qr[:, :], 4, tb, qf[:, :], qfB, qrB)
                    if L == 0:
                        k.op("dve", lambda: DVE.tensor_tensor(
                            out=q16[:, :].rearrange("p (h d) -> p h d", h=4), in0=qf[:, :].rearrange("p (h d) -> p h d", h=4),
                            in1=rq.unsqueeze(2).to_broadcast([128, 4, 128]), op=ALU.mult), r=[qfB, rqB], w=[q16B])
                    else:
                        k.op("dve", lambda: DVE.tensor_copy(out=q16[:, :], in_=qf[:, :]), r=[qfB], w=[q16B])
                else:
                    if L == 0:
                        k.op("dve", lambda: DVE.tensor_tensor(
                            out=qf[:, :].rearrange("p (h d) -> p h d", h=4), in0=qr[:, :].rearrange("p (h d) -> p h d", h=4),
                            in1=gqk[:, 0, 0, :].unsqueeze(1).to_broadcast([128, 4, 128]), op=ALU.mult), r=[qrB, gqkB], w=[qfB])
                        k.op("dve", lambda: DVE.tensor_tensor(
                            out=q16[:, :].rearrange("p (h d) -> p h d", h=4), in0=qf[:, :].rearrange("p (h d) -> p h d", h=4),
                            in1=rq.unsqueeze(2).to_broadcast([128, 4, 128]), op=ALU.mult), r=[qfB, rqB], w=[q16B])
                    else:
                        k.op("dve", lambda: DVE.tensor_copy(out=q16[:, :], in_=qr[:, :]), r=[qrB], w=[q16B])

            def q_tail(grp, tb):
                q16, q16B = qn[tb % 2], qnB[tb % 2]
                pv = pbf(2, 8)
                k.group("pe", [(lambda hh=hh: TE.transpose(pv[:, hh, :], q16[:, hh * 128:(hh + 1) * 128], ident[:]))
                               for hh in range(4)], r=[q16B, constB], w=[pbB[2]])
                k.op("act", lambda: ACT.copy(out=QT[:, 4 * grp:4 * grp + 4, tb * 128:(tb + 1) * 128], in_=pv[:, 0:4, :]),
                     r=[pbB[2]], w=[QTB[grp]])

            def gb_piece(grp, hh):
                if hh == 0:
                    cur["gb"] = ws_get(L, nP + 1 + 2 * grp)
                slot, slotB = cur["gb"]
                pa = hh % 2
                h = 4 * grp + hh
                k.group("pe", [(lambda kk=kk: TE.matmul(
                    pb[pa][:, :], lhsT=slot[:, kk, hh * 128:(hh + 1) * 128], rhs=hTt[:, kk, :], start=(kk == 0), stop=(kk == 15)))
                    for kk in range(16)], r=[slotB] + hb, w=[pbB[pa]])
                k.op("act", lambda: ACT.activation(out=gbt[:, h, :], in_=pb[pa][:, :], func=AF.Silu), r=[pbB[pa]], w=[gbB[h]])

            def x_loads():
                for tb in range(4):
                    t = t0 + tb * 128
                    k.dma("sp", big[:, tb, :], xsrc(L, t), r=([x1B[t // 128]] if L == 1 else []), w=[bigB[tb]])

            if g == 0:
                load_rope(L, t0, 0)
            qgb = []
            for grp in range(2):
                for tb in range(4):
                    qgb.append(lambda grp=grp, tb=tb: q_piece(grp, tb))
                    if tb > 0:
                        qgb.append(lambda grp=grp, tb=tb: q_tail(grp, tb - 1))
                qgb.append(lambda grp=grp: gb_piece(grp, 0))
                qgb.append(lambda grp=grp: q_tail(grp, 3))
                qgb += [(lambda grp=grp, hh=hh: gb_piece(grp, hh)) for hh in range(1, 4)]
            att0 = 8 if L == 0 else 0
            if L == 0:
                for c in range(8):
                    conv_piece(c)
                    for _ in range(3):
                        qgb.pop(0)()
                ln_finalize()
                for h in range(8):
                    attention(L, j, g, h % 4, h, att0 + h)
                    if h < 4:
                        gate_ln_piece(2 * h)
                        gate_ln_piece(2 * h + 1)
                    if h == 3:
                        tile_loads(L, j + 1)
                        x_loads()
                    if 4 <= h <= 6 and len(MOD1_PARTS[j]) > h - 4:
                        ch_ = MOD1_PARTS[j][h - 4]
                        slot_, slotB_ = ws_get("m", ch_)
                        adaln_block(1, ch_, slot_, slotB_, bk=0)
                    if j < 2 or h % 2 == 1:
                        pc_pop(1)
            else:
                while qgb:
                    qgb.pop(0)()
                for h in range(8):
                    attention(L, j, g, h % 4, h, att0 + h)
                    if h < 4:
                        d_piece(2 * h)
                        d_piece(2 * h + 1)
                    if h == 3:
                        tile_loads(L, j + 1)
                        x_loads()
            if L == 0 and j == 0:
                adaln0_late()
            load_gt(L, g)
            wob = nP + (6 if L == 0 else 8)
            accs = [0, 1, 3, 4]
            for nb in range(4):
                slot, slotB = ws_get(L, wob + nb)
                for tb in range(4):
                    a = accs[(nb * 4 + tb) % 4]
                    k.group("pe", [(lambda kk=kk, tb=tb, a=a, slot=slot: TE.matmul(
                        pb[a][:, :], lhsT=abT[:, kk, tb * 128:(tb + 1) * 128], rhs=slot[:, kk, :], start=(kk == 0), stop=(kk == 15)))
                        for kk in range(16)], r=[slotB] + abB, w=[pbB[a]])
                    tm, tmB = f512[1 + tb % 2], f512B[1 + tb % 2]
                    k.op("dve", lambda a=a, nb=nb, tm=tm: DVE.tensor_tensor(out=tm[:, :], in0=pb[a][:, :], in1=gtb[:, nb * 512:(nb + 1) * 512],
                                                                           op=ALU.mult), r=[pbB[a], gtbB], w=[tmB])
                    k.op("dve", lambda tb=tb, nb=nb, tm=tm: DVE.tensor_tensor(out=big[:, tb, nb * 512:(nb + 1) * 512], in0=tm[:, :],
                                                                             in1=big[:, tb, nb * 512:(nb + 1) * 512], op=ALU.add),
                         r=[tmB, bigB[tb]], w=[bigB[tb]])
            for tb in range(4):
                t = t0 + tb * 128
                if L == 0:
                    k.dma("pst", x1s[t:t + 128, :], big[:, tb, :], r=[bigB[tb]], w=[x1B[t // 128]])
                else:
                    st, stB = st_next()
                    k.op("act", lambda tb=tb, st=st: ACT.activation(out=xn[0], in_=big[:, tb, :], func=AF.Square, accum_out=st[:, 0:1]),
                         r=[bigB[tb]], w=[xnB[0], stB])
                    k.op("act", lambda st=st: ACT.activation(out=st[:, 1:2], in_=st[:, 0:1], func=AF.Sqrt, bias=EPS, scale=1.0 / D), r=[stB], w=[stB])
                    k.op("dve", lambda st=st: DVE.reciprocal(out=st[:, 2:3], in_=st[:, 1:2]), r=[stB], w=[stB])
                    k.op("dve", lambda tb=tb, st=st: DVE.scalar_tensor_tensor(out=big[:, tb, :], in0=big[:, tb, :], scalar=st[:, 2:3], in1=fgb,
                                                                             op0=ALU.mult, op1=ALU.mult), r=[bigB[tb], stB, fgbB], w=[bigB[tb]])
                    dst = ys[t:t + 128, :] if t < 2048 else yp[t - 2048:t - 2048 + 128, :]
                    k.dma("pst", dst, big[:, tb, :], r=[bigB[tb]])

    def pc_setup():
        precast(0, 0, 5)
        precast(0, 5, 99, queue=True)
        mwv0 = mod_w[0].rearrange("(k p) n -> p k n", p=128)
        for ch in range(8, 12):
            pcq.append(lambda ch=ch: k.dma("pool", msc0[ch - 8].rearrange("p (k n) -> p k n", k=16),
                                           mwv0[:, :, ch * 512:(ch + 1) * 512], w=[msc0B[ch - 8]]))
        precast_mod1()
        precast(1, queue=True)

    steps = [lambda: adaln0(), lambda: pc_setup(), lambda: load_mod_cols(0), lambda: pre(0),
             lambda: main(0), lambda: (pc_pop(6), load_mod_cols(1)), lambda: pre(1), lambda: (pc_pop(99), main(1))]
    for si, stp in enumerate(steps):
        if si < stop:
            stp()
    k.finish()
    k.es.close()
    nc._kb = k
    return nc


def _rope_tables():
    rows = 2048 // 64
    r = np.repeat(np.arange(rows), 64).astype(np.float32)
    col = np.tile(np.arange(64), rows).astype(np.float32)
    half = 64
    inv = (np.float32(10000.0) ** (-np.arange(0, half, 2, dtype=np.float32) / np.float32(half))).astype(np.float32)
    ang_r = r[:, None] * inv
    ang_c = col[:, None] * inv
    ang = np.concatenate([ang_r, ang_r, ang_c, ang_c], axis=-1).astype(np.float32)
    cos = np.cos(ang).astype(np.float32)
    sin = np.sin(ang).astype(np.float32)
    sign = np.concatenate([-np.ones(32), np.ones(32), -np.ones(32), np.ones(32)]).astype(np.float32)
    return np.ascontiguousarray(cos), np.ascontiguousarray(sin * sign[None, :])


def make_in_maps(inp, cores):
    f = lambda a: np.ascontiguousarray(np.asarray(a, dtype=np.float32))
    cos, sin = _rope_tables()
    shared = {
        "mod_w0": f(inp["mod_w0"]), "mod_w1": f(inp["mod_w1"]),
        "mod_b0": f(inp["mod_b0"]).reshape(1, -1), "mod_b1": f(inp["mod_b1"]).reshape(1, -1),
        "norm_g0": f(inp["norm_g0"]).reshape(16, 128), "norm_g1": f(inp["norm_g1"]).reshape(16, 128),
        "w_in0": f(inp["w_in0"]), "w_in1": f(inp["w_in1"]), "w_out0": f(inp["w_out0"]), "w_out1": f(inp["w_out1"]),
        "conv_w0": f(inp["conv_w0"]),
        "cvec0": np.ascontiguousarray(np.concatenate([f(inp["conv_b0"]).reshape(8, 128), f(inp["ln_g0"]).reshape(8, 128),
                                                      f(inp["ln_b0"]).reshape(8, 128)], axis=0)),
        "qk_g": np.ascontiguousarray(np.stack([f(inp["q_norm_g0"]), f(inp["k_norm_g0"])], axis=0)),
        "sink1": f(inp["sink1"]).reshape(1, 8), "short_w1": f(inp["short_w1"]),
        "final_g": f(inp["final_norm_g"]).reshape(1, -1), "rope_c": cos, "rope_s": sin,
    }
    maps = []
    for b in cores:
        m = dict(shared)
        m["xs"] = f(inp["x_sample"][b])
        m["xp"] = f(inp["x_prompt"][2 * b:2 * b + 2]).reshape(512, D)
        m["ck0"] = f(inp["cache_k0"][b]).reshape(512, 256)
        m["cv0"] = f(inp["cache_v0"][b]).reshape(512, 256)
        m["ck1"] = f(inp["cache_k1"][b]).reshape(512, 256)
        m["cv1"] = f(inp["cache_v1"][b]).reshape(512, 256)
        m["cond"] = np.ascontiguousarray(np.stack([f(inp["c"][b]), f(inp["c_ctx"])], axis=0))
        maps.append(m)
    return maps


def kernel(**inputs):
    nc = build()
    maps = make_in_maps(inputs, list(range(8)))
    res = run_bass_kernel_spmd(nc, maps, core_ids=list(range(8)))
    rs = res.results
    y_sample = np.stack([rs[b]["ys"] for b in range(8)], axis=0).astype(np.float32)
    y_prompt = np.concatenate([rs[b]["yp"].reshape(2, 256, D) for b in range(8)], axis=0).astype(np.float32)
    outs = [y_prompt, y_sample]
    for nm in ("nk0", "nv0", "nk1", "nv1"):
        outs.append(np.concatenate([rs[b][nm].reshape(2, 256, 2, 128) for b in range(8)], axis=0).astype(np.float32))
    return tuple(outs)
```

```python
import numpy as np
from contextlib import ExitStack
import concourse.bass as bass
import concourse.mybir as mybir
from concourse.bass_utils import run_bass_kernel_spmd

F32 = mybir.dt.float32
BF16 = mybir.dt.bfloat16
AF = mybir.ActivationFunctionType
ALU = mybir.AluOpType

D = 2048
NTOK = 2560
EPS = 1e-6
SAME_ENGINE_SYNC = {'pe': False, 'dve': True, 'act': True, 'pool': True, 'sp': True}
NS = 8
NR = 2
TILES = [(0, 0), (0, 512), (0, 1024), (0, 1536), (1, 2048)]
IN_COLS = [5632, 6656]
SCALE = 128.0 ** -0.5


class Buf:
    __slots__ = ("w", "r", "n", "x")

    def __init__(self, n="", x=False):
        self.w = {}
        self.r = {}
        self.n = n
        self.x = x


class KB:
    def __init__(self):
        self.nc = bass.Bass("TRN2", target_bir_lowering=False)
        self.es = ExitStack()
        nc = self.nc
        self.eng = {"pe": nc.tensor, "act": nc.scalar, "dve": nc.vector, "pool": nc.gpsimd, "sp": nc.sync}
        self.sem = {n: self.es.enter_context(nc.semaphore("s_" + n)) for n in ("pe", "act", "dve", "pool")}
        self.cnt = {n: 0 for n in self.sem}
        self.seen = {n: {} for n in self.eng}
        self.dsem = {q: [self.es.enter_context(nc.semaphore("d_%s%d" % (q, i))) for i in range(NS)]
                     for q in ("sp", "pool", "pst")}
        self.dcnt = {q: [0] * NS for q in ("sp", "pool", "pst")}
        self.dnext = {q: 0 for q in ("sp", "pool", "pst")}
        self.qeng = {"sp": "sp", "pool": "pool", "pst": "pool"}
        self.nops = 0
        self.limit = 10 ** 9
        self.lines = []

    def _skip(self):
        import sys
        self.nops += 1
        if self.nops > self.limit:
            return True
        self.lines.append(sys._getframe(2).f_lineno)
        return False

    def sb(self, name, shape, dt):
        return self.es.enter_context(self.nc.sbuf_tensor(name, shape, dt))

    def ps(self, name, shape, dt):
        return self.es.enter_context(self.nc.psum_tensor(name, shape, dt))

    def _wait(self, e, key, val):
        if val <= 0 or self.seen[e].get(key, 0) >= val:
            return
        sem = self.sem[key] if isinstance(key, str) else self.dsem[key[1]][key[2]]
        self.eng[e].wait_ge(sem, val)
        self.seen[e][key] = val

    def _deps(self, e, r, w):
        need = {}
        for b in r:
            for k2, v in b.w.items():
                if need.get(k2, 0) < v:
                    need[k2] = v
            if b.x:
                for k2, v in b.r.items():
                    if k2 != e and need.get(k2, 0) < v:
                        need[k2] = v
        for b in w:
            for k2, v in b.w.items():
                if need.get(k2, 0) < v:
                    need[k2] = v
            for k2, v in b.r.items():
                if need.get(k2, 0) < v:
                    need[k2] = v
        for k2, v in need.items():
            if k2 == e and not SAME_ENGINE_SYNC[e]:
                continue
            self._wait(e, k2, v)

    def _mark(self, key, val, r, w):
        for b in w:
            b.w = {key: val}
            b.r = {}
        for b in r:
            if b.r.get(key, 0) < val:
                b.r[key] = val

    def op(self, e, fn, r=(), w=()):
        if self._skip():
            return
        self._deps(e, r, w)
        ins = fn()
        self.cnt[e] += 1
        ins.then_inc(self.sem[e], 1)
        self._mark(e, self.cnt[e], r, w)

    def group(self, e, fns, r=(), w=()):
        if self._skip():
            return
        self._deps(e, r, w)
        ins = None
        for f in fns:
            ins = f()
        self.cnt[e] += 1
        ins.then_inc(self.sem[e], 1)
        self._mark(e, self.cnt[e], r, w)

    def dma(self, q, out, in_, r=(), w=()):
        if self._skip():
            return
        i = self.dnext[q]
        self.dnext[q] = (i + 1) % NS
        key = ("d", q, i)
        e = self.qeng[q]
        self._wait(e, key, self.dcnt[q][i])
        self._deps(e, r, w)
        ins = self.eng[e].dma_start(out=out, in_=in_)
        self.dcnt[q][i] += 16
        ins.then_inc(self.dsem[q][i], 16)
        self._mark(key, self.dcnt[q][i], r, w)

    def finish(self):
        for q in ("sp", "pool", "pst"):
            for i in range(NS):
                self._wait("sp", ("d", q, i), self.dcnt[q][i])
        for e in ("pe", "act", "dve", "pool"):
            self._wait("sp", e, self.cnt[e])


def build(debug=False, stop=99, limit=10 ** 9):
    k = KB()
    k.limit = limit
    nc = k.nc
    TE, ACT, DVE, POOL = nc.tensor, nc.scalar, nc.vector, nc.gpsimd

    def din(name, shape):
        return nc.dram_tensor(name, shape, F32, kind="ExternalInput").ap()

    def dout(name, shape):
        return nc.dram_tensor(name, shape, F32, kind="ExternalOutput").ap()

    skind = "ExternalOutput" if debug else "Internal"

    def dscr(name, shape, dt):
        return nc.dram_tensor(name, shape, dt, kind=skind).ap()

    xs = din("xs", [2048, D])
    xp = din("xp", [512, D])
    cache = [[din("ck0", [512, 256]), din("cv0", [512, 256])], [din("ck1", [512, 256]), din("cv1", [512, 256])]]
    cond = din("cond", [2, D])
    mod_w = [din("mod_w0", [D, 3 * D]), din("mod_w1", [D, 3 * D])]
    mod_b = [din("mod_b0", [1, 3 * D]), din("mod_b1", [1, 3 * D])]
    norm_g = [din("norm_g0", [16, 128]), din("norm_g1", [16, 128])]
    w_in = [din("w_in0", [D, IN_COLS[0]]), din("w_in1", [D, IN_COLS[1]])]
    w_out = [din("w_out0", [D, D]), din("w_out1", [D, D])]
    conv_w0 = din("conv_w0", [31, 1024])
    cvec0 = din("cvec0", [24, 128])
    qk_g = din("qk_g", [2, 128])
    sink1 = din("sink1", [1, 8])
    short_w1 = din("short_w1", [3, 1024])
    final_g = din("final_g", [1, D])
    rope_c = din("rope_c", [2048, 128])
    rope_s = din("rope_s", [2048, 128])

    ys = dout("ys", [2048, D])
    yp = dout("yp", [512, D])
    nkv = [[dout("nk0", [512, 256]), dout("nv0", [512, 256])], [dout("nk1", [512, 256]), dout("nv1", [512, 256])]]

    def contig(b):
        return [(b * 512, 512)]

    PRE_BLK = [
        [("in", [(4096, 512)])] + [("in", [(2 * i * 128, 256), (1024 + 2 * i * 128, 256)]) for i in range(4)],
        [("in", [(1024, 512)])] + [("in", [(3584 + 2 * i * 128, 256), (4608 + 2 * i * 128, 256)]) for i in range(4)],
    ]
    MAIN_BLK = [
        [("in", contig(6)), ("in", contig(9)), ("in", contig(7)), ("in", contig(10)), ("in", contig(4)),
         ("in", contig(5))] + [("out", contig(i)) for i in range(4)],
        [("in", contig(0)), ("in", contig(3)), ("in", contig(1)), ("in", contig(4))]
        + [("in", [(2560 + 2 * i * 128, 256), (5632 + 2 * i * 128, 256)]) for i in range(4)]
        + [("out", contig(i)) for i in range(4)],
    ]
    nblk = [len(PRE_BLK[L]) + len(MAIN_BLK[L]) for L in range(2)]
    wsc = [dscr("wsc%d" % L, [nblk[L], 128, 16 * 512], BF16) for L in range(2)]
    wscB = [[Buf() for _ in range(nblk[L])] for L in range(2)]
    hs = [dscr("hs%d" % L, [5, 128, 16 * 512], BF16) for L in range(2)]
    hsB = [[Buf() for _ in range(5)] for L in range(2)]
    us = [[dscr("us%d_%d" % (L, g), [128, 8, 2048 if g == 0 else 512], BF16) for g in range(2)] for L in range(2)]
    usB = [[Buf() for _ in range(5)] for L in range(2)]
    x1s = dscr("x1s", [NTOK, D], F32)
    x1B = [Buf() for _ in range(20)]
    mvec = [dscr("mvec%d" % L, [2, 3 * D], F32) for L in range(2)]
    mvecB = [[Buf() for _ in range(12)] for L in range(2)]

    ring = [k.sb("ring%d" % i, [128, 16, 512], BF16) for i in range(NR)]
    ringB = [Buf() for _ in range(NR)]
    hTt = k.sb("hTt", [128, 16, 512], BF16)
    hb = [Buf() for _ in range(16)]
    KT = k.sb("KT", [128, 2, 2560], BF16)
    KTB = Buf()
    V = k.sb("V", [128, 20, 256], BF16)
    VB = Buf()
    KTp = k.sb("KTp", [128, 2, 2, 256], BF16)
    KTpB = Buf()
    Vp = k.sb("Vp", [128, 2, 2, 256], BF16)
    VpB = Buf()
    big = k.sb("big", [128, 4, 2048], F32)
    bigB = [Buf() for _ in range(4)]
    gtb = k.sb("gtb", [128, 2048], F32)
    gtbB = Buf()
    gt_cur = [None]
    ropet = k.sb("ropet", [128, 2, 4, 128], F32)
    ropeB = Buf()
    uxb = [k.sb("uxb%d" % i, [128, 4608], BF16) for i in range(2)]
    uxB = [Buf() for _ in range(2)]
    gts = k.sb("gts", [128, 2, 512], BF16)
    gtsB = [Buf() for _ in range(2)]
    abT = k.sb("abT", [128, 16, 512], BF16)
    abB = [Buf() for _ in range(16)]
    QT = k.sb("QT", [128, 8, 512], BF16)
    QTB = [Buf() for _ in range(2)]
    gbt = k.sb("gbt", [128, 8, 512], BF16)
    gbB = [Buf() for _ in range(8)]
    ptt = [k.sb("ptt%d" % i, [128, 512], BF16) for i in range(2)]
    ptB = [Buf() for _ in range(2)]
    f512 = [k.sb("f512_%d" % i, [128, 512], F32) for i in range(5)]
    f512B = [Buf() for _ in range(5)]
    qn = [k.sb("qn%d" % i, [128, 512], BF16) for i in range(2)]
    qnB = [Buf() for _ in range(2)]
    xn_all = k.sb("xn_all", [128, 2, 2048], BF16)
    xn = [xn_all[:, i, :] for i in range(2)]
    xnB = [Buf() for _ in range(2)]
    dgs = dscr("dgs", [8, 128, 31 * 128], BF16)
    dgsB = [Buf() for _ in range(8)]
    junk = k.sb("junk", [128, 512], BF16)
    junkB = Buf()
    ut = k.sb("ut", [128, 8, 512], BF16)
    utB = Buf()
    fgb = ut[:].rearrange("p c n -> p (c n)").bitcast(F32)
    fgbB = utB
    stt = k.sb("stt", [128, 16, 8], F32)
    sttB = [Buf() for _ in range(16)]
    st_i = [0]
    identf = k.sb("identf", [128, 128], F32)
    ident = k.sb("ident", [128, 128], BF16)
    onesf = k.sb("onesf", [128, 128], F32)
    onesb = k.sb("onesb", [128, 128], BF16)
    mkge = k.sb("mkge", [128, 128], BF16)
    mkle = k.sb("mkle", [128, 128], BF16)
    mtmp = k.sb("mtmp", [128, 128], F32)
    constB = Buf()
    cw = k.sb("cw", [128, 8, 32], F32)
    sw = k.sb("sw", [128, 8, 4], F32)
    cvc = k.sb("cvc", [128, 24], F32)
    ngc = k.sb("ngc", [128, 2, 16], F32)
    parB = Buf()
    mc = k.sb("mc", [128, 64], F32)
    gsc = k.sb("gsc", [128, 2, 16], F32)
    mcB = Buf()
    stage = k.sb("stage", [128, 128], F32)
    stageB = Buf()
    gqk = k.sb("gqk", [128, 2, 2, 128], F32)
    gqkB = Buf()
    es = k.sb("es", [128, 8], F32)
    esB = Buf()
    sT = k.sb("sT", [128, 16, 2], BF16)
    sTB = Buf()
    pb = [k.ps("pb%d" % i, [128, 512], F32) for i in range(8)]
    pbB = [Buf(x=True) for _ in range(8)]

    def pbf(i, kk):
        return pb[i][:].bitcast(BF16).rearrange("p (k t) -> p k t", k=kk)

    def st_next():
        i = st_i[0]
        st_i[0] = (i + 1) % 16
        return stt[:, i, :], sttB[i]

    msc = dscr("msc", [12, 128, 16 * 512], BF16)
    mscB = [Buf() for _ in range(12)]
    MOD1_PARTS = [[], [0, 1, 2], [3, 4, 5], [6, 7, 8], [9, 10, 11]]
    wseq = []
    for L in range(2):
        for j in range(5):
            wseq += [(L, i) for i in range(len(PRE_BLK[L]))]
        for j in range(5):
            nmb = len(MAIN_BLK[L])
            wseq += [(L, len(PRE_BLK[L]) + i) for i in range(nmb - 4)]
            if L == 0 and j == 0:
                wseq += [("m0", ch) for ch in range(8, 12)]
            if L == 0:
                wseq += [("m", ch) for ch in MOD1_PARTS[j]]
            wseq += [(L, len(PRE_BLK[L]) + i) for i in range(nmb - 4, nmb)]
    ws = {"i": 0, "issued": 0}

    pcq = []

    def pc_pop(n):
        for _ in range(n):
            if pcq:
                pcq.pop(0)()

    def precast(L, lo=0, hi=99, queue=False):
        blks = PRE_BLK[L] + MAIN_BLK[L]
        for bi, (which, segs) in enumerate(blks):
            if not (lo <= bi < hi):
                continue
            if queue:
                pcq.append(lambda bi=bi: precast(L, bi, bi + 1))
                continue
            src = (w_in[L] if which == "in" else w_out[L]).rearrange("(k p) n -> p k n", p=128)
            dst = wsc[L][bi].rearrange("p (k n) -> p k n", k=16)
            off = 0
            for (c0, n) in segs:
                k.dma("pool", dst[:, :, off:off + n], src[:, :, c0:c0 + n], w=[])
                key = ("d", "pool", (k.dnext["pool"] - 1) % NS)
                wscB[L][bi].w[key] = k.dcnt["pool"][key[2]]
                off += n

    def ws_issue():
        i = ws["issued"]
        L, bi = wseq[i]
        s = i % NR
        srcB = mscB[bi] if L == "m" else (msc0B[bi - 8] if L == "m0" else wscB[L][bi])
        assert srcB.w, ("precast not issued before ring load", L, bi)
        if L == "m":
            k.dma("sp", ring[s][:].rearrange("p k n -> p (k n)"), msc[bi], r=[mscB[bi]], w=[ringB[s]])
        elif L == "m0":
            k.dma("sp", ring[s][:].rearrange("p k n -> p (k n)"), msc0[bi - 8], r=[msc0B[bi - 8]], w=[ringB[s]])
        else:
            k.dma("sp", ring[s][:].rearrange("p k n -> p (k n)"), wsc[L][bi], r=[wscB[L][bi]], w=[ringB[s]])
        ws["issued"] = i + 1

    def ws_get(L, bi):
        i = ws["i"]
        assert wseq[i] == (L, bi), (wseq[i], L, bi)
        while ws["issued"] < min(len(wseq), i + NR):
            ws_issue()
        ws["i"] = i + 1
        s = i % NR
        return ring[s], ringB[s]

    k.op("pool", lambda: POOL.memset(identf[:], 1.0), w=[constB])
    k.op("pool", lambda: POOL.affine_select(out=identf[:], in_=identf[:], pattern=[[-1, 128]], compare_op=ALU.is_equal,
                                            fill=0.0, base=0, channel_multiplier=1), r=[constB], w=[constB])
    k.op("pool", lambda: POOL.memset(onesf[:], 1.0), w=[constB])
    k.op("pool", lambda: POOL.memset(onesb[:], 1.0), w=[constB])
    k.op("dve", lambda: DVE.tensor_copy(out=ident[:], in_=identf[:]), r=[constB], w=[constB])
    k.op("pool", lambda: POOL.affine_select(out=mtmp[:], in_=onesf[:], pattern=[[-1, 128]], compare_op=ALU.is_ge,
                                            fill=0.0, base=0, channel_multiplier=1), r=[constB], w=[constB])
    k.op("dve", lambda: DVE.tensor_scalar(out=mkge[:], in0=mtmp[:], scalar1=-1.0, scalar2=30000.0, op0=ALU.add, op1=ALU.mult),
         r=[constB], w=[constB])
    k.op("pool", lambda: POOL.affine_select(out=mtmp[:], in_=onesf[:], pattern=[[1, 128]], compare_op=ALU.is_ge,
                                            fill=0.0, base=0, channel_multiplier=-1), r=[constB], w=[constB])
    k.op("dve", lambda: DVE.tensor_scalar(out=mkle[:], in0=mtmp[:], scalar1=-1.0, scalar2=30000.0, op0=ALU.add, op1=ALU.mult),
         r=[constB], w=[constB])

    k.dma("sp", big[0:2, 0, :], cond[:, :], w=[bigB[0]])
    k.op("act", lambda: ACT.activation(out=xn_all[0:2, 0, :], in_=big[0:2, 0, :], func=AF.Silu), r=[bigB[0]], w=[xnB[0]])
    pv7 = pbf(7, 16)
    k.group("pe", [(lambda kk=kk: TE.transpose(pv7[:, kk, 0:2], xn_all[0:2, 0, kk * 128:(kk + 1) * 128], ident[0:2, 0:2]))
                   for kk in range(16)], r=[xnB[0], constB], w=[pbB[7]])
    k.op("dve", lambda: DVE.tensor_copy(out=sT[:], in_=pv7[:, :, 0:2]), r=[pbB[7]], w=[sTB])

    def colload(dst, rows, n, rB=(), wB=()):
        for (ap, r0, m) in rows:
            k.dma("sp", stage[r0:r0 + m, :], ap, r=list(rB), w=[stageB])
        k.group("pe", [lambda: TE.transpose(pb[7][:, 0:n], stage[0:n, :], identf[0:n, 0:n])], r=[stageB, constB], w=[pbB[7]])
        k.op("dve", lambda: DVE.tensor_copy(out=dst, in_=pb[7][:, 0:n]), r=[pbB[7]], w=list(wB))

    colload(cvc[:, :], [(cvec0[:, :], 0, 24)], 24, wB=[parB])
    colload(ngc[:].rearrange("p l k -> p (l k)"), [(norm_g[0][:, :], 0, 16), (norm_g[1][:, :], 16, 16)], 32, wB=[parB])
    k.dma("sp", big[0:31, 1, 0:1024], conv_w0[:, :], w=[bigB[1]])
    pv7c = pb[7][:, 0:256].rearrange("p (c j) -> p c j", c=8)
    k.group("pe", [(lambda c=c: TE.transpose(pv7c[:, c, 0:31], big[0:31, 1, c * 128:(c + 1) * 128], identf[0:31, 0:31]))
                   for c in range(8)], r=[bigB[1], constB], w=[pbB[7]])
    k.op("dve", lambda: DVE.tensor_copy(out=cw[:, :, 0:31], in_=pv7c[:, :, 0:31]), r=[pbB[7]], w=[parB])
    k.dma("sp", big[0:3, 1, 0:1024], short_w1[:, :], w=[bigB[1]])
    pv7s = pb[7][:, 0:32].rearrange("p (c j) -> p c j", c=8)
    k.group("pe", [(lambda c=c: TE.transpose(pv7s[:, c, 0:3], big[0:3, 1, c * 128:(c + 1) * 128], identf[0:3, 0:3]))
                   for c in range(8)], r=[bigB[1], constB], w=[pbB[7]])
    k.op("dve", lambda: DVE.tensor_copy(out=sw[:, :, 0:3], in_=pv7s[:, :, 0:3]), r=[pbB[7]], w=[parB])
    for i in range(2):
        k.dma("sp", gqk[:, i, 0, :], qk_g[i:i + 1, :].partition_broadcast(128), w=[gqkB])
    for i in range(2):
        g5 = gqk[:, i, 0, :].rearrange("p (a w e) -> p a w e", a=2, w=2)
        s5 = gqk[:, i, 1, :].rearrange("p (a w e) -> p a w e", a=2, w=2)
        k.op("dve", lambda g5=g5, s5=s5: DVE.tensor_copy(out=s5[:, :, 0, :], in_=g5[:, :, 1, :]), r=[gqkB], w=[gqkB])
        k.op("dve", lambda g5=g5, s5=s5: DVE.tensor_copy(out=s5[:, :, 1, :], in_=g5[:, :, 0, :]), r=[gqkB], w=[gqkB])
    k.dma("sp", es[:, :], sink1[0:1, :].partition_broadcast(128), w=[esB])
    k.op("act", lambda: ACT.activation(out=es[:, :], in_=es[:, :], func=AF.Exp), r=[esB], w=[esB])

    def build_diag(c):
        dgt = uxb[0][:, 0:4096]
        dg3 = dgt[:, 0:31 * 128].rearrange("p (j m) -> p j m", j=31)
        for jj in range(31):
            if jj % 2 == 0:
                k.op("act", lambda jj=jj: ACT.activation(out=dg3[:, jj, :], in_=identf[:, :], func=AF.Identity, scale=cw[:, c, jj:jj + 1]),
                     r=[constB, parB], w=[uxB[0]])
            else:
                k.op("dve", lambda jj=jj: DVE.tensor_scalar(out=dg3[:, jj, :], in0=identf[:, :], scalar1=cw[:, c, jj:jj + 1],
                                                           scalar2=None, op0=ALU.mult), r=[constB, parB], w=[uxB[0]])
        k.dma("pst", dgs[c], dgt[:, 0:31 * 128], r=[uxB[0]], w=[dgsB[c]])

    mtile = f512[4][0:2, :]
    mtileB = f512B[4]

    def adaln_block(L, ch, slot, slotB, bk=7):
        k.dma("sp", f512[0][0:2, :], mod_b[L][0:1, ch * 512:(ch + 1) * 512].partition_broadcast(2), w=[f512B[0]])
        k.group("pe", [(lambda kk=kk: TE.matmul(pb[bk][0:2, :], lhsT=sT[:, kk, :], rhs=slot[:, kk, :],
                                                start=(kk == 0), stop=(kk == 15))) for kk in range(16)],
                r=[slotB, sTB], w=[pbB[bk]])
        k.op("dve", lambda: DVE.tensor_tensor(out=mtile[:, 0:512], in0=pb[bk][0:2, :], in1=f512[0][0:2, :], op=ALU.add),
             r=[pbB[bk], f512B[0]], w=[mtileB])
        k.dma("sp", mvec[L][:, ch * 512:(ch + 1) * 512], mtile[:, 0:512], r=[mtileB], w=[mvecB[L][ch]])

    msc0 = dscr("msc0", [4, 128, 16 * 512], BF16)
    msc0B = [Buf() for _ in range(4)]

    def adaln0_late():
        for ch in range(8, 12):
            slot, slotB = ws_get("m0", ch)
            adaln_block(0, ch, slot, slotB)

    def adaln0():
        mwv = mod_w[0].rearrange("(k p) n -> p k n", p=128)
        for ch in range(8):
            s_ = ch % NR
            k.dma("pool", ring[s_][:, :, :], mwv[:, :, ch * 512:(ch + 1) * 512], w=[ringB[s_]])
            adaln_block(0, ch, ring[s_], ringB[s_])

    def precast_mod1():
        mwv = mod_w[1].rearrange("(k p) n -> p k n", p=128)
        for ch in range(12):
            pcq.append(lambda ch=ch: k.dma("pool", msc[ch].rearrange("p (k n) -> p k n", k=16), mwv[:, :, ch * 512:(ch + 1) * 512],
                                           w=[mscB[ch]]))

    def adaln1_part(j):
        for ch in MOD1_PARTS[j]:
            slot, slotB = ws_get("m", ch)
            adaln_block(1, ch, slot, slotB)

    def load_mod_cols(L):
        rows = []
        for g in range(2):
            rows.append((mvec[L][g:g + 1, 0:2048].rearrange("o (k p) -> (o k) p", p=128), 32 * g, 16))
            rows.append((mvec[L][g:g + 1, 2048:4096].rearrange("o (k p) -> (o k) p", p=128), 32 * g + 16, 16))
        colload(mc[:, :], rows, 64, rB=mvecB[L][0:8], wB=[mcB])
        for g in range(2):
            k.op("dve", lambda g=g: DVE.tensor_scalar(out=gsc[:, g, :], in0=mc[:, 32 * g + 16:32 * g + 32], scalar1=1.0,
                                                      scalar2=None, op0=ALU.add), r=[mcB], w=[mcB])
            k.op("dve", lambda g=g: DVE.tensor_tensor(out=gsc[:, g, :], in0=gsc[:, g, :], in1=ngc[:, L, :], op=ALU.mult),
                 r=[mcB, parB], w=[mcB])

    def load_gt(L, g):
        if gt_cur[0] == (L, g):
            return
        gt_cur[0] = (L, g)
        k.dma("sp", gtb[:, :], mvec[L][g:g + 1, 4096:6144].partition_broadcast(128), r=mvecB[L], w=[gtbB])

    def xsrc(L, t):
        if L == 1:
            return x1s[t:t + 128, :]
        return xs[t:t + 128, :] if t < 2048 else xp[t - 2048:t - 2048 + 128, :]

    def load_rope(L, t0, which):
        k.dma("sp", ropet[:, 0, :, :], rope_c[t0:t0 + 512, :].rearrange("(tb p) d -> p tb d", p=128), w=[ropeB])
        k.dma("sp", ropet[:, 1, :, :], rope_s[t0:t0 + 512, :].rearrange("(tb p) d -> p tb d", p=128), w=[ropeB])
        if L == 0:
            for cs_ in range(2):
                k.op("dve", lambda cs_=cs_: DVE.tensor_tensor(
                    out=ropet[:, cs_, :, :], in0=ropet[:, cs_, :, :],
                    in1=gqk[:, which, cs_, :].unsqueeze(1).to_broadcast([128, 4, 128]), op=ALU.mult),
                    r=[ropeB, gqkB], w=[ropeB])

    def rope_apply(src_ap, H, tb, dst_f32, dstB, srcB):
        t1, t1B = f512[0], f512B[0]
        x3 = src_ap.rearrange("p (h d) -> p h d", h=H)
        x5 = src_ap.rearrange("p (h a w e) -> p h a w e", h=H, a=2, w=2)
        c3 = ropet[:, 0, tb, :].unsqueeze(1).to_broadcast([128, H, 128])
        s5 = ropet[:, 1, tb, :].rearrange("p (a w e) -> p a w e", a=2, w=2)
        d3 = dst_f32.rearrange("p (h d) -> p h d", h=H)
        t5 = t1[:, 0:H * 128].rearrange("p (h a w e) -> p h a w e", h=H, a=2, w=2)
        k.op("dve", lambda: DVE.tensor_tensor(out=d3, in0=x3, in1=c3, op=ALU.mult), r=[srcB, ropeB], w=[dstB])
        k.op("dve", lambda: DVE.tensor_tensor(out=t5[:, :, :, 0, :], in0=x5[:, :, :, 1, :],
                                              in1=s5[:, :, 0, :].unsqueeze(1).to_broadcast([128, H, 2, 32]), op=ALU.mult),
             r=[srcB, ropeB], w=[t1B])
        k.op("dve", lambda: DVE.tensor_tensor(out=t5[:, :, :, 1, :], in0=x5[:, :, :, 0, :],
                                              in1=s5[:, :, 1, :].unsqueeze(1).to_broadcast([128, H, 2, 32]), op=ALU.mult),
             r=[srcB, ropeB, t1B], w=[t1B])
        k.op("dve", lambda: DVE.tensor_tensor(out=dst_f32, in0=dst_f32, in1=t1[:, 0:H * 128], op=ALU.add),
             r=[t1B, dstB], w=[dstB])

    def head_rstd(src_ap, H, srcB):
        st, stB = st_next()
        for hh in range(H):
            k.op("act", lambda hh=hh: ACT.activation(out=junk[:, hh * 128:(hh + 1) * 128], in_=src_ap[:, hh * 128:(hh + 1) * 128],
                                                     func=AF.Square, accum_out=st[:, hh:hh + 1]), r=[srcB], w=[junkB, stB])
        k.op("act", lambda: ACT.activation(out=st[:, 0:H], in_=st[:, 0:H], func=AF.Sqrt, bias=EPS, scale=1.0 / 128),
             r=[stB], w=[stB])
        k.op("dve", lambda: DVE.reciprocal(out=st[:, 0:H], in_=st[:, 0:H]), r=[stB], w=[stB])
        return st[:, 0:H], stB

    hTbuf = [hTt, abT]
    hTB = [hb, abB]

    def build_tb(L, j, tb):
        build_a(L, j, tb)
        build_b(L, j, tb)

    def build_a(L, j, tb):
        if j >= 5:
            return
        g, t0 = TILES[j]
        t = t0 + tb * 128
        s = tb % 2
        k.dma("sp", big[:, s, :], xsrc(L, t), r=([x1B[t // 128]] if L == 1 else []), w=[bigB[s]])
        st, stB = st_next()
        k.op("act", lambda: ACT.activation(out=xn[s], in_=big[:, s, :], func=AF.Square, accum_out=st[:, 0:1]),
             r=[bigB[s]], w=[xnB[s], stB])
        k.op("act", lambda: ACT.activation(out=st[:, 1:2], in_=st[:, 0:1], func=AF.Sqrt, bias=EPS, scale=1.0 / D), r=[stB], w=[stB])
        k.op("dve", lambda: DVE.reciprocal(out=st[:, 2:3], in_=st[:, 1:2]), r=[stB], w=[stB])
        k.op("dve", lambda: DVE.tensor_single_scalar(out=xn[s], in_=big[:, s, :], scalar=st[:, 2:3], op=ALU.mult),
             r=[bigB[s], stB], w=[xnB[s]])

    def build_b(L, j, tb):
        if j >= 5:
            return
        g, t0 = TILES[j]
        hT_, hb_ = hTbuf[j % 2], hTB[j % 2]
        s = tb % 2
        pa, pbk = 2 * s, 2 * s + 1
        va, vb = pbf(pa, 8), pbf(pbk, 8)
        k.group("pe", [(lambda kk=kk: TE.transpose((va if kk < 8 else vb)[:, kk % 8, :], xn_all[:, s, kk * 128:(kk + 1) * 128], ident[:]))
                       for kk in range(16)], r=[xnB[s], constB], w=[pbB[pa], pbB[pbk]])
        for kk in range(16):
            src = (va if kk < 8 else vb)[:, kk % 8, :]
            dst = hT_[:, kk, tb * 128:(tb + 1) * 128]
            if kk < 8:
                k.op("act", lambda src=src, dst=dst, kk=kk: ACT.activation(
                    out=dst, in_=src, func=AF.Identity, bias=mc[:, 32 * g + kk:32 * g + kk + 1],
                    scale=gsc[:, g, kk:kk + 1]), r=[pbB[pa], mcB], w=[hb_[kk]])
            else:
                k.op("dve", lambda src=src, dst=dst, kk=kk: DVE.tensor_scalar(
                    out=dst, in0=src, scalar1=gsc[:, g, kk:kk + 1], scalar2=mc[:, 32 * g + kk:32 * g + kk + 1],
                    op0=ALU.mult, op1=ALU.add), r=[pbB[pbk], mcB], w=[hb_[kk]])
        if tb == 3:
            k.dma("pst", hs[L][j], hT_[:].rearrange("p k n -> p (k n)"), r=hb_, w=[hsB[L][j]])

    def pre(L):
        for tb in range(4):
            build_tb(L, 0, tb)
        for j, (g, t0) in enumerate(TILES):
            hTt_, hb_ = hTbuf[j % 2], hTB[j % 2]
            if g == 0:
                load_rope(L, t0, 1)
            slot, slotB = ws_get(L, 0)
            kv_tail = [None]
            for tb in range(4):
                t = t0 + tb * 128
                a = 4 + tb % 2
                k.group("pe", [(lambda kk=kk, tb=tb, a=a, slot=slot: TE.matmul(
                    pb[a][:, :], lhsT=hTt_[:, kk, tb * 128:(tb + 1) * 128], rhs=slot[:, kk, :], start=(kk == 0), stop=(kk == 15)))
                    for kk in range(16)], r=[slotB] + hb_, w=[pbB[a]])
                kf, kfB = f512[1 + tb % 2], f512B[1 + tb % 2]
                raw, rawB = f512[3 + tb % 2], f512B[3 + tb % 2]
                k.op("act", lambda a=a, raw=raw: ACT.copy(out=raw[:, :], in_=pb[a][:, :]), r=[pbB[a]], w=[rawB])
                kraw = raw[:, 0:256]
                if kv_tail[0] is not None:
                    kv_tail[0]()
                    kv_tail[0] = None
                if g == 1:
                    tl = t - 2048
                    seq, blk = tl // 256, (tl % 256) // 128
                    if L == 0:
                        rk, rkB = head_rstd(kraw, 2, rawB)
                        k.op("dve", lambda kf=kf, kraw=kraw, rk=rk: DVE.tensor_tensor(
                            out=kf[:, 0:256].rearrange("p (h d) -> p h d", h=2), in0=kraw.rearrange("p (h d) -> p h d", h=2),
                            in1=rk.unsqueeze(2).to_broadcast([128, 2, 128]), op=ALU.mult), r=[rawB, rkB], w=[kfB])
                        k.op("dve", lambda kf=kf: DVE.tensor_tensor(
                            out=kf[:, 0:256].rearrange("p (h d) -> p h d", h=2), in0=kf[:, 0:256].rearrange("p (h d) -> p h d", h=2),
                            in1=gqk[:, 1, 0, :].unsqueeze(1).to_broadcast([128, 2, 128]), op=ALU.mult), r=[kfB, gqkB], w=[kfB])
                    else:
                        kf, kfB = raw, rawB
                    k.dma("pst", nkv[L][0][tl:tl + 128, :], kf[:, 0:256], r=[kfB])
                    k.dma("pst", nkv[L][1][tl:tl + 128, :], raw[:, 256:512], r=[rawB])
                    kb16, kb16B = qn[tb % 2], qnB[tb % 2]
                    k.op("dve", lambda kf=kf, kb16=kb16: DVE.tensor_copy(out=kb16[:, 0:256], in_=kf[:, 0:256]), r=[kfB], w=[kb16B])
                    k.op("pool", lambda raw=raw, seq=seq, blk=blk: POOL.tensor_copy(out=Vp[:, seq, blk, :], in_=raw[:, 256:512]),
                         r=[rawB], w=[VpB])
                    def tail(tb=tb, kb16=kb16, kb16B=kb16B, seq=seq, blk=blk):
                        pv = pbf(6 + tb % 2, 8)
                        k.group("pe", [(lambda hh=hh: TE.transpose(pv[:, hh, :], kb16[:, hh * 128:(hh + 1) * 128], ident[:]))
                                       for hh in range(2)], r=[kb16B, constB], w=[pbB[6 + tb % 2]])
                        k.op("act", lambda: ACT.copy(out=KTp[:, seq, :, blk * 128:(blk + 1) * 128], in_=pv[:, 0:2, :]),
                             r=[pbB[6 + tb % 2]], w=[KTpB])
                    kv_tail[0] = tail
                else:
                    rope_apply(kraw, 2, tb, kf[:, 0:256], kfB, rawB)
                    kb16, kb16B = qn[tb % 2], qnB[tb % 2]
                    if L == 0:
                        rk, rkB = head_rstd(kraw, 2, rawB)
                        k.op("dve", lambda kf=kf, rk=rk, kb16=kb16: DVE.tensor_tensor(
                            out=kb16[:, 0:256].rearrange("p (h d) -> p h d", h=2), in0=kf[:, 0:256].rearrange("p (h d) -> p h d", h=2),
                            in1=rk.unsqueeze(2).to_broadcast([128, 2, 128]), op=ALU.mult), r=[kfB, rkB], w=[kb16B])
                    else:
                        k.op("dve", lambda kf=kf, kb16=kb16: DVE.tensor_copy(out=kb16[:, 0:256], in_=kf[:, 0:256]), r=[kfB], w=[kb16B])
                    blkg = 4 + t // 128
                    k.op("pool", lambda raw=raw, blkg=blkg: POOL.tensor_copy(out=V[:, blkg, :], in_=raw[:, 256:512]), r=[rawB], w=[VB])
                    def tail(tb=tb, kb16=kb16, kb16B=kb16B, t=t):
                        pv = pbf(6 + tb % 2, 8)
                        k.group("pe", [(lambda hh=hh: TE.transpose(pv[:, hh, :], kb16[:, hh * 128:(hh + 1) * 128], ident[:]))
                                       for hh in range(2)], r=[kb16B, constB], w=[pbB[6 + tb % 2]])
                        k.op("act", lambda: ACT.copy(out=KT[:, :, 512 + t:512 + t + 128], in_=pv[:, 0:2, :]),
                             r=[pbB[6 + tb % 2]], w=[KTB])
                    kv_tail[0] = tail
            for i in range(4):
                if i > 0:
                    build_b(L, j + 1, i - 1)
                if i == 1 and kv_tail[0] is not None:
                    kv_tail[0]()
                    kv_tail[0] = None
                build_a(L, j + 1, i)
                if L == 0 and j < 4 and i in (1, 3):
                    build_diag(2 * j + (i - 1) // 2)
                slot, slotB = ws_get(L, 1 + i)
                for cc in range(2):
                    c = 2 * i + cc
                    pa, pbk = 4 + 2 * cc, 5 + 2 * cc
                    k.group("pe", [(lambda kk=kk, cc=cc, pa=pa, slot=slot: TE.matmul(
                        pb[pa][:, :], lhsT=slot[:, kk, cc * 128:(cc + 1) * 128], rhs=hTt_[:, kk, :], start=(kk == 0), stop=(kk == 15)))
                        for kk in range(16)], r=[slotB] + hb_, w=[pbB[pa]])
                    k.group("pe", [(lambda kk=kk, cc=cc, pbk=pbk, slot=slot: TE.matmul(
                        pb[pbk][:, :], lhsT=slot[:, kk, 256 + cc * 128:256 + (cc + 1) * 128], rhs=hTt_[:, kk, :], start=(kk == 0), stop=(kk == 15)))
                        for kk in range(16)], r=[slotB] + hb_, w=[pbB[pbk]])
                    sg, sgB = f512[3 + cc], f512B[3 + cc]
                    if L == 0:
                        k.op("act", lambda sg=sg, pbk=pbk: ACT.activation(out=sg[:, :], in_=pb[pbk][:, :], func=AF.Sigmoid),
                             r=[pbB[pbk]], w=[sgB])
                    else:
                        k.op("act", lambda sg=sg, pbk=pbk: ACT.copy(out=sg[:, :], in_=pb[pbk][:, :]), r=[pbB[pbk]], w=[sgB])
                    k.op("dve", lambda sg=sg, pa=pa, c=c: DVE.tensor_tensor(out=ut[:, c, :], in0=pb[pa][:, :], in1=sg[:, :], op=ALU.mult),
                         r=[pbB[pa], sgB], w=[utB])
            tl = t0 if g == 0 else 0
            k.dma("pst", us[L][g][:, :, tl:tl + 512], ut[:, :, :], r=[utB], w=[usB[L][j]])
            build_b(L, j + 1, 3)
            pc_pop(1 if L == 0 else 3)
        for half in range(2):
            s = half
            k.dma("sp", big[:, s, 0:1024].rearrange("p (b n) -> p b n", b=4),
                  cache[L][half].rearrange("(b p) n -> p b n", p=128), w=[bigB[s]])
        k.op("dve", lambda: DVE.tensor_copy(out=xn_all[:, 0, 0:1024], in_=big[:, 0, 0:1024]), r=[bigB[0]], w=[xnB[0]])
        k.op("act", lambda: ACT.copy(out=V[:, 0:4, :], in_=big[:, 1, 0:1024].rearrange("p (b n) -> p b n", b=4)), r=[bigB[1]], w=[VB])
        for b_ in range(4):
            pv = pbf(6 + b_ % 2, 8)
            k.group("pe", [(lambda hh=hh, b_=b_, pv=pv: TE.transpose(pv[:, hh, :], xn_all[:, 0, b_ * 256 + hh * 128:b_ * 256 + (hh + 1) * 128], ident[:]))
                           for hh in range(2)], r=[xnB[0], constB], w=[pbB[6 + b_ % 2]])
            k.op("act", lambda pv=pv, b_=b_: ACT.copy(out=KT[:, :, b_ * 128:(b_ + 1) * 128], in_=pv[:, 0:2, :]),
                 r=[pbB[6 + b_ % 2]], w=[KTB])

    def attention(L, j, g, hh, h, out_chunk):
        kvh = h // 4
        ob = 5 if h % 2 == 0 else 2
        chains = []
        if g == 1:
            for s in range(2):
                items = [(KTp[:, s, kvh, b_ * 128:(b_ + 1) * 128], Vp[:, s, b_, kvh * 128:(kvh + 1) * 128], s * 256, 256, [])
                         for b_ in range(2)]
                chains.append(items)
        else:
            items = []
            if L == 0:
                for kb in range(20):
                    items.append((KT[:, kvh, kb * 128:(kb + 1) * 128], V[:, kb, kvh * 128:(kvh + 1) * 128], 0, 512, []))
            else:
                for kb in range(4):
                    items.append((KT[:, kvh, kb * 128:(kb + 1) * 128], V[:, kb, kvh * 128:(kvh + 1) * 128], 0, 512, []))
                for kb in range(max(0, 4 * j - 1), min(15, 4 * j + 4) + 1):
                    lo, hi = max(4 * j, kb - 1), min(4 * j + 3, kb + 1)
                    masks = []
                    for qb in range(lo, hi + 1):
                        if qb == kb + 1:
                            masks.append(((qb - 4 * j) * 128, mkge))
                        elif qb == kb - 1:
                            masks.append(((qb - 4 * j) * 128, mkle))
                    items.append((KT[:, kvh, 512 + kb * 128:512 + (kb + 1) * 128], V[:, 4 + kb, kvh * 128:(kvh + 1) * 128],
                                  (lo - 4 * j) * 128, (hi - lo + 1) * 128, masks))
            chains.append(items)
        kB, vB = (KTpB, VpB) if g == 1 else (KTB, VB)
        flat = [(ci, ii, it) for ci, items in enumerate(chains) for ii, it in enumerate(items)]
        n = len(flat)

        SB_ = [3, 4, 7]
        ptt3 = [ptt[0], ptt[1], qn[0]]
        ptB3 = [ptB[0], ptB[1], qnB[0]]

        def qk(idx):
            ci, ii, (kap, vap, c0, nn, masks) = flat[idx]
            s = idx % 3
            fns = [lambda: TE.matmul(pb[SB_[s]][:, c0:c0 + nn], lhsT=kap, rhs=QT[:, h, c0:c0 + nn], start=True, stop=(len(masks) == 0),
                                     skip_group_check=True)]
            for mi, (mc0, mk) in enumerate(masks):
                fns.append(lambda mc0=mc0, mk=mk, mi=mi: TE.matmul(pb[SB_[s]][:, mc0:mc0 + 128], lhsT=ident[:, :], rhs=mk[:, :], start=False,
                                                                   stop=(mi == len(masks) - 1), skip_group_check=True))
            k.group("pe", fns, r=[kB, QTB[h // 4], constB], w=[pbB[SB_[s]]])

        qk(0)
        if n > 1:
            qk(1)
        for idx in range(n):
            ci, ii, (kap, vap, c0, nn, masks) = flat[idx]
            s = idx % 3
            if idx + 2 < n:
                qk(idx + 2)
            k.op("act", lambda: ACT.activation(out=ptt3[s][:, c0:c0 + nn], in_=pb[SB_[s]][:, c0:c0 + nn], func=AF.Exp, scale=SCALE),
                 r=[pbB[SB_[s]]], w=[ptB3[s]])
            first, last = (ii == 0), (ii == len(chains[ci]) - 1)
            k.group("pe", [
                lambda: TE.matmul(pb[ob][:, c0:c0 + nn], lhsT=vap, rhs=ptt3[s][:, c0:c0 + nn], start=first, stop=last,
                                  skip_group_check=True),
                lambda: TE.matmul(pb[6][:, c0:c0 + nn], lhsT=onesb[:, :], rhs=ptt3[s][:, c0:c0 + nn], start=first, stop=last,
                                  skip_group_check=True)],
                r=[ptB3[s], vB, constB], w=[pbB[ob], pbB[6]])
        rv, rvB = f512[1 + h % 2], f512B[1 + h % 2]
        if L == 1:
            k.op("dve", lambda: DVE.tensor_scalar(out=rv[:, :], in0=pb[6][:, :], scalar1=es[:, h:h + 1], scalar2=None, op0=ALU.add),
                 r=[pbB[6], esB], w=[rvB])
            k.op("dve", lambda: DVE.reciprocal(out=rv[:, :], in_=rv[:, :]), r=[rvB], w=[rvB])
        else:
            k.op("dve", lambda: DVE.reciprocal(out=rv[:, :], in_=pb[6][:, :]), r=[pbB[6]], w=[rvB])
        k.op("dve", lambda: DVE.tensor_tensor(out=rv[:, :], in0=pb[ob][:, :], in1=rv[:, :], op=ALU.mult), r=[pbB[ob], rvB], w=[rvB])
        k.op("dve", lambda: DVE.tensor_tensor(out=abT[:, out_chunk, :], in0=rv[:, :], in1=gbt[:, h, :], op=ALU.mult),
             r=[rvB, gbB[h]], w=[abB[out_chunk]])

    def main(L):
        H_ = 15 if L == 0 else 1
        nP = len(PRE_BLK[L])
        if L == 1:
            k.dma("sp", fgb, final_g[0:1, :].partition_broadcast(128), w=[fgbB])
        def dgbuf(c):
            return (ut[:].rearrange("p c n -> p (c n)"), [utB]) if c % 2 == 0 else (xn_all[:].rearrange("p a n -> p (a n)"), xnB)

        def dg_load(c):
            dgt, dgtB = dgbuf(c)
            k.dma("sp", dgt[:, 0:31 * 128], dgs[c], r=[dgsB[c]], w=dgtB)

        def uviews(j):
            g, t0 = TILES[j]
            ux = uxb[j % 2]
            if g == 0:
                W_ = 512 + 2 * H_
                return ux, ux[:, 0:8 * W_].rearrange("p (c w) -> p c w", c=8)
            W_ = 256 + 2 * H_
            return ux, ux[:, 0:16 * W_].rearrange("p (c s w) -> p c s w", c=8, s=2)

        def tile_loads(L_, j):
            if j >= 5:
                return
            g, t0 = TILES[j]
            k.dma("sp", hTt[:].rearrange("p k n -> p (k n)"), hs[L][j], r=[hsB[L][j]], w=hb)
            ux, uv = uviews(j)
            uB = uxB[j % 2]
            if g == 0:
                lo = H_ if j > 0 else 0
                hi = H_ if j < 3 else 0
                if lo == 0:
                    k.op("pool", lambda: POOL.memset(uv[:, :, 0:H_], 0.0), w=[uB])
                if hi == 0:
                    k.op("pool", lambda: POOL.memset(uv[:, :, H_ + 512:H_ + 512 + H_], 0.0), w=[uB])
                rb = [usB[L][jj] for jj in range(max(0, j - 1), min(3, j + 1) + 1)]
                k.dma("sp", uv[:, :, H_ - lo:H_ + 512 + hi], us[L][0][:, :, t0 - lo:t0 + 512 + hi], r=rb, w=[uB])
            else:
                W_ = 256 + 2 * H_
                k.op("pool", lambda: POOL.memset(uv[:, :, :, 0:H_], 0.0), w=[uB])
                k.op("pool", lambda: POOL.memset(uv[:, :, :, H_ + 256:H_ + 256 + H_], 0.0), w=[uB])
                k.dma("sp", uv[:, :, :, H_:H_ + 256], us[L][1][:, :, :].rearrange("p c (s w) -> p c s w", s=2), r=[usB[L][4]], w=[uB])
            if L == 0:
                dg_load(0)
                dg_load(1)

        tile_loads(L, 0)
        for j, (g, t0) in enumerate(TILES):
            if L == 0 and j == 0:
                pc_pop(5)
            ux, uv = uviews(j)
            uB = uxB[j % 2]
            if g == 0:
                nseg, sl = 1, 512

                def uwin(c, jj, uv=uv):
                    return uv[:, c, jj:jj + 512]
            else:
                nseg, sl = 2, 256

                def uwin(c, jj, uv=uv):
                    return uv[:, c, :, jj:jj + 256]

            def seg(ap):
                return ap if g == 0 else ap.rearrange("p (s w) -> p s w", s=2)

            ap_ = big[:, 2:4, :].rearrange("p s (c n) -> p (s c) n", c=4)

            def apB(c):
                return bigB[2 + c // 4]
            mean, rstd = big[:, 0, 0:512], big[:, 0, 512:1024]
            meanB = rstdB = bigB[0]
            cur = {}
            pend = {}

            def conv_piece(c):
                dgt, dgtB = dgbuf(c)
                dg3 = dgt[:, 0:31 * 128].rearrange("p (j m) -> p j m", j=31)
                pc = 3 + c % 2
                fns = []
                for sgi in range(nseg):
                    for jj in range(31):
                        if g == 0:
                            fns.append(lambda jj=jj: TE.matmul(pb[pc][:, :], lhsT=dg3[:, jj, :], rhs=uwin(c, jj),
                                                               start=(jj == 0), stop=(jj == 30), skip_group_check=True))
                        else:
                            fns.append(lambda jj=jj, sgi=sgi: TE.matmul(
                                pb[pc][:, sgi * 256:(sgi + 1) * 256], lhsT=dg3[:, jj, :], rhs=uwin(c, jj)[:, sgi, :],
                                start=(jj == 0), stop=(jj == 30), skip_group_check=True))
                k.group("pe", fns, r=[uB] + dgtB, w=[pbB[pc]])
                if c + 2 < 8:
                    dg_load(c + 2)
                k.op("act", lambda: ACT.activation(out=ap_[:, c, :], in_=pb[pc][:, :], func=AF.Identity, bias=cvc[:, c:c + 1],
                                                   scale=1.0), r=[pbB[pc], parB], w=[apB(c)])
                sq = big[:, 1, (c % 2) * 512:(c % 2 + 1) * 512]
                k.op("act", lambda: ACT.activation(out=sq, in_=ap_[:, c, :], func=AF.Square), r=[apB(c)], w=[bigB[1]])

                def stats():
                    k.group("pe", [lambda: TE.matmul(pb[5][:, :], lhsT=onesf[:, :], rhs=ap_[:, c, :], start=(c == 0), stop=(c == 7),
                                                     skip_group_check=True),
                                   lambda: TE.matmul(pb[6][:, :], lhsT=onesf[:, :], rhs=sq, start=(c == 0), stop=(c == 7),
                                                     skip_group_check=True)],
                            r=[apB(c), bigB[1], constB], w=[pbB[5], pbB[6]])
                if "s" in pend:
                    pend.pop("s")()
                pend["s"] = stats

            def ln_finalize():
                pend.pop("s")()
                k.op("dve", lambda: DVE.tensor_scalar(out=mean, in0=pb[5][:, :], scalar1=1.0 / 1024, scalar2=None, op0=ALU.mult),
                     r=[pbB[5]], w=[meanB])
                k.op("dve", lambda: DVE.tensor_tensor(out=rstd, in0=mean, in1=mean, op=ALU.mult), r=[meanB], w=[rstdB])
                k.op("dve", lambda: DVE.scalar_tensor_tensor(out=rstd, in0=pb[6][:, :], scalar=1.0 / 1024, in1=rstd,
                                                             op0=ALU.mult, op1=ALU.subtract), r=[pbB[6], rstdB], w=[rstdB])
                k.op("act", lambda: ACT.activation(out=rstd, in_=rstd, func=AF.Sqrt, bias=EPS, scale=1.0), r=[rstdB], w=[rstdB])
                k.op("dve", lambda: DVE.reciprocal(out=rstd, in_=rstd), r=[rstdB], w=[rstdB])

            def gate_ln_piece(c):
                if c % 4 == 0:
                    cur["ga"] = ws_get(L, nP + 4 + c // 4)
                slot, slotB = cur["ga"]
                pa = c % 2
                k.group("pe", [(lambda kk=kk: TE.matmul(
                    pb[pa][:, :], lhsT=slot[:, kk, (c % 4) * 128:(c % 4 + 1) * 128], rhs=hTt[:, kk, :], start=(kk == 0), stop=(kk == 15)))
                    for kk in range(16)], r=[slotB] + hb, w=[pbB[pa]])
                k.op("act", lambda: ACT.activation(out=gts[:, pa, :], in_=pb[pa][:, :], func=AF.Silu), r=[pbB[pa]], w=[gtsB[pa]])
                k.op("dve", lambda: DVE.tensor_tensor(out=ap_[:, c, :], in0=ap_[:, c, :], in1=mean, op=ALU.subtract),
                     r=[apB(c), meanB], w=[apB(c)])
                k.op("dve", lambda: DVE.tensor_tensor(out=ap_[:, c, :], in0=ap_[:, c, :], in1=rstd, op=ALU.mult),
                     r=[apB(c), rstdB], w=[apB(c)])
                k.op("act", lambda: ACT.activation(out=ap_[:, c, :], in_=ap_[:, c, :], func=AF.Silu, bias=cvc[:, 16 + c:17 + c],
                                                   scale=cvc[:, 8 + c:9 + c]), r=[apB(c), parB], w=[apB(c)])
                k.op("dve", lambda: DVE.tensor_tensor(out=abT[:, c, :], in0=ap_[:, c, :], in1=gts[:, pa, :], op=ALU.mult),
                     r=[apB(c), gtsB[pa]], w=[abB[c]])

            def d_piece(c):
                i, cc = c // 2, c % 2
                if cc == 0:
                    cur["d"] = ws_get(L, nP + 4 + i)
                slot, slotB = cur["d"]
                k.group("pe", [(lambda kk=kk: TE.matmul(
                    pb[0][:, :], lhsT=slot[:, kk, cc * 128:(cc + 1) * 128], rhs=hTt[:, kk, :], start=(kk == 0), stop=(kk == 15)))
                    for kk in range(16)], r=[slotB] + hb, w=[pbB[0]])
                k.group("pe", [(lambda kk=kk: TE.matmul(
                    pb[1][:, :], lhsT=slot[:, kk, 256 + cc * 128:256 + (cc + 1) * 128], rhs=hTt[:, kk, :], start=(kk == 0), stop=(kk == 15)))
                    for kk in range(16)], r=[slotB] + hb, w=[pbB[1]])
                a3, a3B = f512[3 + cc], f512B[3 + cc]
                sgd = gts[:, cc, :]
                k.op("act", lambda: ACT.activation(out=sgd, in_=pb[1][:, :], func=AF.Silu), r=[pbB[1]], w=[gtsB[cc]])
                k.op("dve", lambda: DVE.tensor_scalar(out=seg(a3[:, :]), in0=uwin(c, 0), scalar1=sw[:, c, 0:1], scalar2=None,
                                                      op0=ALU.mult), r=[uB, parB], w=[a3B])
                for jj in range(1, 3):
                    k.op("dve", lambda jj=jj: DVE.scalar_tensor_tensor(out=seg(a3[:, :]), in0=uwin(c, jj), scalar=sw[:, c, jj:jj + 1],
                                                                         in1=seg(a3[:, :]), op0=ALU.mult, op1=ALU.add),
                         r=[uB, parB, a3B], w=[a3B])
                k.op("dve", lambda: DVE.tensor_tensor(out=a3[:, :], in0=pb[0][:, :], in1=a3[:, :], op=ALU.mult), r=[pbB[0], a3B], w=[a3B])
                k.op("dve", lambda: DVE.tensor_tensor(out=abT[:, 8 + c, :], in0=a3[:, :], in1=sgd, op=ALU.mult),
                     r=[a3B, gtsB[cc]], w=[abB[8 + c]])

            def q_piece(grp, tb):
                if tb == 0:
                    cur["q"] = ws_get(L, nP + 2 * grp)
                slot, slotB = cur["q"]
                pa = tb % 2
                k.group("pe", [(lambda kk=kk: TE.matmul(
                    pb[pa][:, :], lhsT=hTt[:, kk, tb * 128:(tb + 1) * 128], rhs=slot[:, kk, :], start=(kk == 0), stop=(kk == 15)))
                    for kk in range(16)], r=[slotB] + hb, w=[pbB[pa]])
                qf, qfB = f512[3 + tb % 2], f512B[3 + tb % 2]
                q16, q16B = qn[tb % 2], qnB[tb % 2]
                qr, qrB = f512[1 + tb % 2], f512B[1 + tb % 2]
                k.op("act", lambda: ACT.copy(out=qr[:, :], in_=pb[pa][:, :]), r=[pbB[pa]], w=[qrB])
                if L == 0:
                    rq, rqB = head_rstd(qr[:, :], 4, qrB)
                if g == 0:
                    rope_apply(qr[:, :], 4, tb, qf[:, :], qfB, qrB)
                    if L == 0:
                        k.op("dve", lambda: DVE.tensor_tensor(
                            out=q16[:, :].rearrange("p (h d) -> p h d", h=4), in0=qf[:, :].rearrange("p (h d) -> p h d", h=4),
                            in1=rq.unsqueeze(2).to_broadcast([128, 4, 128]), op=ALU.mult), r=[qfB, rqB], w=[q16B])
                    else:
                        k.op("dve", lambda: DVE.tensor_copy(out=q16[:, :], in_=qf[:, :]), r=[qfB], w=[q16B])
                else:
                    if L == 0:
                        k.op("dve", lambda: DVE.tensor_tensor(
                            out=qf[:, :].rearrange("p (h d) -> p h d", h=4), in0=qr[:, :].rearrange("p (h d) -> p h d", h=4),
                            in1=gqk[:, 0, 0, :].unsqueeze(1).to_broadcast([128, 4, 128]), op=ALU.mult), r=[qrB, gqkB], w=[qfB])
                        k.op("dve", lambda: DVE.tensor_tensor(
                            out=q16[:, :].rearrange("p (h d) -> p h d", h=4), in0=qf[:, :].rearrange("p (h d) -> p h d", h=4),
                            in1=rq.unsqueeze(2).to_broadcast([128, 4, 128]), op=ALU.mult), r=[qfB, rqB], w=[q16B])
                    else:
                        k.op("dve", lambda: DVE.tensor_copy(out=q16[:, :], in_=qr[:, :]), r=[qrB], w=[q16B])

            def q_tail(grp, tb):
                q16, q16B = qn[tb % 2], qnB[tb % 2]
                pv = pbf(2, 8)
                k.group("pe", [(lambda hh=hh: TE.transpose(pv[:, hh, :], q16[:, hh * 128:(hh + 1) * 128], ident[:]))
                               for hh in range(4)], r=[q16B, constB], w=[pbB[2]])
                k.op("act", lambda: ACT.copy(out=QT[:, 4 * grp:4 * grp + 4, tb * 128:(tb + 1) * 128], in_=pv[:, 0:4, :]),
                     r=[pbB[2]], w=[QTB[grp]])

            def gb_piece(grp, hh):
                if hh == 0:
                    cur["gb"] = ws_get(L, nP + 1 + 2 * grp)
                slot, slotB = cur["gb"]
                pa = hh % 2
                h = 4 * grp + hh
                k.group("pe", [(lambda kk=kk: TE.matmul(
                    pb[pa][:, :], lhsT=slot[:, kk, hh * 128:(hh + 1) * 128], rhs=hTt[:, kk, :], start=(kk == 0), stop=(kk == 15)))
                    for kk in range(16)], r=[slotB] + hb, w=[pbB[pa]])
                k.op("act", lambda: ACT.activation(out=gbt[:, h, :], in_=pb[pa][:, :], func=AF.Silu), r=[pbB[pa]], w=[gbB[h]])

            def x_loads():
                for tb in range(4):
                    t = t0 + tb * 128
                    k.dma("sp", big[:, tb, :], xsrc(L, t), r=([x1B[t // 128]] if L == 1 else []), w=[bigB[tb]])

            if g == 0:
                load_rope(L, t0, 0)
            qgb = []
            for grp in range(2):
                for tb in range(4):
                    qgb.append(lambda grp=grp, tb=tb: q_piece(grp, tb))
                    if tb > 0:
                        qgb.append(lambda grp=grp, tb=tb: q_tail(grp, tb - 1))
                qgb.append(lambda grp=grp: gb_piece(grp, 0))
                qgb.append(lambda grp=grp: q_tail(grp, 3))
                qgb += [(lambda grp=grp, hh=hh: gb_piece(grp, hh)) for hh in range(1, 4)]
            att0 = 8 if L == 0 else 0
            if L == 0:
                for c in range(8):
                    conv_piece(c)
                    for _ in range(3):
                        qgb.pop(0)()
                ln_finalize()
                for h in range(8):
                    attention(L, j, g, h % 4, h, att0 + h)
                    if h < 4:
                        gate_ln_piece(2 * h)
                        gate_ln_piece(2 * h + 1)
                    if h == 3:
                        tile_loads(L, j + 1)
                        x_loads()
                    if 4 <= h <= 6 and len(MOD1_PARTS[j]) > h - 4:
                        ch_ = MOD1_PARTS[j][h - 4]
                        slot_, slotB_ = ws_get("m", ch_)
                        adaln_block(1, ch_, slot_, slotB_, bk=0)
                    if j < 2 or h % 2 == 1:
                        pc_pop(1)
            else:
                while qgb:
                    qgb.pop(0)()
                for h in range(8):
                    attention(L, j, g, h % 4, h, att0 + h)
                    if h < 4:
                        d_piece(2 * h)
                        d_piece(2 * h + 1)
                    if h == 3:
                        tile_loads(L, j + 1)
                        x_loads()
            if L == 0 and j == 0:
                adaln0_late()
            load_gt(L, g)
            wob = nP + (6 if L == 0 else 8)
            accs = [0, 1, 3, 4]
            for nb in range(4):
                slot, slotB = ws_get(L, wob + nb)
                for tb in range(4):
                    a = accs[(nb * 4 + tb) % 4]
                    k.group("pe", [(lambda kk=kk, tb=tb, a=a, slot=slot: TE.matmul(
                        pb[a][:, :], lhsT=abT[:, kk, tb * 128:(tb + 1) * 128], rhs=slot[:, kk, :], start=(kk == 0), stop=(kk == 15)))
                        for kk in range(16)], r=[slotB] + abB, w=[pbB[a]])
                    tm, tmB = f512[1 + tb % 2], f512B[1 + tb % 2]
                    k.op("dve", lambda a=a, nb=nb, tm=tm: DVE.tensor_tensor(out=tm[:, :], in0=pb[a][:, :], in1=gtb[:, nb * 512:(nb + 1) * 512],
                                                                           op=ALU.mult), r=[pbB[a], gtbB], w=[tmB])
                    k.op("dve", lambda tb=tb, nb=nb, tm=tm: DVE.tensor_tensor(out=big[:, tb, nb * 512:(nb + 1) * 512], in0=tm[:, :],
                                                                             in1=big[:, tb, nb * 512:(nb + 1) * 512], op=ALU.add),
                         r=[tmB, bigB[tb]], w=[bigB[tb]])
            for tb in range(4):
                t = t0 + tb * 128
                if L == 0:
                    k.dma("pst", x1s[t:t + 128, :], big[:, tb, :], r=[bigB[tb]], w=[x1B[t // 128]])
                    if tb == 3 and j == 3:
                        load_mod_cols(1)
                else:
                    st, stB = st_next()
                    k.op("act", lambda tb=tb, st=st: ACT.activation(out=xn[0], in_=big[:, tb, :], func=AF.Square, accum_out=st[:, 0:1]),
                         r=[bigB[tb]], w=[xnB[0], stB])
                    k.op("act", lambda st=st: ACT.activation(out=st[:, 1:2], in_=st[:, 0:1], func=AF.Sqrt, bias=EPS, scale=1.0 / D), r=[stB], w=[stB])
                    k.op("dve", lambda st=st: DVE.reciprocal(out=st[:, 2:3], in_=st[:, 1:2]), r=[stB], w=[stB])
                    k.op("dve", lambda tb=tb, st=st: DVE.scalar_tensor_tensor(out=big[:, tb, :], in0=big[:, tb, :], scalar=st[:, 2:3], in1=fgb,
                                                                             op0=ALU.mult, op1=ALU.mult), r=[bigB[tb], stB, fgbB], w=[bigB[tb]])
                    dst = ys[t:t + 128, :] if t < 2048 else yp[t - 2048:t - 2048 + 128, :]
                    k.dma("pst", dst, big[:, tb, :], r=[bigB[tb]])

    def pc_setup():
        precast(0, 0, 5)
        precast(0, 5, 99, queue=True)
        mwv0 = mod_w[0].rearrange("(k p) n -> p k n", p=128)
        for ch in range(8, 12):
            pcq.append(lambda ch=ch: k.dma("pool", msc0[ch - 8].rearrange("p (k n) -> p k n", k=16),
                                           mwv0[:, :, ch * 512:(ch + 1) * 512], w=[msc0B[ch - 8]]))
        precast_mod1()
        precast(1, queue=True)

    steps = [lambda: adaln0(), lambda: pc_setup(), lambda: load_mod_cols(0), lambda: pre(0),
             lambda: main(0), lambda: pc_pop(6), lambda: pre(1), lambda: (pc_pop(99), main(1))]
    for si, stp in enumerate(steps):
        if si < stop:
            stp()
    k.finish()
    k.es.close()
    nc._kb = k
    return nc


def _rope_tables():
    rows = 2048 // 64
    r = np.repeat(np.arange(rows), 64).astype(np.float32)
    col = np.tile(np.arange(64), rows).astype(np.float32)
    half = 64
    inv = (np.float32(10000.0) ** (-np.arange(0, half, 2, dtype=np.float32) / np.float32(half))).astype(np.float32)
    ang_r = r[:, None] * inv
    ang_c = col[:, None] * inv
    ang = np.concatenate([ang_r, ang_r, ang_c, ang_c], axis=-1).astype(np.float32)
    cos = np.cos(ang).astype(np.float32)
    sin = np.sin(ang).astype(np.float32)
    sign = np.concatenate([-np.ones(32), np.ones(32), -np.ones(32), np.ones(32)]).astype(np.float32)
    return np.ascontiguousarray(cos), np.ascontiguousarray(sin * sign[None, :])


def make_in_maps(inp, cores):
    f = lambda a: np.ascontiguousarray(np.asarray(a, dtype=np.float32))
    cos, sin = _rope_tables()
    shared = {
        "mod_w0": f(inp["mod_w0"]), "mod_w1": f(inp["mod_w1"]),
        "mod_b0": f(inp["mod_b0"]).reshape(1, -1), "mod_b1": f(inp["mod_b1"]).reshape(1, -1),
        "norm_g0": f(inp["norm_g0"]).reshape(16, 128), "norm_g1": f(inp["norm_g1"]).reshape(16, 128),
        "w_in0": f(inp["w_in0"]), "w_in1": f(inp["w_in1"]), "w_out0": f(inp["w_out0"]), "w_out1": f(inp["w_out1"]),
        "conv_w0": f(inp["conv_w0"]),
        "cvec0": np.ascontiguousarray(np.concatenate([f(inp["conv_b0"]).reshape(8, 128), f(inp["ln_g0"]).reshape(8, 128),
                                                      f(inp["ln_b0"]).reshape(8, 128)], axis=0)),
        "qk_g": np.ascontiguousarray(np.stack([f(inp["q_norm_g0"]), f(inp["k_norm_g0"])], axis=0)),
        "sink1": f(inp["sink1"]).reshape(1, 8), "short_w1": f(inp["short_w1"]),
        "final_g": f(inp["final_norm_g"]).reshape(1, -1), "rope_c": cos, "rope_s": sin,
    }
    maps = []
    for b in cores:
        m = dict(shared)
        m["xs"] = f(inp["x_sample"][b])
        m["xp"] = f(inp["x_prompt"][2 * b:2 * b + 2]).reshape(512, D)
        m["ck0"] = f(inp["cache_k0"][b]).reshape(512, 256)
        m["cv0"] = f(inp["cache_v0"][b]).reshape(512, 256)
        m["ck1"] = f(inp["cache_k1"][b]).reshape(512, 256)
        m["cv1"] = f(inp["cache_v1"][b]).reshape(512, 256)
        m["cond"] = np.ascontiguousarray(np.stack([f(inp["c"][b]), f(inp["c_ctx"])], axis=0))
        maps.append(m)
    return maps


def kernel(**inputs):
    nc = build()
    maps = make_in_maps(inputs, list(range(8)))
    res = run_bass_kernel_spmd(nc, maps, core_ids=list(range(8)))
    rs = res.results
    y_sample = np.stack([rs[b]["ys"] for b in range(8)], axis=0).astype(np.float32)
    y_prompt = np.concatenate([rs[b]["yp"].reshape(2, 256, D) for b in range(8)], axis=0).astype(np.float32)
    outs = [y_prompt, y_sample]
    for nm in ("nk0", "nv0", "nk1", "nv1"):
        outs.append(np.concatenate([rs[b][nm].reshape(2, 256, 2, 128) for b in range(8)], axis=0).astype(np.float32))
    return tuple(outs)
```

```python
import numpy as np
from contextlib import ExitStack
import concourse.bass as bass
import concourse.mybir as mybir
from concourse.bass_utils import run_bass_kernel_spmd

F32 = mybir.dt.float32
BF16 = mybir.dt.bfloat16
AF = mybir.ActivationFunctionType
ALU = mybir.AluOpType

D = 2048
NTOK = 2560
EPS = 1e-6
SAME_ENGINE_SYNC = {'pe': False, 'dve': True, 'act': True, 'pool': True, 'sp': True}
NS = 8
NR = 2
TILES = [(0, 0), (0, 512), (0, 1024), (0, 1536), (1, 2048)]
IN_COLS = [5632, 6656]
SCALE = 128.0 ** -0.5


class Buf:
    __slots__ = ("w", "r", "n", "x")

    def __init__(self, n="", x=False):
        self.w = {}
        self.r = {}
        self.n = n
        self.x = x


class KB:
    def __init__(self):
        self.nc = bass.Bass("TRN2", target_bir_lowering=False)
        self.es = ExitStack()
        nc = self.nc
        self.eng = {"pe": nc.tensor, "act": nc.scalar, "dve": nc.vector, "pool": nc.gpsimd, "sp": nc.sync}
        self.sem = {n: self.es.enter_context(nc.semaphore("s_" + n)) for n in ("pe", "act", "dve", "pool")}
        self.cnt = {n: 0 for n in self.sem}
        self.seen = {n: {} for n in self.eng}
        self.dsem = {q: [self.es.enter_context(nc.semaphore("d_%s%d" % (q, i))) for i in range(NS)]
                     for q in ("sp", "pool", "pst")}
        self.dcnt = {q: [0] * NS for q in ("sp", "pool", "pst")}
        self.dnext = {q: 0 for q in ("sp", "pool", "pst")}
        self.qeng = {"sp": "sp", "pool": "pool", "pst": "pool"}
        self.nops = 0
        self.limit = 10 ** 9
        self.lines = []

    def _skip(self):
        import sys
        self.nops += 1
        if self.nops > self.limit:
            return True
        self.lines.append(sys._getframe(2).f_lineno)
        return False

    def sb(self, name, shape, dt):
        return self.es.enter_context(self.nc.sbuf_tensor(name, shape, dt))

    def ps(self, name, shape, dt):
        return self.es.enter_context(self.nc.psum_tensor(name, shape, dt))

    def _wait(self, e, key, val):
        if val <= 0 or self.seen[e].get(key, 0) >= val:
            return
        sem = self.sem[key] if isinstance(key, str) else self.dsem[key[1]][key[2]]
        self.eng[e].wait_ge(sem, val)
        self.seen[e][key] = val

    def _deps(self, e, r, w):
        need = {}
        for b in r:
            for k2, v in b.w.items():
                if need.get(k2, 0) < v:
                    need[k2] = v
            if b.x:
                for k2, v in b.r.items():
                    if k2 != e and need.get(k2, 0) < v:
                        need[k2] = v
        for b in w:
            for k2, v in b.w.items():
                if need.get(k2, 0) < v:
                    need[k2] = v
            for k2, v in b.r.items():
                if need.get(k2, 0) < v:
                    need[k2] = v
        for k2, v in need.items():
            if k2 == e and not SAME_ENGINE_SYNC[e]:
                continue
            self._wait(e, k2, v)

    def _mark(self, key, val, r, w):
        for b in w:
            b.w = {key: val}
            b.r = {}
        for b in r:
            if b.r.get(key, 0) < val:
                b.r[key] = val

    def op(self, e, fn, r=(), w=()):
        if self._skip():
            return
        self._deps(e, r, w)
        ins = fn()
        self.cnt[e] += 1
        ins.then_inc(self.sem[e], 1)
        self._mark(e, self.cnt[e], r, w)

    def group(self, e, fns, r=(), w=()):
        if self._skip():
            return
        self._deps(e, r, w)
        ins = None
        for f in fns:
            ins = f()
        self.cnt[e] += 1
        ins.then_inc(self.sem[e], 1)
        self._mark(e, self.cnt[e], r, w)

    def dma(self, q, out, in_, r=(), w=()):
        if self._skip():
            return
        i = self.dnext[q]
        self.dnext[q] = (i + 1) % NS
        key = ("d", q, i)
        e = self.qeng[q]
        self._wait(e, key, self.dcnt[q][i])
        self._deps(e, r, w)
        ins = self.eng[e].dma_start(out=out, in_=in_)
        self.dcnt[q][i] += 16
        ins.then_inc(self.dsem[q][i], 16)
        self._mark(key, self.dcnt[q][i], r, w)

    def finish(self):
        for q in ("sp", "pool", "pst"):
            for i in range(NS):
                self._wait("sp", ("d", q, i), self.dcnt[q][i])
        for e in ("pe", "act", "dve", "pool"):
            self._wait("sp", e, self.cnt[e])


def build(debug=False, stop=99, limit=10 ** 9):
    k = KB()
    k.limit = limit
    nc = k.nc
    TE, ACT, DVE, POOL = nc.tensor, nc.scalar, nc.vector, nc.gpsimd

    def din(name, shape):
        return nc.dram_tensor(name, shape, F32, kind="ExternalInput").ap()

    def dout(name, shape):
        return nc.dram_tensor(name, shape, F32, kind="ExternalOutput").ap()

    skind = "ExternalOutput" if debug else "Internal"

    def dscr(name, shape, dt):
        return nc.dram_tensor(name, shape, dt, kind=skind).ap()

    xs = din("xs", [2048, D])
    xp = din("xp", [512, D])
    cache = [[din("ck0", [512, 256]), din("cv0", [512, 256])], [din("ck1", [512, 256]), din("cv1", [512, 256])]]
    cond = din("cond", [2, D])
    mod_w = [din("mod_w0", [D, 3 * D]), din("mod_w1", [D, 3 * D])]
    mod_b = [din("mod_b0", [1, 3 * D]), din("mod_b1", [1, 3 * D])]
    norm_g = [din("norm_g0", [16, 128]), din("norm_g1", [16, 128])]
    w_in = [din("w_in0", [D, IN_COLS[0]]), din("w_in1", [D, IN_COLS[1]])]
    w_out = [din("w_out0", [D, D]), din("w_out1", [D, D])]
    conv_w0 = din("conv_w0", [31, 1024])
    cvec0 = din("cvec0", [24, 128])
    qk_g = din("qk_g", [2, 128])
    sink1 = din("sink1", [1, 8])
    short_w1 = din("short_w1", [3, 1024])
    final_g = din("final_g", [1, D])
    rope_c = din("rope_c", [2048, 128])
    rope_s = din("rope_s", [2048, 128])

    ys = dout("ys", [2048, D])
    yp = dout("yp", [512, D])
    nkv = [[dout("nk0", [512, 256]), dout("nv0", [512, 256])], [dout("nk1", [512, 256]), dout("nv1", [512, 256])]]

    def contig(b):
        return [(b * 512, 512)]

    PRE_BLK = [
        [("in", [(4096, 512)])] + [("in", [(2 * i * 128, 256), (1024 + 2 * i * 128, 256)]) for i in range(4)],
        [("in", [(1024, 512)])] + [("in", [(3584 + 2 * i * 128, 256), (4608 + 2 * i * 128, 256)]) for i in range(4)],
    ]
    MAIN_BLK = [
        [("in", contig(6)), ("in", contig(9)), ("in", contig(7)), ("in", contig(10)), ("in", contig(4)),
         ("in", contig(5))] + [("out", contig(i)) for i in range(4)],
        [("in", contig(0)), ("in", contig(3)), ("in", contig(1)), ("in", contig(4))]
        + [("in", [(2560 + 2 * i * 128, 256), (5632 + 2 * i * 128, 256)]) for i in range(4)]
        + [("out", contig(i)) for i in range(4)],
    ]
    nblk = [len(PRE_BLK[L]) + len(MAIN_BLK[L]) for L in range(2)]
    wsc = [dscr("wsc%d" % L, [nblk[L], 128, 16 * 512], BF16) for L in range(2)]
    wscB = [[Buf() for _ in range(nblk[L])] for L in range(2)]
    hs = [dscr("hs%d" % L, [5, 128, 16 * 512], BF16) for L in range(2)]
    hsB = [[Buf() for _ in range(5)] for L in range(2)]
    us = [[dscr("us%d_%d" % (L, g), [128, 8, 2048 if g == 0 else 512], BF16) for g in range(2)] for L in range(2)]
    usB = [[Buf() for _ in range(5)] for L in range(2)]
    x1s = dscr("x1s", [NTOK, D], F32)
    x1B = [Buf() for _ in range(20)]
    mvec = [dscr("mvec%d" % L, [2, 3 * D], F32) for L in range(2)]
    mvecB = [[Buf() for _ in range(12)] for L in range(2)]

    ring = [k.sb("ring%d" % i, [128, 16, 512], BF16) for i in range(NR)]
    ringB = [Buf() for _ in range(NR)]
    hTt = k.sb("hTt", [128, 16, 512], BF16)
    hb = [Buf() for _ in range(16)]
    KT = k.sb("KT", [128, 2, 2560], BF16)
    KTB = Buf()
    V = k.sb("V", [128, 20, 256], BF16)
    VB = Buf()
    KTp = k.sb("KTp", [128, 2, 2, 256], BF16)
    KTpB = Buf()
    Vp = k.sb("Vp", [128, 2, 2, 256], BF16)
    VpB = Buf()
    big = k.sb("big", [128, 4, 2048], F32)
    bigB = [Buf() for _ in range(4)]
    gtb = k.sb("gtb", [128, 2048], F32)
    gtbB = Buf()
    gt_cur = [None]
    ropet = k.sb("ropet", [128, 2, 4, 128], F32)
    ropeB = Buf()
    uxb = [k.sb("uxb%d" % i, [128, 4608], BF16) for i in range(2)]
    uxB = [Buf() for _ in range(2)]
    gts = k.sb("gts", [128, 2, 512], BF16)
    gtsB = [Buf() for _ in range(2)]
    abT = k.sb("abT", [128, 16, 512], BF16)
    abB = [Buf() for _ in range(16)]
    QT = k.sb("QT", [128, 8, 512], BF16)
    QTB = [Buf() for _ in range(2)]
    gbt = k.sb("gbt", [128, 8, 512], BF16)
    gbB = [Buf() for _ in range(8)]
    ptt = [k.sb("ptt%d" % i, [128, 512], BF16) for i in range(2)]
    ptB = [Buf() for _ in range(2)]
    f512 = [k.sb("f512_%d" % i, [128, 512], F32) for i in range(5)]
    f512B = [Buf() for _ in range(5)]
    qn = [k.sb("qn%d" % i, [128, 512], BF16) for i in range(2)]
    qnB = [Buf() for _ in range(2)]
    xn_all = k.sb("xn_all", [128, 2, 2048], BF16)
    xn = [xn_all[:, i, :] for i in range(2)]
    xnB = [Buf() for _ in range(2)]
    dgs = dscr("dgs", [8, 128, 31 * 128], BF16)
    dgsB = [Buf() for _ in range(8)]
    junk = k.sb("junk", [128, 512], BF16)
    junkB = Buf()
    ut = k.sb("ut", [128, 8, 512], BF16)
    utB = Buf()
    fgb = ut[:].rearrange("p c n -> p (c n)").bitcast(F32)
    fgbB = utB
    stt = k.sb("stt", [128, 16, 8], F32)
    sttB = [Buf() for _ in range(16)]
    st_i = [0]
    identf = k.sb("identf", [128, 128], F32)
    ident = k.sb("ident", [128, 128], BF16)
    onesf = k.sb("onesf", [128, 128], F32)
    onesb = k.sb("onesb", [128, 128], BF16)
    mkge = k.sb("mkge", [128, 128], BF16)
    mkle = k.sb("mkle", [128, 128], BF16)
    mtmp = k.sb("mtmp", [128, 128], F32)
    constB = Buf()
    cw = k.sb("cw", [128, 8, 32], F32)
    sw = k.sb("sw", [128, 8, 4], F32)
    cvc = k.sb("cvc", [128, 24], F32)
    ngc = k.sb("ngc", [128, 2, 16], F32)
    parB = Buf()
    mc = k.sb("mc", [128, 64], F32)
    gsc = k.sb("gsc", [128, 2, 16], F32)
    mcB = Buf()
    stage = k.sb("stage", [128, 128], F32)
    stageB = Buf()
    gqk = k.sb("gqk", [128, 2, 2, 128], F32)
    gqkB = Buf()
    es = k.sb("es", [128, 8], F32)
    esB = Buf()
    sT = k.sb("sT", [128, 16, 2], BF16)
    sTB = Buf()
    pb = [k.ps("pb%d" % i, [128, 512], F32) for i in range(8)]
    pbB = [Buf(x=True) for _ in range(8)]

    def pbf(i, kk):
        return pb[i][:].bitcast(BF16).rearrange("p (k t) -> p k t", k=kk)

    def st_next():
        i = st_i[0]
        st_i[0] = (i + 1) % 16
        return stt[:, i, :], sttB[i]

    msc = dscr("msc", [12, 128, 16 * 512], BF16)
    mscB = [Buf() for _ in range(12)]
    MOD1_PARTS = [[], [0, 1, 2], [3, 4, 5], [6, 7, 8], [9, 10, 11]]
    wseq = []
    for L in range(2):
        for j in range(5):
            wseq += [(L, i) for i in range(len(PRE_BLK[L]))]
        for j in range(5):
            nmb = len(MAIN_BLK[L])
            wseq += [(L, len(PRE_BLK[L]) + i) for i in range(nmb - 4)]
            if L == 0 and j == 0:
                wseq += [("m0", ch) for ch in range(8, 12)]
            if L == 0:
                wseq += [("m", ch) for ch in MOD1_PARTS[j]]
            wseq += [(L, len(PRE_BLK[L]) + i) for i in range(nmb - 4, nmb)]
    ws = {"i": 0, "issued": 0}

    pcq = []

    def pc_pop(n):
        for _ in range(n):
            if pcq:
                pcq.pop(0)()

    def precast(L, lo=0, hi=99, queue=False):
        blks = PRE_BLK[L] + MAIN_BLK[L]
        for bi, (which, segs) in enumerate(blks):
            if not (lo <= bi < hi):
                continue
            if queue:
                pcq.append(lambda bi=bi: precast(L, bi, bi + 1))
                continue
            src = (w_in[L] if which == "in" else w_out[L]).rearrange("(k p) n -> p k n", p=128)
            dst = wsc[L][bi].rearrange("p (k n) -> p k n", k=16)
            off = 0
            for (c0, n) in segs:
                k.dma("pool", dst[:, :, off:off + n], src[:, :, c0:c0 + n], w=[])
                key = ("d", "pool", (k.dnext["pool"] - 1) % NS)
                wscB[L][bi].w[key] = k.dcnt["pool"][key[2]]
                off += n

    def ws_issue():
        i = ws["issued"]
        L, bi = wseq[i]
        s = i % NR
        srcB = mscB[bi] if L == "m" else (msc0B[bi - 8] if L == "m0" else wscB[L][bi])
        assert srcB.w, ("precast not issued before ring load", L, bi)
        if L == "m":
            k.dma("sp", ring[s][:].rearrange("p k n -> p (k n)"), msc[bi], r=[mscB[bi]], w=[ringB[s]])
        elif L == "m0":
            k.dma("sp", ring[s][:].rearrange("p k n -> p (k n)"), msc0[bi - 8], r=[msc0B[bi - 8]], w=[ringB[s]])
        else:
            k.dma("sp", ring[s][:].rearrange("p k n -> p (k n)"), wsc[L][bi], r=[wscB[L][bi]], w=[ringB[s]])
        ws["issued"] = i + 1

    def ws_get(L, bi):
        i = ws["i"]
        assert wseq[i] == (L, bi), (wseq[i], L, bi)
        while ws["issued"] < min(len(wseq), i + NR):
            ws_issue()
        ws["i"] = i + 1
        s = i % NR
        return ring[s], ringB[s]

    k.op("pool", lambda: POOL.memset(identf[:], 1.0), w=[constB])
    k.op("pool", lambda: POOL.affine_select(out=identf[:], in_=identf[:], pattern=[[-1, 128]], compare_op=ALU.is_equal,
                                            fill=0.0, base=0, channel_multiplier=1), r=[constB], w=[constB])
    k.op("pool", lambda: POOL.memset(onesf[:], 1.0), w=[constB])
    k.op("pool", lambda: POOL.memset(onesb[:], 1.0), w=[constB])
    k.op("dve", lambda: DVE.tensor_copy(out=ident[:], in_=identf[:]), r=[constB], w=[constB])
    k.op("pool", lambda: POOL.affine_select(out=mtmp[:], in_=onesf[:], pattern=[[-1, 128]], compare_op=ALU.is_ge,
                                            fill=0.0, base=0, channel_multiplier=1), r=[constB], w=[constB])
    k.op("dve", lambda: DVE.tensor_scalar(out=mkge[:], in0=mtmp[:], scalar1=-1.0, scalar2=30000.0, op0=ALU.add, op1=ALU.mult),
         r=[constB], w=[constB])
    k.op("pool", lambda: POOL.affine_select(out=mtmp[:], in_=onesf[:], pattern=[[1, 128]], compare_op=ALU.is_ge,
                                            fill=0.0, base=0, channel_multiplier=-1), r=[constB], w=[constB])
    k.op("dve", lambda: DVE.tensor_scalar(out=mkle[:], in0=mtmp[:], scalar1=-1.0, scalar2=30000.0, op0=ALU.add, op1=ALU.mult),
         r=[constB], w=[constB])

    k.dma("sp", big[0:2, 0, :], cond[:, :], w=[bigB[0]])
    k.op("act", lambda: ACT.activation(out=xn_all[0:2, 0, :], in_=big[0:2, 0, :], func=AF.Silu), r=[bigB[0]], w=[xnB[0]])
    pv7 = pbf(7, 16)
    k.group("pe", [(lambda kk=kk: TE.transpose(pv7[:, kk, 0:2], xn_all[0:2, 0, kk * 128:(kk + 1) * 128], ident[0:2, 0:2]))
                   for kk in range(16)], r=[xnB[0], constB], w=[pbB[7]])
    k.op("dve", lambda: DVE.tensor_copy(out=sT[:], in_=pv7[:, :, 0:2]), r=[pbB[7]], w=[sTB])

    def colload(dst, rows, n, rB=(), wB=()):
        for (ap, r0, m) in rows:
            k.dma("sp", stage[r0:r0 + m, :], ap, r=list(rB), w=[stageB])
        k.group("pe", [lambda: TE.transpose(pb[7][:, 0:n], stage[0:n, :], identf[0:n, 0:n])], r=[stageB, constB], w=[pbB[7]])
        k.op("dve", lambda: DVE.tensor_copy(out=dst, in_=pb[7][:, 0:n]), r=[pbB[7]], w=list(wB))

    colload(cvc[:, :], [(cvec0[:, :], 0, 24)], 24, wB=[parB])
    colload(ngc[:].rearrange("p l k -> p (l k)"), [(norm_g[0][:, :], 0, 16), (norm_g[1][:, :], 16, 16)], 32, wB=[parB])
    k.dma("sp", big[0:31, 1, 0:1024], conv_w0[:, :], w=[bigB[1]])
    pv7c = pb[7][:, 0:256].rearrange("p (c j) -> p c j", c=8)
    k.group("pe", [(lambda c=c: TE.transpose(pv7c[:, c, 0:31], big[0:31, 1, c * 128:(c + 1) * 128], identf[0:31, 0:31]))
                   for c in range(8)], r=[bigB[1], constB], w=[pbB[7]])
    k.op("dve", lambda: DVE.tensor_copy(out=cw[:, :, 0:31], in_=pv7c[:, :, 0:31]), r=[pbB[7]], w=[parB])
    k.dma("sp", big[0:3, 1, 0:1024], short_w1[:, :], w=[bigB[1]])
    pv7s = pb[7][:, 0:32].rearrange("p (c j) -> p c j", c=8)
    k.group("pe", [(lambda c=c: TE.transpose(pv7s[:, c, 0:3], big[0:3, 1, c * 128:(c + 1) * 128], identf[0:3, 0:3]))
                   for c in range(8)], r=[bigB[1], constB], w=[pbB[7]])
    k.op("dve", lambda: DVE.tensor_copy(out=sw[:, :, 0:3], in_=pv7s[:, :, 0:3]), r=[pbB[7]], w=[parB])
    for i in range(2):
        k.dma("sp", gqk[:, i, 0, :], qk_g[i:i + 1, :].partition_broadcast(128), w=[gqkB])
    for i in range(2):
        g5 = gqk[:, i, 0, :].rearrange("p (a w e) -> p a w e", a=2, w=2)
        s5 = gqk[:, i, 1, :].rearrange("p (a w e) -> p a w e", a=2, w=2)
        k.op("dve", lambda g5=g5, s5=s5: DVE.tensor_copy(out=s5[:, :, 0, :], in_=g5[:, :, 1, :]), r=[gqkB], w=[gqkB])
        k.op("dve", lambda g5=g5, s5=s5: DVE.tensor_copy(out=s5[:, :, 1, :], in_=g5[:, :, 0, :]), r=[gqkB], w=[gqkB])
    k.dma("sp", es[:, :], sink1[0:1, :].partition_broadcast(128), w=[esB])
    k.op("act", lambda: ACT.activation(out=es[:, :], in_=es[:, :], func=AF.Exp), r=[esB], w=[esB])

    def build_diag(c):
        dgt = uxb[0][:, 0:4096]
        dg3 = dgt[:, 0:31 * 128].rearrange("p (j m) -> p j m", j=31)
        for jj in range(31):
            if jj % 2 == 0:
                k.op("act", lambda jj=jj: ACT.activation(out=dg3[:, jj, :], in_=identf[:, :], func=AF.Identity, scale=cw[:, c, jj:jj + 1]),
                     r=[constB, parB], w=[uxB[0]])
            else:
                k.op("dve", lambda jj=jj: DVE.tensor_scalar(out=dg3[:, jj, :], in0=identf[:, :], scalar1=cw[:, c, jj:jj + 1],
                                                           scalar2=None, op0=ALU.mult), r=[constB, parB], w=[uxB[0]])
        k.dma("pst", dgs[c], dgt[:, 0:31 * 128], r=[uxB[0]], w=[dgsB[c]])

    mtile = f512[4][0:2, :]
    mtileB = f512B[4]

    def adaln_block(L, ch, slot, slotB, bk=7):
        k.dma("sp", f512[0][0:2, :], mod_b[L][0:1, ch * 512:(ch + 1) * 512].partition_broadcast(2), w=[f512B[0]])
        k.group("pe", [(lambda kk=kk: TE.matmul(pb[bk][0:2, :], lhsT=sT[:, kk, :], rhs=slot[:, kk, :],
                                                start=(kk == 0), stop=(kk == 15))) for kk in range(16)],
                r=[slotB, sTB], w=[pbB[bk]])
        k.op("dve", lambda: DVE.tensor_tensor(out=mtile[:, 0:512], in0=pb[bk][0:2, :], in1=f512[0][0:2, :], op=ALU.add),
             r=[pbB[bk], f512B[0]], w=[mtileB])
        k.dma("sp", mvec[L][:, ch * 512:(ch + 1) * 512], mtile[:, 0:512], r=[mtileB], w=[mvecB[L][ch]])

    msc0 = dscr("msc0", [4, 128, 16 * 512], BF16)
    msc0B = [Buf() for _ in range(4)]

    def adaln0_late():
        for ch in range(8, 12):
            slot, slotB = ws_get("m0", ch)
            adaln_block(0, ch, slot, slotB)

    def adaln0():
        mwv = mod_w[0].rearrange("(k p) n -> p k n", p=128)
        for ch in range(8):
            s_ = ch % NR
            k.dma("pool", ring[s_][:, :, :], mwv[:, :, ch * 512:(ch + 1) * 512], w=[ringB[s_]])
            adaln_block(0, ch, ring[s_], ringB[s_])

    def precast_mod1():
        mwv = mod_w[1].rearrange("(k p) n -> p k n", p=128)
        for ch in range(12):
            pcq.append(lambda ch=ch: k.dma("pool", msc[ch].rearrange("p (k n) -> p k n", k=16), mwv[:, :, ch * 512:(ch + 1) * 512],
                                           w=[mscB[ch]]))

    def adaln1_part(j):
        for ch in MOD1_PARTS[j]:
            slot, slotB = ws_get("m", ch)
            adaln_block(1, ch, slot, slotB)

    def load_mod_cols(L):
        rows = []
        for g in range(2):
            rows.append((mvec[L][g:g + 1, 0:2048].rearrange("o (k p) -> (o k) p", p=128), 32 * g, 16))
            rows.append((mvec[L][g:g + 1, 2048:4096].rearrange("o (k p) -> (o k) p", p=128), 32 * g + 16, 16))
        colload(mc[:, :], rows, 64, rB=mvecB[L][0:8], wB=[mcB])
        for g in range(2):
            k.op("dve", lambda g=g: DVE.tensor_scalar(out=gsc[:, g, :], in0=mc[:, 32 * g + 16:32 * g + 32], scalar1=1.0,
                                                      scalar2=None, op0=ALU.add), r=[mcB], w=[mcB])
            k.op("dve", lambda g=g: DVE.tensor_tensor(out=gsc[:, g, :], in0=gsc[:, g, :], in1=ngc[:, L, :], op=ALU.mult),
                 r=[mcB, parB], w=[mcB])

    def load_gt(L, g):
        if gt_cur[0] == (L, g):
            return
        gt_cur[0] = (L, g)
        k.dma("sp", gtb[:, :], mvec[L][g:g + 1, 4096:6144].partition_broadcast(128), r=mvecB[L], w=[gtbB])

    def xsrc(L, t):
        if L == 1:
            return x1s[t:t + 128, :]
        return xs[t:t + 128, :] if t < 2048 else xp[t - 2048:t - 2048 + 128, :]

    def load_rope(L, t0, which):
        k.dma("sp", ropet[:, 0, :, :], rope_c[t0:t0 + 512, :].rearrange("(tb p) d -> p tb d", p=128), w=[ropeB])
        k.dma("sp", ropet[:, 1, :, :], rope_s[t0:t0 + 512, :].rearrange("(tb p) d -> p tb d", p=128), w=[ropeB])
        if L == 0:
            for cs_ in range(2):
                k.op("dve", lambda cs_=cs_: DVE.tensor_tensor(
                    out=ropet[:, cs_, :, :], in0=ropet[:, cs_, :, :],
                    in1=gqk[:, which, cs_, :].unsqueeze(1).to_broadcast([128, 4, 128]), op=ALU.mult),
                    r=[ropeB, gqkB], w=[ropeB])

    def rope_apply(src_ap, H, tb, dst_f32, dstB, srcB):
        t1, t1B = f512[0], f512B[0]
        x3 = src_ap.rearrange("p (h d) -> p h d", h=H)
        x5 = src_ap.rearrange("p (h a w e) -> p h a w e", h=H, a=2, w=2)
        c3 = ropet[:, 0, tb, :].unsqueeze(1).to_broadcast([128, H, 128])
        s5 = ropet[:, 1, tb, :].rearrange("p (a w e) -> p a w e", a=2, w=2)
        d3 = dst_f32.rearrange("p (h d) -> p h d", h=H)
        t5 = t1[:, 0:H * 128].rearrange("p (h a w e) -> p h a w e", h=H, a=2, w=2)
        k.op("dve", lambda: DVE.tensor_tensor(out=d3, in0=x3, in1=c3, op=ALU.mult), r=[srcB, ropeB], w=[dstB])
        k.op("dve", lambda: DVE.tensor_tensor(out=t5[:, :, :, 0, :], in0=x5[:, :, :, 1, :],
                                              in1=s5[:, :, 0, :].unsqueeze(1).to_broadcast([128, H, 2, 32]), op=ALU.mult),
             r=[srcB, ropeB], w=[t1B])
        k.op("dve", lambda: DVE.tensor_tensor(out=t5[:, :, :, 1, :], in0=x5[:, :, :, 0, :],
                                              in1=s5[:, :, 1, :].unsqueeze(1).to_broadcast([128, H, 2, 32]), op=ALU.mult),
             r=[srcB, ropeB, t1B], w=[t1B])
        k.op("dve", lambda: DVE.tensor_tensor(out=dst_f32, in0=dst_f32, in1=t1[:, 0:H * 128], op=ALU.add),
             r=[t1B, dstB], w=[dstB])

    def head_rstd(src_ap, H, srcB):
        st, stB = st_next()
        for hh in range(H):
            k.op("act", lambda hh=hh: ACT.activation(out=junk[:, hh * 128:(hh + 1) * 128], in_=src_ap[:, hh * 128:(hh + 1) * 128],
                                                     func=AF.Square, accum_out=st[:, hh:hh + 1]), r=[srcB], w=[junkB, stB])
        k.op("act", lambda: ACT.activation(out=st[:, 0:H], in_=st[:, 0:H], func=AF.Sqrt, bias=EPS, scale=1.0 / 128),
             r=[stB], w=[stB])
        k.op("dve", lambda: DVE.reciprocal(out=st[:, 0:H], in_=st[:, 0:H]), r=[stB], w=[stB])
        return st[:, 0:H], stB

    hTbuf = [hTt, abT]
    hTB = [hb, abB]

    def build_tb(L, j, tb):
        build_a(L, j, tb)
        build_b(L, j, tb)

    def build_a(L, j, tb):
        if j >= 5:
            return
        g, t0 = TILES[j]
        t = t0 + tb * 128
        s = tb % 2
        k.dma("sp", big[:, s, :], xsrc(L, t), r=([x1B[t // 128]] if L == 1 else []), w=[bigB[s]])
        st, stB = st_next()
        k.op("act", lambda: ACT.activation(out=xn[s], in_=big[:, s, :], func=AF.Square, accum_out=st[:, 0:1]),
             r=[bigB[s]], w=[xnB[s], stB])
        k.op("act", lambda: ACT.activation(out=st[:, 1:2], in_=st[:, 0:1], func=AF.Sqrt, bias=EPS, scale=1.0 / D), r=[stB], w=[stB])
        k.op("dve", lambda: DVE.reciprocal(out=st[:, 2:3], in_=st[:, 1:2]), r=[stB], w=[stB])
        k.op("dve", lambda: DVE.tensor_single_scalar(out=xn[s], in_=big[:, s, :], scalar=st[:, 2:3], op=ALU.mult),
             r=[bigB[s], stB], w=[xnB[s]])

    def build_b(L, j, tb):
        if j >= 5:
            return
        g, t0 = TILES[j]
        hT_, hb_ = hTbuf[j % 2], hTB[j % 2]
        s = tb % 2
        pa, pbk = 2 * s, 2 * s + 1
        va, vb = pbf(pa, 8), pbf(pbk, 8)
        k.group("pe", [(lambda kk=kk: TE.transpose((va if kk < 8 else vb)[:, kk % 8, :], xn_all[:, s, kk * 128:(kk + 1) * 128], ident[:]))
                       for kk in range(16)], r=[xnB[s], constB], w=[pbB[pa], pbB[pbk]])
        for kk in range(16):
            src = (va if kk < 8 else vb)[:, kk % 8, :]
            dst = hT_[:, kk, tb * 128:(tb + 1) * 128]
            if kk < 8:
                k.op("act", lambda src=src, dst=dst, kk=kk: ACT.activation(
                    out=dst, in_=src, func=AF.Identity, bias=mc[:, 32 * g + kk:32 * g + kk + 1],
                    scale=gsc[:, g, kk:kk + 1]), r=[pbB[pa], mcB], w=[hb_[kk]])
            else:
                k.op("dve", lambda src=src, dst=dst, kk=kk: DVE.tensor_scalar(
                    out=dst, in0=src, scalar1=gsc[:, g, kk:kk + 1], scalar2=mc[:, 32 * g + kk:32 * g + kk + 1],
                    op0=ALU.mult, op1=ALU.add), r=[pbB[pbk], mcB], w=[hb_[kk]])
        if tb == 3:
            k.dma("pst", hs[L][j], hT_[:].rearrange("p k n -> p (k n)"), r=hb_, w=[hsB[L][j]])

    def pre(L):
        for tb in range(4):
            build_tb(L, 0, tb)
        for j, (g, t0) in enumerate(TILES):
            hTt_, hb_ = hTbuf[j % 2], hTB[j % 2]
            if g == 0:
                load_rope(L, t0, 1)
            slot, slotB = ws_get(L, 0)
            kv_tail = [None]
            for tb in range(4):
                t = t0 + tb * 128
                a = 4 + tb % 2
                k.group("pe", [(lambda kk=kk, tb=tb, a=a, slot=slot: TE.matmul(
                    pb[a][:, :], lhsT=hTt_[:, kk, tb * 128:(tb + 1) * 128], rhs=slot[:, kk, :], start=(kk == 0), stop=(kk == 15)))
                    for kk in range(16)], r=[slotB] + hb_, w=[pbB[a]])
                kf, kfB = f512[1 + tb % 2], f512B[1 + tb % 2]
                raw, rawB = f512[3 + tb % 2], f512B[3 + tb % 2]
                k.op("act", lambda a=a, raw=raw: ACT.copy(out=raw[:, :], in_=pb[a][:, :]), r=[pbB[a]], w=[rawB])
                kraw = raw[:, 0:256]
                if kv_tail[0] is not None:
                    kv_tail[0]()
                    kv_tail[0] = None
                if g == 1:
                    tl = t - 2048
                    seq, blk = tl // 256, (tl % 256) // 128
                    if L == 0:
                        rk, rkB = head_rstd(kraw, 2, rawB)
                        k.op("dve", lambda kf=kf, kraw=kraw, rk=rk: DVE.tensor_tensor(
                            out=kf[:, 0:256].rearrange("p (h d) -> p h d", h=2), in0=kraw.rearrange("p (h d) -> p h d", h=2),
                            in1=rk.unsqueeze(2).to_broadcast([128, 2, 128]), op=ALU.mult), r=[rawB, rkB], w=[kfB])
                        k.op("dve", lambda kf=kf: DVE.tensor_tensor(
                            out=kf[:, 0:256].rearrange("p (h d) -> p h d", h=2), in0=kf[:, 0:256].rearrange("p (h d) -> p h d", h=2),
                            in1=gqk[:, 1, 0, :].unsqueeze(1).to_broadcast([128, 2, 128]), op=ALU.mult), r=[kfB, gqkB], w=[kfB])
                    else:
                        kf, kfB = raw, rawB
                    k.dma("pst", nkv[L][0][tl:tl + 128, :], kf[:, 0:256], r=[kfB])
                    k.dma("pst", nkv[L][1][tl:tl + 128, :], raw[:, 256:512], r=[rawB])
                    kb16, kb16B = qn[tb % 2], qnB[tb % 2]
                    k.op("dve", lambda kf=kf, kb16=kb16: DVE.tensor_copy(out=kb16[:, 0:256], in_=kf[:, 0:256]), r=[kfB], w=[kb16B])
                    k.op("pool", lambda raw=raw, seq=seq, blk=blk: POOL.tensor_copy(out=Vp[:, seq, blk, :], in_=raw[:, 256:512]),
                         r=[rawB], w=[VpB])
                    def tail(tb=tb, kb16=kb16, kb16B=kb16B, seq=seq, blk=blk):
                        pv = pbf(6 + tb % 2, 8)
                        k.group("pe", [(lambda hh=hh: TE.transpose(pv[:, hh, :], kb16[:, hh * 128:(hh + 1) * 128], ident[:]))
                                       for hh in range(2)], r=[kb16B, constB], w=[pbB[6 + tb % 2]])
                        k.op("act", lambda: ACT.copy(out=KTp[:, seq, :, blk * 128:(blk + 1) * 128], in_=pv[:, 0:2, :]),
                             r=[pbB[6 + tb % 2]], w=[KTpB])
                    kv_tail[0] = tail
                else:
                    rope_apply(kraw, 2, tb, kf[:, 0:256], kfB, rawB)
                    kb16, kb16B = qn[tb % 2], qnB[tb % 2]
                    if L == 0:
                        rk, rkB = head_rstd(kraw, 2, rawB)
                        k.op("dve", lambda kf=kf, rk=rk, kb16=kb16: DVE.tensor_tensor(
                            out=kb16[:, 0:256].rearrange("p (h d) -> p h d", h=2), in0=kf[:, 0:256].rearrange("p (h d) -> p h d", h=2),
                            in1=rk.unsqueeze(2).to_broadcast([128, 2, 128]), op=ALU.mult), r=[kfB, rkB], w=[kb16B])
                    else:
                        k.op("dve", lambda kf=kf, kb16=kb16: DVE.tensor_copy(out=kb16[:, 0:256], in_=kf[:, 0:256]), r=[kfB], w=[kb16B])
                    blkg = 4 + t // 128
                    k.op("pool", lambda raw=raw, blkg=blkg: POOL.tensor_copy(out=V[:, blkg, :], in_=raw[:, 256:512]), r=[rawB], w=[VB])
                    def tail(tb=tb, kb16=kb16, kb16B=kb16B, t=t):
                        pv = pbf(6 + tb % 2, 8)
                        k.group("pe", [(lambda hh=hh: TE.transpose(pv[:, hh, :], kb16[:, hh * 128:(hh + 1) * 128], ident[:]))
                                       for hh in range(2)], r=[kb16B, constB], w=[pbB[6 + tb % 2]])
                        k.op("act", lambda: ACT.copy(out=KT[:, :, 512 + t:512 + t + 128], in_=pv[:, 0:2, :]),
                             r=[pbB[6 + tb % 2]], w=[KTB])
                    kv_tail[0] = tail
            for i in range(4):
                if i > 0:
                    build_b(L, j + 1, i - 1)
                if i == 1 and kv_tail[0] is not None:
                    kv_tail[0]()
                    kv_tail[0] = None
                build_a(L, j + 1, i)
                if L == 0 and j < 4 and i in (1, 3):
                    build_diag(2 * j + (i - 1) // 2)
                slot, slotB = ws_get(L, 1 + i)
                for cc in range(2):
                    c = 2 * i + cc
                    pa, pbk = 4 + 2 * cc, 5 + 2 * cc
                    k.group("pe", [(lambda kk=kk, cc=cc, pa=pa, slot=slot: TE.matmul(
                        pb[pa][:, :], lhsT=slot[:, kk, cc * 128:(cc + 1) * 128], rhs=hTt_[:, kk, :], start=(kk == 0), stop=(kk == 15)))
                        for kk in range(16)], r=[slotB] + hb_, w=[pbB[pa]])
                    k.group("pe", [(lambda kk=kk, cc=cc, pbk=pbk, slot=slot: TE.matmul(
                        pb[pbk][:, :], lhsT=slot[:, kk, 256 + cc * 128:256 + (cc + 1) * 128], rhs=hTt_[:, kk, :], start=(kk == 0), stop=(kk == 15)))
                        for kk in range(16)], r=[slotB] + hb_, w=[pbB[pbk]])
                    sg, sgB = f512[3 + cc], f512B[3 + cc]
                    if L == 0:
                        k.op("act", lambda sg=sg, pbk=pbk: ACT.activation(out=sg[:, :], in_=pb[pbk][:, :], func=AF.Sigmoid),
                             r=[pbB[pbk]], w=[sgB])
                    else:
                        k.op("act", lambda sg=sg, pbk=pbk: ACT.copy(out=sg[:, :], in_=pb[pbk][:, :]), r=[pbB[pbk]], w=[sgB])
                    k.op("dve", lambda sg=sg, pa=pa, c=c: DVE.tensor_tensor(out=ut[:, c, :], in0=pb[pa][:, :], in1=sg[:, :], op=ALU.mult),
                         r=[pbB[pa], sgB], w=[utB])
            tl = t0 if g == 0 else 0
            k.dma("pst", us[L][g][:, :, tl:tl + 512], ut[:, :, :], r=[utB], w=[usB[L][j]])
            build_b(L, j + 1, 3)
            pc_pop(1 if L == 0 else 3)
        for half in range(2):
            s = half
            k.dma("sp", big[:, s, 0:1024].rearrange("p (b n) -> p b n", b=4),
                  cache[L][half].rearrange("(b p) n -> p b n", p=128), w=[bigB[s]])
        k.op("dve", lambda: DVE.tensor_copy(out=xn_all[:, 0, 0:1024], in_=big[:, 0, 0:1024]), r=[bigB[0]], w=[xnB[0]])
        k.op("act", lambda: ACT.copy(out=V[:, 0:4, :], in_=big[:, 1, 0:1024].rearrange("p (b n) -> p b n", b=4)), r=[bigB[1]], w=[VB])
        for b_ in range(4):
            pv = pbf(6 + b_ % 2, 8)
            k.group("pe", [(lambda hh=hh, b_=b_, pv=pv: TE.transpose(pv[:, hh, :], xn_all[:, 0, b_ * 256 + hh * 128:b_ * 256 + (hh + 1) * 128], ident[:]))
                           for hh in range(2)], r=[xnB[0], constB], w=[pbB[6 + b_ % 2]])
            k.op("act", lambda pv=pv, b_=b_: ACT.copy(out=KT[:, :, b_ * 128:(b_ + 1) * 128], in_=pv[:, 0:2, :]),
                 r=[pbB[6 + b_ % 2]], w=[KTB])

    def attention(L, j, g, hh, h, out_chunk):
        kvh = h // 4
        ob = 5 if h % 2 == 0 else 2
        chains = []
        if g == 1:
            for s in range(2):
                items = [(KTp[:, s, kvh, b_ * 128:(b_ + 1) * 128], Vp[:, s, b_, kvh * 128:(kvh + 1) * 128], s * 256, 256, [])
                         for b_ in range(2)]
                chains.append(items)
        else:
            items = []
            if L == 0:
                for kb in range(20):
                    items.append((KT[:, kvh, kb * 128:(kb + 1) * 128], V[:, kb, kvh * 128:(kvh + 1) * 128], 0, 512, []))
            else:
                for kb in range(4):
                    items.append((KT[:, kvh, kb * 128:(kb + 1) * 128], V[:, kb, kvh * 128:(kvh + 1) * 128], 0, 512, []))
                for kb in range(max(0, 4 * j - 1), min(15, 4 * j + 4) + 1):
                    lo, hi = max(4 * j, kb - 1), min(4 * j + 3, kb + 1)
                    masks = []
                    for qb in range(lo, hi + 1):
                        if qb == kb + 1:
                            masks.append(((qb - 4 * j) * 128, mkge))
                        elif qb == kb - 1:
                            masks.append(((qb - 4 * j) * 128, mkle))
                    items.append((KT[:, kvh, 512 + kb * 128:512 + (kb + 1) * 128], V[:, 4 + kb, kvh * 128:(kvh + 1) * 128],
                                  (lo - 4 * j) * 128, (hi - lo + 1) * 128, masks))
            chains.append(items)
        kB, vB = (KTpB, VpB) if g == 1 else (KTB, VB)
        flat = [(ci, ii, it) for ci, items in enumerate(chains) for ii, it in enumerate(items)]
        n = len(flat)

        SB_ = [3, 4, 7]
        ptt3 = [ptt[0], ptt[1], qn[0]]
        ptB3 = [ptB[0], ptB[1], qnB[0]]

        def qk(idx):
            ci, ii, (kap, vap, c0, nn, masks) = flat[idx]
            s = idx % 3
            fns = [lambda: TE.matmul(pb[SB_[s]][:, c0:c0 + nn], lhsT=kap, rhs=QT[:, h, c0:c0 + nn], start=True, stop=(len(masks) == 0),
                                     skip_group_check=True)]
            for mi, (mc0, mk) in enumerate(masks):
                fns.append(lambda mc0=mc0, mk=mk, mi=mi: TE.matmul(pb[SB_[s]][:, mc0:mc0 + 128], lhsT=ident[:, :], rhs=mk[:, :], start=False,
                                                                   stop=(mi == len(masks) - 1), skip_group_check=True))
            k.group("pe", fns, r=[kB, QTB[h // 4], constB], w=[pbB[SB_[s]]])

        qk(0)
        if n > 1:
            qk(1)
        for idx in range(n):
            ci, ii, (kap, vap, c0, nn, masks) = flat[idx]
            s = idx % 3
            if idx + 2 < n:
                qk(idx + 2)
            k.op("act", lambda: ACT.activation(out=ptt3[s][:, c0:c0 + nn], in_=pb[SB_[s]][:, c0:c0 + nn], func=AF.Exp, scale=SCALE),
                 r=[pbB[SB_[s]]], w=[ptB3[s]])
            first, last = (ii == 0), (ii == len(chains[ci]) - 1)
            k.group("pe", [
                lambda: TE.matmul(pb[ob][:, c0:c0 + nn], lhsT=vap, rhs=ptt3[s][:, c0:c0 + nn], start=first, stop=last,
                                  skip_group_check=True),
                lambda: TE.matmul(pb[6][:, c0:c0 + nn], lhsT=onesb[:, :], rhs=ptt3[s][:, c0:c0 + nn], start=first, stop=last,
                                  skip_group_check=True)],
                r=[ptB3[s], vB, constB], w=[pbB[ob], pbB[6]])
        rv, rvB = f512[1 + h % 2], f512B[1 + h % 2]
        if L == 1:
            k.op("dve", lambda: DVE.tensor_scalar(out=rv[:, :], in0=pb[6][:, :], scalar1=es[:, h:h + 1], scalar2=None, op0=ALU.add),
                 r=[pbB[6], esB], w=[rvB])
            k.op("dve", lambda: DVE.reciprocal(out=rv[:, :], in_=rv[:, :]), r=[rvB], w=[rvB])
        else:
            k.op("dve", lambda: DVE.reciprocal(out=rv[:, :], in_=pb[6][:, :]), r=[pbB[6]], w=[rvB])
        k.op("dve", lambda: DVE.tensor_tensor(out=rv[:, :], in0=pb[ob][:, :], in1=rv[:, :], op=ALU.mult), r=[pbB[ob], rvB], w=[rvB])
        k.op("dve", lambda: DVE.tensor_tensor(out=abT[:, out_chunk, :], in0=rv[:, :], in1=gbt[:, h, :], op=ALU.mult),
             r=[rvB, gbB[h]], w=[abB[out_chunk]])

    def main(L):
        H_ = 15 if L == 0 else 1
        nP = len(PRE_BLK[L])
        if L == 1:
            k.dma("sp", fgb, final_g[0:1, :].partition_broadcast(128), w=[fgbB])
        def dgbuf(c):
            return (ut[:].rearrange("p c n -> p (c n)"), [utB]) if c % 2 == 0 else (xn_all[:].rearrange("p a n -> p (a n)"), xnB)

        def dg_load(c):
            dgt, dgtB = dgbuf(c)
            k.dma("sp", dgt[:, 0:31 * 128], dgs[c], r=[dgsB[c]], w=dgtB)

        def uviews(j):
            g, t0 = TILES[j]
            ux = uxb[j % 2]
            if g == 0:
                W_ = 512 + 2 * H_
                return ux, ux[:, 0:8 * W_].rearrange("p (c w) -> p c w", c=8)
            W_ = 256 + 2 * H_
            return ux, ux[:, 0:16 * W_].rearrange("p (c s w) -> p c s w", c=8, s=2)

        def tile_loads(L_, j):
            if j >= 5:
                return
            g, t0 = TILES[j]
            k.dma("sp", hTt[:].rearrange("p k n -> p (k n)"), hs[L][j], r=[hsB[L][j]], w=hb)
            ux, uv = uviews(j)
            uB = uxB[j % 2]
            if g == 0:
                lo = H_ if j > 0 else 0
                hi = H_ if j < 3 else 0
                if lo == 0:
                    k.op("pool", lambda: POOL.memset(uv[:, :, 0:H_], 0.0), w=[uB])
                if hi == 0:
                    k.op("pool", lambda: POOL.memset(uv[:, :, H_ + 512:H_ + 512 + H_], 0.0), w=[uB])
                rb = [usB[L][jj] for jj in range(max(0, j - 1), min(3, j + 1) + 1)]
                k.dma("sp", uv[:, :, H_ - lo:H_ + 512 + hi], us[L][0][:, :, t0 - lo:t0 + 512 + hi], r=rb, w=[uB])
            else:
                W_ = 256 + 2 * H_
                k.op("pool", lambda: POOL.memset(uv[:, :, :, 0:H_], 0.0), w=[uB])
                k.op("pool", lambda: POOL.memset(uv[:, :, :, H_ + 256:H_ + 256 + H_], 0.0), w=[uB])
                k.dma("sp", uv[:, :, :, H_:H_ + 256], us[L][1][:, :, :].rearrange("p c (s w) -> p c s w", s=2), r=[usB[L][4]], w=[uB])
            if L == 0:
                dg_load(0)
                dg_load(1)

        tile_loads(L, 0)
        for j, (g, t0) in enumerate(TILES):
            if L == 0 and j == 0:
                pc_pop(5)
            ux, uv = uviews(j)
            uB = uxB[j % 2]
            if g == 0:
                nseg, sl = 1, 512

                def uwin(c, jj, uv=uv):
                    return uv[:, c, jj:jj + 512]
            else:
                nseg, sl = 2, 256

                def uwin(c, jj, uv=uv):
                    return uv[:, c, :, jj:jj + 256]

            def seg(ap):
                return ap if g == 0 else ap.rearrange("p (s w) -> p s w", s=2)

            ap_ = big[:, 2:4, :].rearrange("p s (c n) -> p (s c) n", c=4)

            def apB(c):
                return bigB[2 + c // 4]
            mean, rstd = big[:, 0, 0:512], big[:, 0, 512:1024]
            meanB = rstdB = bigB[0]
            cur = {}
            pend = {}

            def conv_piece(c):
                dgt, dgtB = dgbuf(c)
                dg3 = dgt[:, 0:31 * 128].rearrange("p (j m) -> p j m", j=31)
                pc = 3 + c % 2
                fns = []
                for sgi in range(nseg):
                    for jj in range(31):
                        if g == 0:
                            fns.append(lambda jj=jj: TE.matmul(pb[pc][:, :], lhsT=dg3[:, jj, :], rhs=uwin(c, jj),
                                                               start=(jj == 0), stop=(jj == 30), skip_group_check=True))
                        else:
                            fns.append(lambda jj=jj, sgi=sgi: TE.matmul(
                                pb[pc][:, sgi * 256:(sgi + 1) * 256], lhsT=dg3[:, jj, :], rhs=uwin(c, jj)[:, sgi, :],
                                start=(jj == 0), stop=(jj == 30), skip_group_check=True))
                k.group("pe", fns, r=[uB] + dgtB, w=[pbB[pc]])
                if c + 2 < 8:
                    dg_load(c + 2)
                k.op("act", lambda: ACT.activation(out=ap_[:, c, :], in_=pb[pc][:, :], func=AF.Identity, bias=cvc[:, c:c + 1],
                                                   scale=1.0), r=[pbB[pc], parB], w=[apB(c)])
                sq = big[:, 1, (c % 2) * 512:(c % 2 + 1) * 512]
                k.op("act", lambda: ACT.activation(out=sq, in_=ap_[:, c, :], func=AF.Square), r=[apB(c)], w=[bigB[1]])

                def stats():
                    k.group("pe", [lambda: TE.matmul(pb[5][:, :], lhsT=onesf[:, :], rhs=ap_[:, c, :], start=(c == 0), stop=(c == 7),
                                                     skip_group_check=True),
                                   lambda: TE.matmul(pb[6][:, :], lhsT=onesf[:, :], rhs=sq, start=(c == 0), stop=(c == 7),
                                                     skip_group_check=True)],
                            r=[apB(c), bigB[1], constB], w=[pbB[5], pbB[6]])
                if "s" in pend:
                    pend.pop("s")()
                pend["s"] = stats

            def ln_finalize():
                pend.pop("s")()
                k.op("dve", lambda: DVE.tensor_scalar(out=mean, in0=pb[5][:, :], scalar1=1.0 / 1024, scalar2=None, op0=ALU.mult),
                     r=[pbB[5]], w=[meanB])
                k.op("dve", lambda: DVE.tensor_tensor(out=rstd, in0=mean, in1=mean, op=ALU.mult), r=[meanB], w=[rstdB])
                k.op("dve", lambda: DVE.scalar_tensor_tensor(out=rstd, in0=pb[6][:, :], scalar=1.0 / 1024, in1=rstd,
                                                             op0=ALU.mult, op1=ALU.subtract), r=[pbB[6], rstdB], w=[rstdB])
                k.op("act", lambda: ACT.activation(out=rstd, in_=rstd, func=AF.Sqrt, bias=EPS, scale=1.0), r=[rstdB], w=[rstdB])
                k.op("dve", lambda: DVE.reciprocal(out=rstd, in_=rstd), r=[rstdB], w=[rstdB])

            def gate_ln_piece(c):
                if c % 4 == 0:
                    cur["ga"] = ws_get(L, nP + 4 + c // 4)
                slot, slotB = cur["ga"]
                pa = c % 2
                k.group("pe", [(lambda kk=kk: TE.matmul(
                    pb[pa][:, :], lhsT=slot[:, kk, (c % 4) * 128:(c % 4 + 1) * 128], rhs=hTt[:, kk, :], start=(kk == 0), stop=(kk == 15)))
                    for kk in range(16)], r=[slotB] + hb, w=[pbB[pa]])
                k.op("act", lambda: ACT.activation(out=gts[:, pa, :], in_=pb[pa][:, :], func=AF.Silu), r=[pbB[pa]], w=[gtsB[pa]])
                k.op("dve", lambda: DVE.tensor_tensor(out=ap_[:, c, :], in0=ap_[:, c, :], in1=mean, op=ALU.subtract),
                     r=[apB(c), meanB], w=[apB(c)])
                k.op("dve", lambda: DVE.tensor_tensor(out=ap_[:, c, :], in0=ap_[:, c, :], in1=rstd, op=ALU.mult),
                     r=[apB(c), rstdB], w=[apB(c)])
                k.op("act", lambda: ACT.activation(out=ap_[:, c, :], in_=ap_[:, c, :], func=AF.Silu, bias=cvc[:, 16 + c:17 + c],
                                                   scale=cvc[:, 8 + c:9 + c]), r=[apB(c), parB], w=[apB(c)])
                k.op("dve", lambda: DVE.tensor_tensor(out=abT[:, c, :], in0=ap_[:, c, :], in1=gts[:, pa, :], op=ALU.mult),
                     r=[apB(c), gtsB[pa]], w=[abB[c]])

            def d_piece(c):
                i, cc = c // 2, c % 2
                if cc == 0:
                    cur["d"] = ws_get(L, nP + 4 + i)
                slot, slotB = cur["d"]
                k.group("pe", [(lambda kk=kk: TE.matmul(
                    pb[0][:, :], lhsT=slot[:, kk, cc * 128:(cc + 1) * 128], rhs=hTt[:, kk, :], start=(kk == 0), stop=(kk == 15)))
                    for kk in range(16)], r=[slotB] + hb, w=[pbB[0]])
                k.group("pe", [(lambda kk=kk: TE.matmul(
                    pb[1][:, :], lhsT=slot[:, kk, 256 + cc * 128:256 + (cc + 1) * 128], rhs=hTt[:, kk, :], start=(kk == 0), stop=(kk == 15)))
                    for kk in range(16)], r=[slotB] + hb, w=[pbB[1]])
                a3, a3B = f512[3 + cc], f512B[3 + cc]
                sgd = gts[:, cc, :]
                k.op("act", lambda: ACT.activation(out=sgd, in_=pb[1][:, :], func=AF.Silu), r=[pbB[1]], w=[gtsB[cc]])
                k.op("dve", lambda: DVE.tensor_scalar(out=seg(a3[:, :]), in0=uwin(c, 0), scalar1=sw[:, c, 0:1], scalar2=None,
                                                      op0=ALU.mult), r=[uB, parB], w=[a3B])
                for jj in range(1, 3):
                    k.op("dve", lambda jj=jj: DVE.scalar_tensor_tensor(out=seg(a3[:, :]), in0=uwin(c, jj), scalar=sw[:, c, jj:jj + 1],
                                                                         in1=seg(a3[:, :]), op0=ALU.mult, op1=ALU.add),
                         r=[uB, parB, a3B], w=[a3B])
                k.op("dve", lambda: DVE.tensor_tensor(out=a3[:, :], in0=pb[0][:, :], in1=a3[:, :], op=ALU.mult), r=[pbB[0], a3B], w=[a3B])
                k.op("dve", lambda: DVE.tensor_tensor(out=abT[:, 8 + c, :], in0=a3[:, :], in1=sgd, op=ALU.mult),
                     r=[a3B, gtsB[cc]], w=[abB[8 + c]])

            def q_piece(grp, tb):
                if tb == 0:
                    cur["q"] = ws_get(L, nP + 2 * grp)
                slot, slotB = cur["q"]
                pa = tb % 2
                k.group("pe", [(lambda kk=kk: TE.matmul(
                    pb[pa][:, :], lhsT=hTt[:, kk, tb * 128:(tb + 1) * 128], rhs=slot[:, kk, :], start=(kk == 0), stop=(kk == 15)))
                    for kk in range(16)], r=[slotB] + hb, w=[pbB[pa]])
                qf, qfB = f512[3 + tb % 2], f512B[3 + tb % 2]
                q16, q16B = qn[tb % 2], qnB[tb % 2]
                qr, qrB = f512[1 + tb % 2], f512B[1 + tb % 2]
                k.op("act", lambda: ACT.copy(out=qr[:, :], in_=pb[pa][:, :]), r=[pbB[pa]], w=[qrB])
                if L == 0:
                    rq, rqB = head_rstd(qr[:, :], 4, qrB)
                if g == 0:
                    rope_apply(qr[:, :], 4, tb, qf[:, :], qfB, qrB)
                    if L == 0:
                        k.op("dve", lambda: DVE.tensor_tensor(
                            out=q16[:, :].rearrange("p (h d) -> p h d", h=4), in0=qf[:, :].rearrange("p (h d) -> p h d", h=4),
                            in1=rq.unsqueeze(2).to_broadcast([128, 4, 128]), op=ALU.mult), r=[qfB, rqB], w=[q16B])
                    else:
                        k.op("dve", lambda: DVE.tensor_copy(out=q16[:, :], in_=qf[:, :]), r=[qfB], w=[q16B])
                else:
                    if L == 0:
                        k.op("dve", lambda: DVE.tensor_tensor(
                            out=qf[:, :].rearrange("p (h d) -> p h d", h=4), in0=qr[:, :].rearrange("p (h d) -> p h d", h=4),
                            in1=gqk[:, 0, 0, :].unsqueeze(1).to_broadcast([128, 4, 128]), op=ALU.mult), r=[qrB, gqkB], w=[qfB])
                        k.op("dve", lambda: DVE.tensor_tensor(
                            out=q16[:, :].rearrange("p (h d) -> p h d", h=4), in0=qf[:, :].rearrange("p (h d) -> p h d", h=4),
                            in1=rq.unsqueeze(2).to_broadcast([128, 4, 128]), op=ALU.mult), r=[qfB, rqB], w=[q16B])
                    else:
                        k.op("dve", lambda: DVE.tensor_copy(out=q16[:, :], in_=qr[:, :]), r=[qrB], w=[q16B])

            def q_tail(grp, tb):
                q16, q16B = qn[tb % 2], qnB[tb % 2]
                pv = pbf(2, 8)
                k.group("pe", [(lambda hh=hh: TE.transpose(pv[:, hh, :], q16[:, hh * 128:(hh + 1) * 128], ident[:]))
                               for hh in range(4)], r=[q16B, constB], w=[pbB[2]])
                k.op("act", lambda: ACT.copy(out=QT[:, 4 * grp:4 * grp + 4, tb * 128:(tb + 1) * 128], in_=pv[:, 0:4, :]),
                     r=[pbB[2]], w=[QTB[grp]])

            def gb_piece(grp, hh):
                if hh == 0:
                    cur["gb"] = ws_get(L, nP + 1 + 2 * grp)
                slot, slotB = cur["gb"]
                pa = hh % 2
                h = 4 * grp + hh
                k.group("pe", [(lambda kk=kk: TE.matmul(
                    pb[pa][:, :], lhsT=slot[:, kk, hh * 128:(hh + 1) * 128], rhs=hTt[:, kk, :], start=(kk == 0), stop=(kk == 15)))
                    for kk in range(16)], r=[slotB] + hb, w=[pbB[pa]])
                k.op("act", lambda: ACT.activation(out=gbt[:, h, :], in_=pb[pa][:, :], func=AF.Silu), r=[pbB[pa]], w=[gbB[h]])

            def x_loads():
                for tb in range(4):
                    t = t0 + tb * 128
                    k.dma("sp", big[:, tb, :], xsrc(L, t), r=([x1B[t // 128]] if L == 1 else []), w=[bigB[tb]])

            if g == 0:
                load_rope(L, t0, 0)
            qgb = []
            for grp in range(2):
                for tb in range(4):
                    qgb.append(lambda grp=grp, tb=tb: q_piece(grp, tb))
                    if tb > 0:
                        qgb.append(lambda grp=grp, tb=tb: q_tail(grp, tb - 1))
                qgb.append(lambda grp=grp: gb_piece(grp, 0))
                qgb.append(lambda grp=grp: q_tail(grp, 3))
                qgb += [(lambda grp=grp, hh=hh: gb_piece(grp, hh)) for hh in range(1, 4)]
            att0 = 8 if L == 0 else 0
            if L == 0:
                for c in range(8):
                    conv_piece(c)
                    for _ in range(3):
                        qgb.pop(0)()
                ln_finalize()
                for h in range(8):
                    attention(L, j, g, h % 4, h, att0 + h)
                    if h < 4:
                        gate_ln_piece(2 * h)
                        gate_ln_piece(2 * h + 1)
                    if h == 3:
                        tile_loads(L, j + 1)
                    if h == 5:
                        x_loads()
                    if 4 <= h <= 6 and len(MOD1_PARTS[j]) > h - 4:
                        ch_ = MOD1_PARTS[j][h - 4]
                        slot_, slotB_ = ws_get("m", ch_)
                        adaln_block(1, ch_, slot_, slotB_, bk=0)
                    if j < 2 or h % 2 == 1:
                        pc_pop(1)
            else:
                while qgb:
                    qgb.pop(0)()
                for h in range(8):
                    attention(L, j, g, h % 4, h, att0 + h)
                    if h < 4:
                        d_piece(2 * h)
                        d_piece(2 * h + 1)
                    if h == 3:
                        tile_loads(L, j + 1)
                    if h == 5:
                        x_loads()
            if L == 0 and j == 0:
                adaln0_late()
            load_gt(L, g)
            wob = nP + (6 if L == 0 else 8)
            accs = [0, 1, 3, 4]
            for nb in range(4):
                slot, slotB = ws_get(L, wob + nb)
                for tb in range(4):
                    a = accs[(nb * 4 + tb) % 4]
                    k.group("pe", [(lambda kk=kk, tb=tb, a=a, slot=slot: TE.matmul(
                        pb[a][:, :], lhsT=abT[:, kk, tb * 128:(tb + 1) * 128], rhs=slot[:, kk, :], start=(kk == 0), stop=(kk == 15)))
                        for kk in range(16)], r=[slotB] + abB, w=[pbB[a]])
                    tm, tmB = f512[1 + tb % 2], f512B[1 + tb % 2]
                    k.op("dve", lambda a=a, nb=nb, tm=tm: DVE.tensor_tensor(out=tm[:, :], in0=pb[a][:, :], in1=gtb[:, nb * 512:(nb + 1) * 512],
                                                                           op=ALU.mult), r=[pbB[a], gtbB], w=[tmB])
                    k.op("dve", lambda tb=tb, nb=nb, tm=tm: DVE.tensor_tensor(out=big[:, tb, nb * 512:(nb + 1) * 512], in0=tm[:, :],
                                                                             in1=big[:, tb, nb * 512:(nb + 1) * 512], op=ALU.add),
                         r=[tmB, bigB[tb]], w=[bigB[tb]])
            for tb in range(4):
                t = t0 + tb * 128
                if L == 0:
                    k.dma("pst", x1s[t:t + 128, :], big[:, tb, :], r=[bigB[tb]], w=[x1B[t // 128]])
                    if tb == 3 and j == 3:
                        load_mod_cols(1)
                else:
                    st, stB = st_next()
                    k.op("act", lambda tb=tb, st=st: ACT.activation(out=xn[0], in_=big[:, tb, :], func=AF.Square, accum_out=st[:, 0:1]),
                         r=[bigB[tb]], w=[xnB[0], stB])
                    k.op("act", lambda st=st: ACT.activation(out=st[:, 1:2], in_=st[:, 0:1], func=AF.Sqrt, bias=EPS, scale=1.0 / D), r=[stB], w=[stB])
                    k.op("dve", lambda st=st: DVE.reciprocal(out=st[:, 2:3], in_=st[:, 1:2]), r=[stB], w=[stB])
                    k.op("dve", lambda tb=tb, st=st: DVE.scalar_tensor_tensor(out=big[:, tb, :], in0=big[:, tb, :], scalar=st[:, 2:3], in1=fgb,
                                                                             op0=ALU.mult, op1=ALU.mult), r=[bigB[tb], stB, fgbB], w=[bigB[tb]])
                    dst = ys[t:t + 128, :] if t < 2048 else yp[t - 2048:t - 2048 + 128, :]
                    k.dma("pst", dst, big[:, tb, :], r=[bigB[tb]])

    def pc_setup():
        precast(0, 0, 5)
        precast(0, 5, 99, queue=True)
        mwv0 = mod_w[0].rearrange("(k p) n -> p k n", p=128)
        for ch in range(8, 12):
            pcq.append(lambda ch=ch: k.dma("pool", msc0[ch - 8].rearrange("p (k n) -> p k n", k=16),
                                           mwv0[:, :, ch * 512:(ch + 1) * 512], w=[msc0B[ch - 8]]))
        precast_mod1()
        precast(1, queue=True)

    steps = [lambda: adaln0(), lambda: pc_setup(), lambda: load_mod_cols(0), lambda: pre(0),
             lambda: main(0), lambda: pc_pop(6), lambda: pre(1), lambda: (pc_pop(99), main(1))]
    for si, stp in enumerate(steps):
        if si < stop:
            stp()
    k.finish()
    k.es.close()
    nc._kb = k
    return nc


def _rope_tables():
    rows = 2048 // 64
    r = np.repeat(np.arange(rows), 64).astype(np.float32)
    col = np.tile(np.arange(64), rows).astype(np.float32)
    half = 64
    inv = (np.float32(10000.0) ** (-np.arange(0, half, 2, dtype=np.float32) / np.float32(half))).astype(np.float32)
    ang_r = r[:, None] * inv
    ang_c = col[:, None] * inv
    ang = np.concatenate([ang_r, ang_r, ang_c, ang_c], axis=-1).astype(np.float32)
    cos = np.cos(ang).astype(np.float32)
    sin = np.sin(ang).astype(np.float32)
    sign = np.concatenate([-np.ones(32), np.ones(32), -np.ones(32), np.ones(32)]).astype(np.float32)
    return np.ascontiguousarray(cos), np.ascontiguousarray(sin * sign[None, :])


def make_in_maps(inp, cores):
    f = lambda a: np.ascontiguousarray(np.asarray(a, dtype=np.float32))
    cos, sin = _rope_tables()
    shared = {
        "mod_w0": f(inp["mod_w0"]), "mod_w1": f(inp["mod_w1"]),
        "mod_b0": f(inp["mod_b0"]).reshape(1, -1), "mod_b1": f(inp["mod_b1"]).reshape(1, -1),
        "norm_g0": f(inp["norm_g0"]).reshape(16, 128), "norm_g1": f(inp["norm_g1"]).reshape(16, 128),
        "w_in0": f(inp["w_in0"]), "w_in1": f(inp["w_in1"]), "w_out0": f(inp["w_out0"]), "w_out1": f(inp["w_out1"]),
        "conv_w0": f(inp["conv_w0"]),
        "cvec0": np.ascontiguousarray(np.concatenate([f(inp["conv_b0"]).reshape(8, 128), f(inp["ln_g0"]).reshape(8, 128),
                                                      f(inp["ln_b0"]).reshape(8, 128)], axis=0)),
        "qk_g": np.ascontiguousarray(np.stack([f(inp["q_norm_g0"]), f(inp["k_norm_g0"])], axis=0)),
        "sink1": f(inp["sink1"]).reshape(1, 8), "short_w1": f(inp["short_w1"]),
        "final_g": f(inp["final_norm_g"]).reshape(1, -1), "rope_c": cos, "rope_s": sin,
    }
    maps = []
    for b in cores:
        m = dict(shared)
        m["xs"] = f(inp["x_sample"][b])
        m["xp"] = f(inp["x_prompt"][2 * b:2 * b + 2]).reshape(512, D)
        m["ck0"] = f(inp["cache_k0"][b]).reshape(512, 256)
        m["cv0"] = f(inp["cache_v0"][b]).reshape(512, 256)
        m["ck1"] = f(inp["cache_k1"][b]).reshape(512, 256)
        m["cv1"] = f(inp["cache_v1"][b]).reshape(512, 256)
        m["cond"] = np.ascontiguousarray(np.stack([f(inp["c"][b]), f(inp["c_ctx"])], axis=0))
        maps.append(m)
    return maps


def kernel(**inputs):
    nc = build()
    maps = make_in_maps(inputs, list(range(8)))
    res = run_bass_kernel_spmd(nc, maps, core_ids=list(range(8)))
    rs = res.results
    y_sample = np.stack([rs[b]["ys"] for b in range(8)], axis=0).astype(np.float32)
    y_prompt = np.concatenate([rs[b]["yp"].reshape(2, 256, D) for b in range(8)], axis=0).astype(np.float32)
    outs = [y_prompt, y_sample]
    for nm in ("nk0", "nv0", "nk1", "nv1"):
        outs.append(np.concatenate([rs[b][nm].reshape(2, 256, 2, 128) for b in range(8)], axis=0).astype(np.float32))
    return tuple(outs)
```
